# Optimizing a Trainium2 kernel written in Bass

```python
import math
import jax, jax.numpy as jnp
from jax import lax
import numpy as np

D_MODEL = 2048
BATCH = 16
SEQ = 256
DEPTH = 2
DEC_BATCH = 8
DEC_SEQ = 4096
PAST_LEN = 256

GRID_W = 64
N_MOD = 9
FFN_DIM = ((8 * D_MODEL // 3 + 127) // 128) * 128
HY_W = D_MODEL // 2
HY_SHORT = 3
HY_BANDS = 8
HY_EMB = 1 + 2 * HY_BANDS
HY_FILTER_HIDDEN = 64
HY_FILTER_GAIN = 0.1
HY_DECAY_SLOW = abs(math.log(1e-2)) / 1.5
HY_DECAY_FAST = abs(math.log(1e-2)) / 0.3
GLA_HEADS = 4
GLA_DK = D_MODEL // 16
GLA_DV = D_MODEL // 8
GLA_W = GLA_HEADS * GLA_DV
GLA_RANK = 16
GLA_TAU = 16.0
GLA_CHUNK = 64
DIFF_HEADS = 8
DIFF_DH = D_MODEL // 32
DIFF_DV = 2 * DIFF_DH
DIFF_W = DIFF_HEADS * DIFF_DV
ROPE_THETA = 10000.0
Q_BLOCK = 128
N_BRANCH = 3
LN_EPS = 1e-5
RMS_EPS = 1e-6
ALPHA = (2 * DEPTH) ** 0.25
BETA = (8 * DEPTH) ** -0.25
IN_SPLITS = (3 * HY_W, GLA_HEADS * GLA_DK, GLA_HEADS * GLA_DK, GLA_W, GLA_W, 2 * GLA_RANK,
             DIFF_HEADS * 2 * DIFF_DH, DIFF_HEADS * 2 * DIFF_DH, DIFF_W, N_BRANCH * D_MODEL)
IN_COLS = sum(IN_SPLITS)

kernel_name = 'hybrid_diffusion_prefix_backbone_step'


def _layer_norm(x, g, b):
    xf = x.astype(jnp.float32)
    mu = xf.mean(-1, keepdims=True)
    var = jnp.square(xf - mu).mean(-1, keepdims=True)
    return ((xf - mu) * lax.rsqrt(var + LN_EPS) * g + b).astype(x.dtype)


def _rms_norm(x, g):
    xf = x.astype(jnp.float32)
    return (xf * lax.rsqrt(jnp.mean(xf * xf, -1, keepdims=True) + RMS_EPS) * g).astype(x.dtype)


def _swiglu(h, w1, w3, w2):
    return (jax.nn.silu(h @ w1) * (h @ w3)) @ w2


def _short_conv(u, w, b):
    up = jnp.pad(u, ((0, 0), (1, 1), (0, 0)))
    return up[:, :-2] * w[0] + up[:, 1:-1] * w[1] + up[:, 2:] * w[2] + b


def _hyena_filters(L, P):
    t = jnp.linspace(0.0, 1.0, L, dtype=jnp.float32)
    w = (2.0 * math.pi / L) * jnp.arange(L, dtype=jnp.float32)
    f = jnp.linspace(1e-4, HY_BANDS - 1, HY_BANDS, dtype=jnp.float32)
    feats = jnp.concatenate([t[:, None], jnp.cos(w[:, None] * f), -jnp.sin(w[:, None] * f)], -1)
    hdn = jnp.sin(P['hy_freq'][0] * (feats @ P['hy_w1'] + P['hy_b1']))
    hdn = jnp.sin(P['hy_freq'][1] * (hdn @ P['hy_w2'] + P['hy_b2']))
    h = (hdn @ P['hy_w3']) * jnp.exp(-t[:, None] * jnp.abs(P['hy_decay']))
    h = h.astype(jnp.float32).reshape(L, 2, 2, HY_W)
    fwd, bwd = h[:, :, 0], h[:, :, 1]
    two_sided = jnp.concatenate([fwd, jnp.zeros((1, 2, HY_W), jnp.float32), bwd[:0:-1]], axis=0)
    return jnp.fft.rfft(two_sided, axis=0)


def _long_conv(z, kf, bias):
    L = z.shape[1]
    zf32 = z.astype(jnp.float32)
    y = jnp.fft.irfft(jnp.fft.rfft(zf32, n=2 * L, axis=1) * kf, n=2 * L, axis=1)[:, :L]
    return (y + zf32 * bias).astype(z.dtype)


def _hyena(u, P):
    L = u.shape[1]
    u = _short_conv(u, P['hy_conv_w'], P['hy_conv_b'])
    v, x1, x2 = jnp.split(u, 3, axis=-1)
    kf = _hyena_filters(L, P)
    z = x1 * _long_conv(v, kf[:, 0], P['hy_bias'][0])
    return x2 * _long_conv(z, kf[:, 1], P['hy_bias'][1])


def _gla_chunked(q, k, v, log_a, s0):
    B_, L, H, _ = q.shape
    dv = v.shape[-1]
    n = L // GLA_CHUNK

    def chunks(a):
        return a.astype(jnp.float32).reshape(B_, n, GLA_CHUNK, H, a.shape[-1])

    q, k, v, log_a = chunks(q), chunks(k), chunks(v), chunks(log_a)
    b = jnp.cumsum(log_a, axis=2)
    ref = b[:, :, GLA_CHUNK // 2 - 1:GLA_CHUNK // 2]
    att = jnp.einsum('bnthk,bnshk->bnhts', q * jnp.exp(b - ref), k * jnp.exp(ref - b))
    lower = jnp.tril(jnp.ones((GLA_CHUNK, GLA_CHUNK), dtype=bool))
    att = jnp.where(lower, att, 0.0)
    o_intra = jnp.einsum('bnhts,bnshv->bnthv', att, v)
    b_last = b[:, :, -1:]
    q_in = q * jnp.exp(b)
    k_st = k * jnp.exp(b_last - b)
    decay = jnp.exp(b_last[:, :, 0])

    def step(S, xs):
        qn, kn, vn, dn = xs
        o = jnp.einsum('bthk,bhkv->bthv', qn, S)
        S = dn[..., None] * S + jnp.einsum('bshk,bshv->bhkv', kn, vn)
        return S, o

    xs = (jnp.moveaxis(q_in, 1, 0), jnp.moveaxis(k_st, 1, 0), jnp.moveaxis(v, 1, 0), jnp.moveaxis(decay, 1, 0))
    S, o_inter = lax.scan(step, s0.astype(jnp.float32), xs)
    o = o_intra + jnp.moveaxis(o_inter, 0, 1)
    return o.reshape(B_, L, H, dv), S


def _gla_branch(gq, gk, gv, gr, glr, P, s0f, s0b):
    B_, L, _ = gq.shape
    dt = gq.dtype
    q = gq.reshape(B_, L, GLA_HEADS, GLA_DK) * GLA_DK ** -0.5
    k = gk.reshape(B_, L, GLA_HEADS, GLA_DK)
    v = gv.reshape(B_, L, GLA_HEADS, GLA_DV)
    lr = glr.astype(jnp.float32)

    def log_gate(d):
        logits = lr[..., d * GLA_RANK:(d + 1) * GLA_RANK] @ P['gla_wa'][d].astype(jnp.float32) + P['gla_ba'][d].astype(jnp.float32)
        return (jax.nn.log_sigmoid(logits) / GLA_TAU).reshape(B_, L, GLA_HEADS, GLA_DK)

    def flip(a):
        return jnp.flip(a, axis=1)

    o_f, s_f = _gla_chunked(q, k, v, log_gate(0), s0f)
    o_b, s_b = _gla_chunked(flip(q), flip(k), flip(v), flip(log_gate(1)), s0b)
    o = o_f + flip(o_b)
    o = _rms_norm(o, P['gla_norm_g']) * jax.nn.silu(gr.astype(jnp.float32)).reshape(B_, L, GLA_HEADS, GLA_DV)
    return o.reshape(B_, L, GLA_W).astype(dt), jnp.stack([s_f, s_b], axis=1).astype(dt)


def _axial_rope(L):
    rows = L // GRID_W
    r = jnp.repeat(jnp.arange(rows, dtype=jnp.float32), GRID_W)
    col = jnp.tile(jnp.arange(GRID_W, dtype=jnp.float32), rows)
    nf = DIFF_DH // 4
    inv = ROPE_THETA ** (-jnp.arange(nf, dtype=jnp.float32) / nf)
    ang = jnp.stack([r[:, None] * inv, col[:, None] * inv], axis=1)
    return jnp.cos(ang)[None, :, None, None], jnp.sin(ang)[None, :, None, None]


def _apply_rope(x, cos, sin):
    xs = x.reshape(x.shape[:-1] + (2, 2, DIFF_DH // 4))
    x1, x2 = xs[..., 0, :], xs[..., 1, :]
    cos, sin = cos.astype(x.dtype), sin.astype(x.dtype)
    out = jnp.stack([x1 * cos - x2 * sin, x1 * sin + x2 * cos], axis=-2)
    return out.reshape(x.shape)


def _diff_attend(q, k, v, lam):
    B_, Lq, H, _, dh = q.shape
    nb = Lq // Q_BLOCK
    qb = jnp.swapaxes(q.reshape(B_, nb, Q_BLOCK, H, 2, dh), 0, 1)
    scale = dh ** -0.5

    def block(qi):
        s = jnp.einsum('bqhjd,bkhjd->bhjqk', qi, k).astype(jnp.float32) * scale
        p = jax.nn.softmax(s, axis=-1)
        a = p[:, :, 0] - lam * p[:, :, 1]
        return jnp.einsum('bhqk,bkhv->bqhv', a.astype(v.dtype), v)

    o = lax.map(block, qb)
    return jnp.swapaxes(o, 0, 1).reshape(B_, Lq, H, v.shape[-1])


def _diff_branch(dq, dk, dv, P, lam_init, ctx_k, ctx_v):
    B_, L, _ = dq.shape
    q = dq.reshape(B_, L, DIFF_HEADS, 2, DIFF_DH)
    k = dk.reshape(B_, L, DIFF_HEADS, 2, DIFF_DH)
    v = dv.reshape(B_, L, DIFF_HEADS, DIFF_DV)
    lp = P['diff_lam'].astype(jnp.float32)
    lam = jnp.exp(jnp.sum(lp[0] * lp[1])) - jnp.exp(jnp.sum(lp[2] * lp[3])) + lam_init
    if ctx_k is None:
        o = _diff_attend(q, k, v, lam)
    else:
        cos, sin = _axial_rope(L)
        ka = jnp.concatenate([ctx_k.astype(k.dtype), _apply_rope(k, cos, sin)], axis=1)
        va = jnp.concatenate([ctx_v.astype(v.dtype), v], axis=1)
        o = _diff_attend(_apply_rope(q, cos, sin), ka, va, lam)
    o = _rms_norm(o, P['diff_norm_g']) * (1.0 - lam_init)
    return o.reshape(B_, L, DIFF_W), k, v


def _mixer(h, P, lam_init, ctx_k, ctx_v, ctx_state):
    B_, L, _ = h.shape
    z = h @ P['w_in']
    zh, gq, gk, gv, gr, glr, dq, dk, dv, zg = jnp.split(z, np.cumsum(IN_SPLITS)[:-1].tolist(), axis=-1)
    ya = _hyena(zh, P)
    if ctx_state is None:
        s0 = jnp.zeros((B_, GLA_HEADS, GLA_DK, GLA_DV), jnp.float32)
        s0f, s0b = s0, s0
    else:
        s0f, s0b = ctx_state[:, 0], ctx_state[:, 1]
    yb, st = _gla_branch(gq, gk, gv, gr, glr, P, s0f, s0b)
    yc, kc, vc = _diff_branch(dq, dk, dv, P, lam_init, ctx_k, ctx_v)
    gates = jax.nn.sigmoid(zg.astype(jnp.float32)).astype(h.dtype).reshape(B_, L, N_BRANCH, D_MODEL)
    y = (gates[:, :, 0] * (ya @ P['w_branch_a'])
         + gates[:, :, 1] * (yb @ P['w_branch_b'])
         + gates[:, :, 2] * (yc @ P['w_branch_c']))
    return y @ P['w_out'], (kc, vc, st)


def _layer(x, cond, P, lam_init, ctx_k=None, ctx_v=None, ctx_state=None):
    mod = (jax.nn.silu(cond) @ P['w_mod'] + P['b_mod']).reshape(cond.shape[0], N_MOD, D_MODEL)[:, :, None, :]
    h = x * (1.0 + mod[:, 1]) + mod[:, 0]
    x = _layer_norm(ALPHA * x + 0.5 * mod[:, 2] * _swiglu(h, P['ffn_w1'][0], P['ffn_w3'][0], P['ffn_w2'][0]),
                    P['ln_g'][0], P['ln_b'][0])
    h = x * (1.0 + mod[:, 4]) + mod[:, 3]
    y, ctx_new = _mixer(h, P, lam_init, ctx_k, ctx_v, ctx_state)
    x = _layer_norm(ALPHA * x + mod[:, 5] * y, P['ln_g'][1], P['ln_b'][1])
    h = x * (1.0 + mod[:, 7]) + mod[:, 6]
    x = _layer_norm(ALPHA * x + 0.5 * mod[:, 8] * _swiglu(h, P['ffn_w1'][1], P['ffn_w3'][1], P['ffn_w2'][1]),
                    P['ln_g'][2], P['ln_b'][2])
    return x, ctx_new


def setup_inputs(seed: int = 0) -> dict:
    key = jax.random.key(seed)
    keys = iter(jax.random.split(key, 40))

    def nrm(shape, scale=1.0):
        return scale * jax.random.normal(next(keys), shape, jnp.float32)

    D = D_MODEL
    FH = HY_FILTER_HIDDEN
    return {
        'x_prompt': nrm((BATCH, SEQ, D)),
        'x_sample': nrm((DEC_BATCH, DEC_SEQ, D)),
        'cache_k': nrm((DEC_BATCH, DEPTH, PAST_LEN, DIFF_HEADS, 2, DIFF_DH)),
        'cache_v': nrm((DEC_BATCH, DEPTH, PAST_LEN, DIFF_HEADS, DIFF_DV)),
        'state_gla': nrm((DEC_BATCH, DEPTH, 2, GLA_HEADS, GLA_DK, GLA_DV)),
        'c': nrm((DEC_BATCH, D)),
        'c_ctx': nrm((D,)),
        'w_mod': nrm((DEPTH, D, N_MOD * D), D ** -0.5),
        'b_mod': nrm((DEPTH, N_MOD * D), 0.02),
        'ln_g': 1.0 + nrm((DEPTH, 3, D), 0.02),
        'ln_b': nrm((DEPTH, 3, D), 0.02),
        'ffn_w1': nrm((DEPTH, 2, D, FFN_DIM), D ** -0.5),
        'ffn_w3': nrm((DEPTH, 2, D, FFN_DIM), D ** -0.5),
        'ffn_w2': nrm((DEPTH, 2, FFN_DIM, D), BETA * FFN_DIM ** -0.5),
        'w_in': nrm((DEPTH, D, IN_COLS), D ** -0.5),
        'hy_conv_w': nrm((DEPTH, HY_SHORT, 3 * HY_W), HY_SHORT ** -0.5),
        'hy_conv_b': nrm((DEPTH, 3 * HY_W), 0.02),
        'hy_w1': nrm((DEPTH, HY_EMB, FH), HY_EMB ** -0.5),
        'hy_b1': nrm((DEPTH, FH), 0.1),
        'hy_freq': 1.0 + nrm((DEPTH, 2, FH), 0.02),
        'hy_w2': nrm((DEPTH, FH, FH), FH ** -0.5),
        'hy_b2': nrm((DEPTH, FH), 0.1),
        'hy_w3': nrm((DEPTH, FH, 4 * HY_W), HY_FILTER_GAIN * FH ** -0.5),
        'hy_decay': jnp.linspace(HY_DECAY_SLOW, HY_DECAY_FAST, 4 * HY_W, dtype=jnp.float32)[None] + nrm((DEPTH, 4 * HY_W), 0.1),
        'hy_bias': nrm((DEPTH, 2, HY_W)),
        'gla_wa': nrm((DEPTH, 2, GLA_RANK, GLA_HEADS * GLA_DK), GLA_RANK ** -0.5),
        'gla_ba': nrm((DEPTH, 2, GLA_HEADS * GLA_DK), 0.1),
        'gla_norm_g': 1.0 + nrm((DEPTH, GLA_DV), 0.02),
        'diff_lam': nrm((DEPTH, 4, DIFF_DH), 0.1),
        'diff_norm_g': 1.0 + nrm((DEPTH, DIFF_DV), 0.02),
        'w_branch_a': nrm((DEPTH, HY_W, D), HY_W ** -0.5),
        'w_branch_b': nrm((DEPTH, GLA_W, D), GLA_W ** -0.5),
        'w_branch_c': nrm((DEPTH, DIFF_W, D), DIFF_W ** -0.5),
        'w_out': nrm((DEPTH, D, D), BETA * D ** -0.5),
    }


def reference(x_prompt, x_sample, cache_k, cache_v, state_gla, c, c_ctx, w_mod, b_mod, ln_g, ln_b,
              ffn_w1, ffn_w3, ffn_w2, w_in, hy_conv_w, hy_conv_b, hy_w1, hy_b1, hy_freq, hy_w2, hy_b2,
              hy_w3, hy_decay, hy_bias, gla_wa, gla_ba, gla_norm_g, diff_lam, diff_norm_g,
              w_branch_a, w_branch_b, w_branch_c, w_out):
    params = [dict(w_mod=w_mod[l], b_mod=b_mod[l], ln_g=ln_g[l], ln_b=ln_b[l], ffn_w1=ffn_w1[l],
                   ffn_w3=ffn_w3[l], ffn_w2=ffn_w2[l], w_in=w_in[l], hy_conv_w=hy_conv_w[l],
                   hy_conv_b=hy_conv_b[l], hy_w1=hy_w1[l], hy_b1=hy_b1[l], hy_freq=hy_freq[l],
                   hy_w2=hy_w2[l], hy_b2=hy_b2[l], hy_w3=hy_w3[l], hy_decay=hy_decay[l],
                   hy_bias=hy_bias[l], gla_wa=gla_wa[l], gla_ba=gla_ba[l], gla_norm_g=gla_norm_g[l],
                   diff_lam=diff_lam[l], diff_norm_g=diff_norm_g[l], w_branch_a=w_branch_a[l],
                   w_branch_b=w_branch_b[l], w_branch_c=w_branch_c[l], w_out=w_out[l])
              for l in range(DEPTH)]
    lam_inits = [0.8 - 0.6 * math.exp(-0.3 * l) for l in range(DEPTH)]

    xp = x_prompt
    ks, vs, ss = [], [], []
    for l in range(DEPTH):
        xp, (kc, vc, sc) = _layer(xp, c_ctx[None], params[l], lam_inits[l])
        ks.append(kc)
        vs.append(vc)
        ss.append(sc)
    new_cache_k = jnp.stack(ks, axis=1)
    new_cache_v = jnp.stack(vs, axis=1)
    new_state_gla = jnp.stack(ss, axis=1)

    xs = x_sample
    for l in range(DEPTH):
        xs, _ = _layer(xs, c, params[l], lam_inits[l], cache_k[:, l], cache_v[:, l], state_gla[:, l])

    return (xp, xs, new_cache_k, new_cache_v, new_state_gla)
```

```python
import math
from contextlib import ExitStack
import numpy as np
import ml_dtypes
import concourse.bass as bass
import concourse.mybir as mybir
from concourse.bass_utils import run_bass_kernel_spmd

F32 = mybir.dt.float32
BF16 = mybir.dt.bfloat16
AF = mybir.ActivationFunctionType
ALU = mybir.AluOpType

D = 2048
FF = 5504
DEPTH = 2
NCOL = 15392
TS = 4096
TP = 512
T = TS + TP
ALPHA = (2 * DEPTH) ** 0.25
MAGIC = 12582912.0


class Buf:
    __slots__ = ("w", "r", "name")

    def __init__(self, name=""):
        self.w = {}
        self.r = {}
        self.name = name


class TT:
    __slots__ = ("t", "b")

    def __init__(self, t, b):
        self.t = t
        self.b = b

    def __getitem__(self, k):
        return self.t[k]


class KB:
    NDSEM = 24

    def __init__(self, nc):
        self.nc = nc
        self.es = ExitStack()
        self.eng = {"pe": nc.tensor, "act": nc.scalar, "dve": nc.vector, "pool": nc.gpsimd, "sp": nc.sync}
        self.tick = {e: 0 for e in ("pe", "act", "dve", "pool")}
        self.sem = {e: self.es.enter_context(nc.semaphore("tick_" + e)) for e in self.tick}
        self.dsem = [self.es.enter_context(nc.semaphore("dma%d" % i)) for i in range(self.NDSEM)]
        self.dcnt = [0] * self.NDSEM
        self.dnext = 0
        self.waited = {}
        self.ninst = 0
        self.uid = 0
        self.wsi = 0

    def sb(self, stack, name, shape, dt):
        self.uid += 1
        t = stack.enter_context(self.nc.sbuf_tensor("%s_%d" % (name, self.uid), list(shape), dt))
        return TT(t, Buf(name))

    def ps(self, stack, name, shape, dt=F32):
        self.uid += 1
        t = stack.enter_context(self.nc.psum_tensor("%s_%d" % (name, self.uid), list(shape), dt))
        return TT(t, Buf(name))

    def _semh(self, key):
        return self.sem[key] if isinstance(key, str) else self.dsem[key[1]]

    def _wait(self, engine, deps):
        e = self.eng[engine]
        for key, val in deps.items():
            wk = (engine, key)
            if self.waited.get(wk, 0) >= val:
                continue
            self.waited[wk] = val
            e.wait_ge(self._semh(key), val)
            self.ninst += 1

    @staticmethod
    def _merge(d, s):
        for k, v in s.items():
            if d.get(k, 0) < v:
                d[k] = v

    def _deps(self, reads, writes):
        deps = {}
        for b in reads:
            self._merge(deps, b.w)
        for b in writes:
            self._merge(deps, b.w)
            self._merge(deps, b.r)
        return deps

    @staticmethod
    def _bufs(xs):
        return [getattr(x, "b", x) for x in xs if x is not None]

    def op(self, engine, fn, reads=(), writes=(), sig=True):
        reads = self._bufs(reads)
        writes = self._bufs(writes)
        deps = self._deps(reads, writes)
        if engine == "pe":
            deps.pop("pe", None)
        self._wait(engine, deps)
        ins = fn(self.eng[engine])
        self.ninst += 1
        if sig:
            self.tick[engine] += 1
            ins.then_inc(self.sem[engine], 1)
            tok = {engine: self.tick[engine]}
        else:
            tok = {engine: self.tick[engine] + 1}
        for b in reads:
            self._merge(b.r, tok)
        for b in writes:
            self._merge(b.w, tok)
            b.r = {}
        return ins

    def dma(self, queue, out, in_, reads=(), writes=()):
        reads = self._bufs(reads)
        writes = self._bufs(writes)
        deps = self._deps(reads, writes)
        s = self.dnext
        self.dnext = (self.dnext + 1) % self.NDSEM
        if self.dcnt[s]:
            self._merge(deps, {("d", s): 16 * self.dcnt[s]})
        self._wait(queue, deps)
        ins = self.eng[queue].dma_start(out=out, in_=in_)
        self.ninst += 1
        self.dcnt[s] += 1
        ins.then_inc(self.dsem[s], 16)
        tok = {("d", s): 16 * self.dcnt[s]}
        for b in reads:
            self._merge(b.r, tok)
        for b in writes:
            self._merge(b.w, tok)
            b.r = {}
        return tok

    def barrier(self):
        deps = {e: v for e, v in self.tick.items() if v}
        for i, c in enumerate(self.dcnt):
            if c:
                deps[("d", i)] = 16 * c
        for e in ("sp", "pe", "act", "dve", "pool"):
            self._wait(e, dict(deps))


_CONST_CACHE = {}


def _bf(a):
    return np.ascontiguousarray(a.astype(ml_dtypes.bfloat16))


def _dft_consts(L):
    N = 2 * L
    t = np.arange(L, dtype=np.int64)
    f = np.arange(L, dtype=np.int64)
    ang = (np.outer(t, f) % N).astype(np.float64) * (2.0 * math.pi / N)
    C = np.cos(ang)
    S = -np.sin(ang)
    sgn = np.where(t % 2 == 0, 1.0, -1.0)
    Ffwd = np.concatenate([C, S], axis=1)
    Ffwd[:, L] = sgn
    Gre = (2.0 / N) * C.T
    Gre[0, :] = 1.0 / N
    Gim = (2.0 / N) * S.T
    Gim[0, :] = sgn / N
    Ginv = np.concatenate([Gre, Gim], axis=0)
    nt = L // 128
    Ft = Ffwd.reshape(nt, 128, 2 * nt, 128).transpose(2, 1, 0, 3)
    Gt = Ginv.reshape(2 * nt, 128, nt, 128).transpose(2, 1, 0, 3)
    nyq = sgn.reshape(nt, 128).T
    return _bf(Ft), _bf(Gt), _bf(nyq)


def _feats(L):
    t = np.linspace(0.0, 1.0, L, dtype=np.float32)
    w = (np.float32(2.0 * math.pi / L) * np.arange(L, dtype=np.float32))
    f = np.linspace(1e-4, 7, 8, dtype=np.float32)
    feats = np.concatenate([t[:, None], np.cos(w[:, None] * f), -np.sin(w[:, None] * f)], -1).astype(np.float32)
    featsT = np.ascontiguousarray(feats.T)
    ntl = np.ascontiguousarray((-t).reshape(L // 128, 128).T)
    return featsT, ntl.astype(np.float32)


def _rope_tables():
    L = TS
    rows = L // 64
    r = np.repeat(np.arange(rows, dtype=np.float32), 64)
    col = np.tile(np.arange(64, dtype=np.float32), rows)
    nf = 16
    inv = (10000.0 ** (-np.arange(nf, dtype=np.float32) / nf)).astype(np.float32)
    ang = np.stack([r[:, None] * inv, col[:, None] * inv], axis=1)
    cos = np.cos(ang).astype(np.float32)
    sin = np.sin(ang).astype(np.float32)
    c64 = np.stack([cos, cos], axis=2).reshape(L, 64)
    s64 = np.stack([-sin, sin], axis=2).reshape(L, 64)
    return np.ascontiguousarray(np.tile(c64, (1, 16))), np.ascontiguousarray(np.tile(s64, (1, 16)))


def _gla_mats():
    j = np.arange(128)[:, None]
    i = np.arange(128)[None, :]
    s = -1.0 / 16.0
    m = np.zeros((6, 128, 128), np.float32)
    m[0] = s * ((j <= i).astype(np.float32) - (j <= 63).astype(np.float32))
    m[1] = s * (j <= i)
    m[2] = s * (j > i)
    m[3] = s * ((j >= i).astype(np.float32) - (j >= 64).astype(np.float32))
    m[4] = s * (j >= i)
    m[5] = s * (j < i)
    masks = np.zeros((2, 128, 128), np.float32)
    masks[0] = (j <= i)
    masks[1] = (j >= i)
    return m, masks


def _consts():
    if _CONST_CACHE:
        return _CONST_CACHE
    c = _CONST_CACHE
    c["c_f4096"], c["c_g4096"], c["c_nyq4096"] = _dft_consts(4096)
    c["c_f256"], c["c_g256"], c["c_nyq256"] = _dft_consts(256)
    c["c_featsT4096"], c["c_ntl4096"] = _feats(4096)
    c["c_featsT256"], c["c_ntl256"] = _feats(256)
    c["c_cosf"], c["c_sinf"] = _rope_tables()
    c["c_glam"], c["c_gmask"] = _gla_mats()
    c["c_ident"] = _bf(np.eye(128, dtype=np.float32))
    return c


WNAMES = ["w_mod", "b_mod", "ln_g", "ln_b", "ffn_w1", "ffn_w3", "ffn_w2", "w_in", "hy_conv_w", "hy_conv_b",
          "hy_w1", "hy_b1", "hy_freq", "hy_w2", "hy_b2", "hy_w3", "hy_decay", "hy_bias", "gla_wa", "gla_ba",
          "gla_norm_g", "diff_lam", "diff_norm_g", "w_branch_a", "w_branch_b", "w_branch_c", "w_out"]


class MK(KB):
    def __init__(self, nc, shapes, cfg):
        super().__init__(nc)
        self.cfg = cfg
        self.I = {}
        for name, (shape, dt) in shapes.items():
            self.I[name] = nc.dram_tensor(name, list(shape), dt, kind="ExternalInput").ap()
        self.O = {}
        self.dbg = cfg.get("dbg", ())

    def out(self, name, shape, dt=F32):
        self.O[name] = self.nc.dram_tensor(name, list(shape), dt, kind="ExternalOutput").ap()
        return self.O[name]

    def scratch(self, name, shape, dt=F32):
        if name in self.dbg:
            return self.out(name, shape, dt)
        return self.nc.dram_tensor(name, list(shape), dt).ap()

    def mm(self, psum_ap, pairs, reads, pw):
        n = len(pairs)
        for i, (l, r) in enumerate(pairs):
            self.op("pe", lambda e, l=l, r=r, i=i: e.matmul(psum_ap, lhsT=l, rhs=r, start=(i == 0), stop=(i == n - 1)),
                    reads=reads, writes=[pw], sig=(i == n - 1))

    def wstage_alloc(self, st, n=3):
        self.wst = [self.sb(st, "wst%d" % i, [128, 2048], F32) for i in range(n)]

    def wload(self, dst, dst_ap_fn, src2d, c0, ncols, nk, rows0=0):
        step = max(1, min(nk, 2048 // ncols))
        for k0 in range(0, nk, step):
            k1 = min(nk, k0 + step)
            n = k1 - k0
            stg = self.wst[self.wsi % len(self.wst)]
            sv = stg[:, 0:n * ncols].rearrange("p (k c) -> p k c", c=ncols)
            src = src2d[rows0 + k0 * 128: rows0 + k1 * 128, c0:c0 + ncols].rearrange("(kc p) n -> p kc n", p=128)
            self.dma("sp", sv, src, writes=[stg])
            eng = ("pool", "dve", "pool", "act")[self.wsi % 4]
            self.wsi += 1
            o_ap = dst_ap_fn(k0, k1)
            if eng == "act":
                self.op("act", lambda e, o_ap=o_ap, sv=sv: e.activation(out=o_ap, in_=sv, func=AF.Copy), reads=[stg], writes=[dst])
            else:
                self.op(eng, lambda e, o_ap=o_ap, sv=sv: e.tensor_copy(out=o_ap, in_=sv), reads=[stg], writes=[dst])

    def groups(self):
        return self.cfg.get("groups", list(range(9)))

    def gci(self, g):
        return 0 if g < 8 else 1

    def phase_init(self, st):
        nc = self.nc
        self.ident = self.sb(st, "ident", [128, 128], BF16)
        self.dma("sp", self.ident[:], self.I["c_ident"][:, :], writes=[self.ident])
        self.ones = self.sb(st, "ones", [128, 128], F32)
        self.op("dve", lambda e: e.memset(self.ones[:], 1.0), writes=[self.ones])
        self.eps5 = self.sb(st, "eps5", [128, 1], F32)
        self.op("dve", lambda e: e.memset(self.eps5[:], 1e-5), writes=[self.eps5])
        self.eps6 = self.sb(st, "eps6", [128, 1], F32)
        self.op("dve", lambda e: e.memset(self.eps6[:], 1e-6), writes=[self.eps6])
        self.X = self.scratch("X", [T, D])
        self.MODB = self.scratch("MODB", [2, 128, 9 * D])
        for i in range(0, TS, 1024):
            self.dma("sp", self.X[i:i + 1024, :], self.I["xs"][i:i + 1024, :])
        self.dma("sp", self.X[TS:T, :], self.I["xp"][:, :])
        self.barrier()

    def phase_mod(self, l):
        I = self.I
        with ExitStack() as st:
            cv = self.sb(st, "cv", [128, 2, 16], F32)
            sc = self.sb(st, "sc", [128, 2, 16], F32)
            lhs = self.sb(st, "modlhs", [128, 2, 16, 128], BF16)
            self.dma("sp", cv[:], I["cvec"].rearrange("c (kc p) -> p c kc", p=128), writes=[cv])
            self.op("act", lambda e: e.activation(out=sc[:], in_=cv[:], func=AF.Silu), reads=[cv], writes=[sc])
            for ci in range(2):
                for kc in range(16):
                    self.op("dve", lambda e, ci=ci, kc=kc: e.tensor_scalar(
                        out=lhs[:, ci, kc, :], in0=self.ones[:, :], scalar1=sc[:, ci, kc:kc + 1], scalar2=None, op0=ALU.mult),
                        reads=[self.ones, sc], writes=[lhs])
            self.wstage_alloc(st)
            wb = [self.sb(st, "modw%d" % i, [128, 16, 512], BF16) for i in range(2)]
            bb = [self.sb(st, "modb%d" % i, [128, 512], F32) for i in range(2)]
            res = [self.sb(st, "modr%d" % i, [128, 512], F32) for i in range(4)]
            banks = [self.ps(st, "modp%d" % i, [128, 512]) for i in range(4)]
            wsrc = I["w_mod"][l]
            for nb in range(36):
                w = wb[nb % 2]
                b = bb[nb % 2]
                self.wload(w, lambda k0, k1, w=w: w[:, k0:k1, :], wsrc, nb * 512, 512, 16)
                self.dma("sp", b[:], I["b_mod"][l, nb * 512:(nb + 1) * 512].partition_broadcast(128), writes=[b])
                for ci in range(2):
                    bk = banks[(nb * 2 + ci) % 4]
                    r = res[(nb * 2 + ci) % 4]
                    self.mm(bk[:, :], [(lhs[:, ci, kc, :], w[:, kc, :]) for kc in range(16)], [lhs, w], bk)
                    self.op("dve", lambda e, bk=bk, r=r, b=b: e.tensor_tensor(out=r[:], in0=bk[:], in1=b[:], op=ALU.add),
                            reads=[bk, b], writes=[r])
                    self.dma("sp", self.MODB[ci, :, nb * 512:(nb + 1) * 512], r[:], reads=[r])
        self.barrier()

    def load_vec(self, dst, ci, idx):
        self.dma("sp", dst[:], self.MODB[ci, :, idx * D:(idx + 1) * D], writes=[dst])

    def prologue(self, g, xt, hb, hT, V0, V1, banks_bf):
        for tt in range(4):
            tok = g * 512 + tt * 128
            x = xt[tt % 2]
            self.dma("sp", x[:], self.X[tok:tok + 128, :], writes=[x])
            self.op("dve", lambda e, x=x: e.tensor_tensor(out=x[:], in0=x[:], in1=V0[:], op=ALU.mult), reads=[x, V0], writes=[x])
            self.op("dve", lambda e, x=x: e.tensor_tensor(out=hb[:], in0=x[:], in1=V1[:], op=ALU.add), reads=[x, V1], writes=[hb])
            for half in range(2):
                pb = banks_bf[half]
                for i in range(8):
                    kc = half * 8 + i
                    self.op("pe", lambda e, pb=pb, i=i, kc=kc: e.transpose(
                        out=pb[:, i * 128:(i + 1) * 128], in_=hb[:, kc * 128:(kc + 1) * 128], identity=self.ident[:]),
                        reads=[hb, self.ident], writes=[pb])
                self.op("act", lambda e, pb=pb, half=half, tt=tt: e.activation(
                    out=hT[:, half * 8:half * 8 + 8, tt * 128:(tt + 1) * 128],
                    in_=pb[:, :].rearrange("p (k t) -> p k t", k=8), func=AF.Copy),
                    reads=[pb], writes=[hT])

    def epilogue(self, l, i, tok, osrc, osrc_bufs, xt, tbuf, V0, V1, V2, half_gate, small):
        stats, mv, rstd = small
        self.dma("sp", xt[:], self.X[tok:tok + 128, :], writes=[xt])
        for db in range(4):
            sl = slice(db * 512, (db + 1) * 512)
            self.op("dve", lambda e, db=db, sl=sl: e.scalar_tensor_tensor(
                out=tbuf[:, sl], in0=osrc(db), scalar=(0.5 if half_gate else 1.0), in1=V0[:, sl], op0=ALU.mult, op1=ALU.mult),
                reads=[osrc_bufs[db], V0], writes=[tbuf])
        self.op("dve", lambda e: e.scalar_tensor_tensor(out=tbuf[:], in0=xt[:], scalar=ALPHA, in1=tbuf[:], op0=ALU.mult, op1=ALU.add),
                reads=[xt, tbuf], writes=[tbuf])
        for c in range(4):
            self.op("dve", lambda e, c=c: e.bn_stats(out=stats[:, c, :], in_=tbuf[:, c * 512:(c + 1) * 512]), reads=[tbuf], writes=[stats])
        self.op("dve", lambda e: e.bn_aggr(out=mv[:], in_=stats[:]), reads=[stats], writes=[mv])
        self.op("act", lambda e: e.activation(out=rstd[:], in_=mv[:, 1:2], func=AF.Sqrt, bias=self.eps5[:, 0:1], scale=1.0),
                reads=[mv, self.eps5], writes=[rstd])
        self.op("dve", lambda e: e.reciprocal(out=rstd[:], in_=rstd[:]), reads=[rstd], writes=[rstd])
        self.op("dve", lambda e: e.tensor_scalar(out=tbuf[:], in0=tbuf[:], scalar1=mv[:, 0:1], scalar2=rstd[:, 0:1],
                                                 op0=ALU.subtract, op1=ALU.mult), reads=[tbuf, mv, rstd], writes=[tbuf])
        self.op("dve", lambda e: e.tensor_tensor(out=tbuf[:], in0=tbuf[:], in1=V1[:], op=ALU.mult), reads=[tbuf, V1], writes=[tbuf])
        self.op("dve", lambda e: e.tensor_tensor(out=xt[:], in0=tbuf[:], in1=V2[:], op=ALU.add), reads=[tbuf, V2], writes=[xt])
        self.dma("sp", self.X[tok:tok + 128, :], xt[:], reads=[xt])
        if l == DEPTH - 1 and i == 2:
            if tok < TS:
                self.dma("sp", self.O["ys"][tok:tok + 128, :], xt[:], reads=[xt])
            else:
                self.dma("sp", self.O["yp"][tok - TS:tok - TS + 128, :], xt[:], reads=[xt])

    def load_ln(self, l, i, V1, V2):
        self.dma("sp", V1[:], self.I["ln_g"][l, i, :].partition_broadcast(128), writes=[V1])
        self.dma("sp", V2[:], self.I["ln_b"][l, i, :].partition_broadcast(128), writes=[V2])

    def phase_ffn(self, l, fi):
        I = self.I
        sub = 0 if fi == 0 else 2
        w1 = I["ffn_w1"][l, fi]
        w3 = I["ffn_w3"][l, fi]
        w2 = I["ffn_w2"][l, fi]
        fblocks = [(i * 256, 256) for i in range(21)] + [(21 * 256, 128)]
        with ExitStack() as st:
            hT = self.sb(st, "hT", [128, 16, 512], BF16)
            gT = self.sb(st, "gT", [128, 43, 512], BF16)
            wA = [self.sb(st, "w1b%d" % i, [128, 16, 256], BF16) for i in range(2)]
            wB = [self.sb(st, "w3b%d" % i, [128, 16, 256], BF16) for i in range(2)]
            wC = [self.sb(st, "w2b%d" % i, [128, 4, 512], BF16) for i in range(2)]
            self.wstage_alloc(st)
            V = [self.sb(st, "V%d" % i, [128, D], F32) for i in range(3)]
            xt = [self.sb(st, "xt%d" % i, [128, D], F32) for i in range(2)]
            hb = self.sb(st, "hb", [128, D], BF16)
            tmp = [self.sb(st, "tmp%d" % i, [128, 512], F32) for i in range(2)]
            tb = self.sb(st, "tb", [128, 4, D], F32)
            small = (self.sb(st, "stats", [128, 4, 6], F32), self.sb(st, "mv", [128, 2], F32), self.sb(st, "rstd", [128, 1], F32))
            banks = [self.ps(st, "bk%d" % i, [128, 512]) for i in range(6)]
            bbf = [self.ps(st, "bbf%d" % i, [128, 1024], BF16) for i in range(2)]
            for g in self.groups():
                ci = self.gci(g)
                self.load_vec(V[0], ci, 3 * sub + 1)
                self.load_vec(V[1], ci, 3 * sub + 0)
                self.op("dve", lambda e: e.tensor_scalar(out=V[0][:], in0=V[0][:], scalar1=1.0, scalar2=None, op0=ALU.add),
                        reads=[V[0]], writes=[V[0]])
                self.prologue(g, xt, hb, hT, V[0], V[1], bbf)
                fidx = 0
                for bi, (c0, nc_) in enumerate(fblocks):
                    a = wA[bi % 2]
                    b = wB[bi % 2]
                    self.wload(a, lambda k0, k1, a=a, nc_=nc_: a[:, k0:k1, 0:nc_], w1, c0, nc_, 16)
                    self.wload(b, lambda k0, k1, b=b, nc_=nc_: b[:, k0:k1, 0:nc_], w3, c0, nc_, 16)
                    for ft in range(nc_ // 128):
                        pa = banks[(fidx % 2) * 2]
                        pb = banks[(fidx % 2) * 2 + 1]
                        tm = tmp[fidx % 2]
                        self.mm(pa[:, :], [(a[:, kc, ft * 128:(ft + 1) * 128], hT[:, kc, :]) for kc in range(16)], [a, hT], pa)
                        self.mm(pb[:, :], [(b[:, kc, ft * 128:(ft + 1) * 128], hT[:, kc, :]) for kc in range(16)], [b, hT], pb)
                        self.op("act", lambda e, pa=pa, tm=tm: e.activation(out=tm[:], in_=pa[:], func=AF.Silu), reads=[pa], writes=[tm])
                        self.op("dve", lambda e, pb=pb, tm=tm, fidx=fidx: e.tensor_tensor(out=gT[:, fidx, :], in0=tm[:], in1=pb[:], op=ALU.mult),
                                reads=[tm, pb], writes=[gT])
                        fidx += 1
                self.load_vec(V[0], ci, 3 * sub + 2)
                self.load_ln(l, sub, V[1], V[2])
                nch = 11
                wi = 0
                for db in range(4):
                    for ch in range(nch):
                        f0 = ch * 4
                        nf = min(4, 43 - f0)
                        w = wC[wi % 2]
                        wi += 1
                        self.wload(w, lambda k0, k1, w=w: w[:, k0:k1, :], w2, db * 512, 512, nf, rows0=f0 * 128)
                        for tt in range(4):
                            pk = banks[2 + tt]
                            for fi_ in range(nf):
                                fc = f0 + fi_
                                self.op("pe", lambda e, pk=pk, w=w, fi_=fi_, fc=fc, tt=tt: e.matmul(
                                    pk[:, :], lhsT=gT[:, fc, tt * 128:(tt + 1) * 128], rhs=w[:, fi_, :], start=(fc == 0), stop=(fc == 42)),
                                    reads=[gT, w], writes=[pk], sig=(fi_ == nf - 1))
                    for tt in range(4):
                        pk = banks[2 + tt]
                        self.op("act", lambda e, pk=pk, tt=tt, db=db: e.activation(out=tb[:, tt, db * 512:(db + 1) * 512], in_=pk[:], func=AF.Copy),
                                reads=[pk], writes=[tb])
                for tt in range(4):
                    tok = g * 512 + tt * 128
                    x = xt[tt % 2]
                    self.epilogue(l, sub, tok, lambda db, tt=tt: tb[:, tt, db * 512:(db + 1) * 512], [tb] * 4, x, _V(tb, tt), V[0], V[1], V[2], True, small)
        self.barrier()


class _V:
    def __init__(self, t, tt):
        self.t = t
        self.tt = tt
        self.b = t.b

    def __getitem__(self, k):
        if isinstance(k, tuple):
            return self.t.t[(k[0], self.tt) + tuple(k[1:])]
        return self.t.t[k, self.tt]


SEQS = [(0, TS, 0, -1), (TS, 256, 1, 0), (TS + 256, 256, 1, 1)]


def _seqs(self):
    return [s for s in SEQS if (s[0] // 512) in self.groups()]


def mixer_alloc(self):
    if hasattr(self, "ZHT"):
        return
    S = self.scratch
    self.ZHT = S("ZHT", [3072, T]); self.GQT = S("GQT", [512, T]); self.GKT = S("GKT", [512, T])
    self.GK = S("GK", [T, 512]); self.GV = S("GV", [T, 1024], BF16); self.GR = S("GR", [T, 1024])
    self.GLRT = S("GLRT", [2, 16, T]); self.DQ = S("DQ", [T, 1024]); self.DK = S("DK", [T, 1024]); self.DV = S("DV", [T, 1024])
    self.GATES = S("GATES", [T, 6144])
    self.HTOK = [S("HTOK%d" % i, [T, 1024], BF16) for i in range(3)]
    self.KF = {4096: S("KF4096", [2, 8192, 1024]), 256: S("KF256", [2, 512, 1024])}
    self.YT = [S("Y%sT" % n, [1024, T], BF16) for n in "ABC"]
    self.OF = S("OF", [T, 1024])
    self.DQT = S("DQT", [8, 128, T], BF16); self.DKT = S("DKT", [8, 128, 256 + T], BF16); self.VA = S("VA", [256 + T, 8, 129], BF16)


def mixer_in(self, l):
    I = self.I
    W = I["w_in"][l]
    feat = [(c0, 512, self.ZHT, c0) for c0 in range(0, 3072, 512)] + [(3072, 512, self.GQT, 0), (3584, 512, self.GKT, 0)]
    tokb = [(3584, self.GK, 0, AF.Copy, F32, None)]
    tokb += [(4096 + i * 512, self.GV, i * 512, AF.Copy, BF16, None) for i in range(2)]
    tokb += [(5120 + i * 512, self.GR, i * 512, AF.Copy, F32, None) for i in range(2)]
    tokb += [(6176 + i * 512, self.DQ, i * 512, AF.Copy, F32, None) for i in range(2)]
    tokb += [(7200 + i * 512, self.DK, i * 512, AF.Copy, F32, "nck") for i in range(2)]
    tokb += [(8224 + i * 512, self.DV, i * 512, AF.Copy, F32, "ncv") for i in range(2)]
    tokb += [(9248 + i * 512, self.GATES, i * 512, AF.Sigmoid, F32, None) for i in range(12)]
    with ExitStack() as st:
        hT = self.sb(st, "hT", [128, 16, 512], BF16)
        wb = [self.sb(st, "wi%d" % i, [128, 16, 512], BF16) for i in range(2)]
        self.wstage_alloc(st)
        V = [self.sb(st, "V%d" % i, [128, D], F32) for i in range(2)]
        xt = [self.sb(st, "xt%d" % i, [128, D], F32) for i in range(2)]
        hb = self.sb(st, "hb", [128, D], BF16)
        stf = [self.sb(st, "stf%d" % i, [128, 512], F32) for i in range(4)]
        stb = [self.sb(st, "stb%d" % i, [128, 512], BF16) for i in range(2)]
        banks = [self.ps(st, "bk%d" % i, [128, 512]) for i in range(4)]
        bbf = [self.ps(st, "bbf%d" % i, [128, 1024], BF16) for i in range(2)]
        cnt = 0
        for g in self.groups():
            ci = self.gci(g)
            tok0 = g * 512
            self.load_vec(V[0], ci, 4)
            self.load_vec(V[1], ci, 3)
            self.op("dve", lambda e: e.tensor_scalar(out=V[0][:], in0=V[0][:], scalar1=1.0, scalar2=None, op0=ALU.add), reads=[V[0]], writes=[V[0]])
            self.prologue(g, xt, hb, hT, V[0], V[1], bbf)
            wi = 0
            for (c0, ncl, dst, r0) in feat:
                w = wb[wi % 2]; wi += 1
                self.wload(w, lambda k0, k1, w=w: w[:, k0:k1, :], W, c0, 512, 16)
                for ct in range(4):
                    bk = banks[cnt % 4]; sg = stf[cnt % 4]; cnt += 1
                    self.mm(bk[:, :], [(w[:, kc, ct * 128:(ct + 1) * 128], hT[:, kc, :]) for kc in range(16)], [w, hT], bk)
                    eng = "act" if cnt % 2 else "dve"
                    if eng == "act":
                        self.op("act", lambda e, bk=bk, sg=sg: e.activation(out=sg[:], in_=bk[:], func=AF.Copy), reads=[bk], writes=[sg])
                    else:
                        self.op("dve", lambda e, bk=bk, sg=sg: e.tensor_copy(out=sg[:], in_=bk[:]), reads=[bk], writes=[sg])
                    self.dma("sp", dst[r0 + ct * 128:r0 + (ct + 1) * 128, tok0:tok0 + 512], sg[:], reads=[sg])
            w = wb[wi % 2]; wi += 1
            self.wload(w, lambda k0, k1, w=w: w[:, k0:k1, 0:32], W, 6144, 32, 16)
            for dd in range(2):
                bk = banks[cnt % 4]; sg = stf[cnt % 4]; cnt += 1
                self.mm(bk[0:16, :], [(w[:, kc, dd * 16:(dd + 1) * 16], hT[:, kc, :]) for kc in range(16)], [w, hT], bk)
                self.op("act", lambda e, bk=bk, sg=sg: e.activation(out=sg[0:16, :], in_=bk[0:16, :], func=AF.Copy), reads=[bk], writes=[sg])
                self.dma("sp", self.GLRT[dd, :, tok0:tok0 + 512], sg[0:16, :], reads=[sg])
            for (c0, dst, dc0, fn, dt, oname) in tokb:
                w = wb[wi % 2]; wi += 1
                self.wload(w, lambda k0, k1, w=w: w[:, k0:k1, :], W, c0, 512, 16)
                for tt in range(4):
                    bk = banks[cnt % 4]
                    sg = stf[cnt % 4] if dt == F32 else stb[cnt % 2]
                    cnt += 1
                    self.mm(bk[:, :], [(hT[:, kc, tt * 128:(tt + 1) * 128], w[:, kc, :]) for kc in range(16)], [w, hT], bk)
                    if fn == AF.Copy and cnt % 2 == 0:
                        self.op("dve", lambda e, bk=bk, sg=sg: e.tensor_copy(out=sg[:], in_=bk[:]), reads=[bk], writes=[sg])
                    else:
                        self.op("act", lambda e, bk=bk, sg=sg, fn=fn: e.activation(out=sg[:], in_=bk[:], func=fn), reads=[bk], writes=[sg])
                    tok = tok0 + tt * 128
                    self.dma("sp", dst[tok:tok + 128, dc0:dc0 + 512], sg[:], reads=[sg])
                    if oname is not None and tok >= TS:
                        pi = (tok - TS) // 256
                        t0 = (tok - TS) % 256
                        self.dma("sp", self.O[oname][pi, l, t0:t0 + 128, dc0:dc0 + 512], sg[:], reads=[sg])
    self.barrier()


def range_reduce(self, a, tmp, reads):
    self.op("dve", lambda e: e.tensor_scalar(out=tmp, in0=a, scalar1=1.0 / (2 * math.pi), scalar2=MAGIC, op0=ALU.mult, op1=ALU.add), reads=reads, writes=reads)
    self.op("dve", lambda e: e.tensor_scalar(out=tmp, in0=tmp, scalar1=-MAGIC, scalar2=-2 * math.pi, op0=ALU.add, op1=ALU.mult), reads=reads, writes=reads)
    self.op("dve", lambda e: e.tensor_tensor(out=a, in0=a, in1=tmp, op=ALU.add), reads=reads, writes=reads)


def hy_filters(self, l, L):
    I = self.I
    nt = L // 128
    KF = self.KF[L]
    Fc = I["c_f%d" % L]; NY = I["c_nyq%d" % L]
    with ExitStack() as st:
        fT = self.sb(st, "fT", [17, L], F32)
        w1 = self.sb(st, "hw1", [17, 64], F32); w2 = self.sb(st, "hw2", [64, 64], F32); w3 = self.sb(st, "hw3", [64, 4096], F32)
        pc = self.sb(st, "hpc", [64, 4], F32)
        ntl = self.sb(st, "ntl", [128, nt], F32)
        nyq = self.sb(st, "nyq", [128, nt], BF16)
        hd1 = self.sb(st, "hd1", [64, L], F32); hd2 = self.sb(st, "hd2", [64, L], F32)
        a_ = self.sb(st, "ha", [64, 512], F32); tm_ = self.sb(st, "htm", [64, 512], F32)
        banks = [self.ps(st, "bk%d" % i, [128, 512]) for i in range(5)]
        self.dma("sp", fT[:], I["c_featsT%d" % L][:, :], writes=[fT])
        self.dma("sp", w1[:], I["hy_w1"][l], writes=[w1]); self.dma("sp", w2[:], I["hy_w2"][l], writes=[w2]); self.dma("sp", w3[:], I["hy_w3"][l], writes=[w3])
        self.dma("sp", pc[:, 0:1], I["hy_b1"][l].rearrange("(m o) -> m o", o=1), writes=[pc])
        self.dma("sp", pc[:, 1:2], I["hy_b2"][l].rearrange("(m o) -> m o", o=1), writes=[pc])
        self.dma("sp", pc[:, 2:4], I["hy_freq"][l].rearrange("a m -> m a"), writes=[pc])
        self.dma("sp", ntl[:], I["c_ntl%d" % L][:, :], writes=[ntl])
        self.dma("sp", nyq[:], NY[:, :], writes=[nyq])
        n = min(512, L)
        for (wsrc, src, dst, bcol, fcol, kk) in ((w1, fT, hd1, 0, 2, 17), (w2, hd1, hd2, 1, 3, 64)):
            for tb in range(L // n):
                bk = banks[tb % 2]
                self.mm(bk[0:64, 0:n], [(wsrc[0:kk, :], src[0:kk, tb * n:(tb + 1) * n])], [wsrc, src], bk)
                self.op("dve", lambda e, bk=bk, bcol=bcol, fcol=fcol: e.tensor_scalar(
                    out=a_[:, 0:n], in0=bk[0:64, 0:n], scalar1=pc[:, bcol:bcol + 1], scalar2=pc[:, fcol:fcol + 1], op0=ALU.add, op1=ALU.mult),
                    reads=[bk, pc], writes=[a_])
                self.op("dve", lambda e: e.tensor_scalar(out=tm_[:, 0:n], in0=a_[:, 0:n], scalar1=1.0 / (2 * math.pi), scalar2=MAGIC, op0=ALU.mult, op1=ALU.add), reads=[a_], writes=[tm_])
                self.op("dve", lambda e: e.tensor_scalar(out=tm_[:, 0:n], in0=tm_[:, 0:n], scalar1=-MAGIC, scalar2=-2 * math.pi, op0=ALU.add, op1=ALU.mult), reads=[tm_], writes=[tm_])
                self.op("dve", lambda e: e.tensor_tensor(out=a_[:, 0:n], in0=a_[:, 0:n], in1=tm_[:, 0:n], op=ALU.add), reads=[a_, tm_], writes=[a_])
                self.op("act", lambda e, dst=dst, tb=tb: e.activation(out=dst[:, tb * n:(tb + 1) * n], in_=a_[:, 0:n], func=AF.Sin), reads=[a_], writes=[dst])
        spl = self.sb(st, "spl", [128, nt, 512], BF16); smi = self.sb(st, "smi", [128, nt, 512], BF16)
        dec = [self.sb(st, "dec%d" % i, [128, 512], F32) for i in range(2)]
        ex = [self.sb(st, "ex%d" % i, [128, 512], F32) for i in range(2)]
        hh = [self.sb(st, "hh%d" % i, [128, 512], F32) for i in range(2)]
        ft = [self.sb(st, "ft%d" % i, [128, nt, 128], BF16) for i in range(2)]
        stg = [self.sb(st, "kst%d" % i, [128, 512], F32) for i in range(2)]
        fcnt = 0
        for o in range(2):
            for cb in range(2):
                for dr in range(2):
                    c0 = o * 2048 + dr * 1024 + cb * 512
                    self.dma("sp", dec[dr][:], I["hy_decay"][l, c0:c0 + 512].partition_broadcast(128), writes=[dec[dr]])
                    self.op("act", lambda e, dr=dr: e.activation(out=dec[dr][:], in_=dec[dr][:], func=AF.Abs), reads=[dec[dr]], writes=[dec[dr]])
                for tt in range(nt):
                    for dr in range(2):
                        c0 = o * 2048 + dr * 1024 + cb * 512
                        bk = banks[dr]
                        self.mm(bk[:, :], [(hd2[:, tt * 128:(tt + 1) * 128], w3[:, c0:c0 + 512])], [hd2, w3], bk)
                        self.op("act", lambda e, dr=dr, tt=tt: e.activation(out=ex[dr][:], in_=dec[dr][:], func=AF.Exp, scale=ntl[:, tt:tt + 1]), reads=[dec[dr], ntl], writes=[ex[dr]])
                        self.op("dve", lambda e, dr=dr, bk=bk: e.tensor_tensor(out=hh[dr][:], in0=ex[dr][:], in1=bk[:], op=ALU.mult), reads=[ex[dr], bk], writes=[hh[dr]])
                    if tt == 0:
                        self.op("dve", lambda e: e.memset(hh[1][0:1, :], 0.0), writes=[hh[1]])
                    self.op("dve", lambda e, tt=tt: e.tensor_tensor(out=spl[:, tt, :], in0=hh[0][:], in1=hh[1][:], op=ALU.add), reads=hh, writes=[spl])
                    self.op("dve", lambda e, tt=tt: e.tensor_tensor(out=smi[:, tt, :], in0=hh[0][:], in1=hh[1][:], op=ALU.subtract), reads=hh, writes=[smi])
                for fti in range(2 * nt):
                    src = spl if fti < nt else smi
                    f = ft[fcnt % 2]; sg = stg[fcnt % 2]; bk = banks[2 + fcnt % 2]; fcnt += 1
                    self.dma("sp", f[:], Fc[fti], writes=[f])
                    self.mm(bk[:, :], [(f[:, kc, :], src[:, kc, :]) for kc in range(nt)], [f, src], bk)
                    self.op("act", lambda e, bk=bk, sg=sg: e.activation(out=sg[:], in_=bk[:], func=AF.Copy), reads=[bk], writes=[sg])
                    if fti == nt:
                        b4 = banks[4]
                        self.mm(b4[0:1, :], [(nyq[:, kc:kc + 1], spl[:, kc, :]) for kc in range(nt)], [nyq, spl], b4)
                        self.op("act", lambda e, sg=sg, b4=b4: e.activation(out=sg[0:1, :], in_=b4[0:1, :], func=AF.Copy), reads=[b4], writes=[sg])
                    self.dma("sp", KF[o, fti * 128:(fti + 1) * 128, cb * 512:(cb + 1) * 512], sg[:], reads=[sg])
    self.barrier()


def hy_prep(self, l):
    I = self.I
    for (s0, L, ci, pi) in self.seqs():
        nt = L // 128
        with ExitStack() as st:
            cT = self.sb(st, "cT", [128, 4, L], BF16)
            u = [self.sb(st, "u%d" % i, [128, L + 2], F32) for i in range(2)]
            acc = self.sb(st, "acc", [128, L], F32)
            wc = [self.sb(st, "wc%d" % i, [128, 4], F32) for i in range(2)]
            stg = [self.sb(st, "pst%d" % i, [128, 512], BF16) for i in range(2)]
            bbf = [self.ps(st, "bbf%d" % i, [128, 1024], BF16) for i in range(2)]
            for i in range(2):
                self.op("dve", lambda e, i=i: e.memset(u[i][:, 0:1], 0.0), writes=[u[i]])
                self.op("dve", lambda e, i=i: e.memset(u[i][:, L + 1:L + 2], 0.0), writes=[u[i]])
            cnt = 0
            for r in range(3):
                for cb in range(2):
                    for ct in range(4):
                        ch0 = r * 1024 + cb * 512 + ct * 128
                        uu = u[cnt % 2]; w = wc[cnt % 2]; cnt += 1
                        self.dma("sp", uu[:, 1:L + 1], self.ZHT[ch0:ch0 + 128, s0:s0 + L], writes=[uu])
                        self.dma("sp", w[:, 0:3], I["hy_conv_w"][l, :, ch0:ch0 + 128].rearrange("k c -> c k"), writes=[w])
                        self.dma("sp", w[:, 3:4], I["hy_conv_b"][l, ch0:ch0 + 128].rearrange("(c o) -> c o", o=1), writes=[w])
                        self.op("dve", lambda e, uu=uu, w=w: e.tensor_scalar(out=acc[:], in0=uu[:, 1:L + 1], scalar1=w[:, 1:2], scalar2=w[:, 3:4], op0=ALU.mult, op1=ALU.add), reads=[uu, w], writes=[acc])
                        self.op("dve", lambda e, uu=uu, w=w: e.scalar_tensor_tensor(out=acc[:], in0=uu[:, 0:L], scalar=w[:, 0:1], in1=acc[:], op0=ALU.mult, op1=ALU.add), reads=[uu, w, acc], writes=[acc])
                        self.op("dve", lambda e, uu=uu, w=w, ct=ct: e.scalar_tensor_tensor(out=cT[:, ct, :], in0=uu[:, 2:L + 2], scalar=w[:, 2:3], in1=acc[:], op0=ALU.mult, op1=ALU.add), reads=[uu, w, acc], writes=[cT])
                    for tt in range(nt):
                        pb = bbf[tt % 2]; sg = stg[tt % 2]
                        for ct in range(4):
                            self.op("pe", lambda e, pb=pb, ct=ct, tt=tt: e.transpose(out=pb[:, ct * 128:(ct + 1) * 128], in_=cT[:, ct, tt * 128:(tt + 1) * 128], identity=self.ident[:]), reads=[cT, self.ident], writes=[pb])
                        self.op("act", lambda e, pb=pb, sg=sg: e.activation(out=sg[:], in_=pb[:, 0:512], func=AF.Copy), reads=[pb], writes=[sg])
                        self.dma("sp", self.HTOK[r][s0 + tt * 128:s0 + (tt + 1) * 128, cb * 512:(cb + 1) * 512], sg[:], reads=[sg])
        self.barrier()


def hy_conv(self, l):
    I = self.I
    for (s0, L, ci, pi) in self.seqs():
        nt = L // 128
        KF = self.KF[L]; Fc = I["c_f%d" % L]; Gc = I["c_g%d" % L]
        with ExitStack() as st:
            vt = self.sb(st, "vt", [128, nt, 512], BF16)
            Y = self.sb(st, "Y", [128, 2 * nt, 512], BF16)
            ft = [self.sb(st, "ft%d" % i, [128, nt, 128], BF16) for i in range(4)]
            gt = [self.sb(st, "gt%d" % i, [128, 2 * nt, 128], BF16) for i in range(2)]
            kf = [self.sb(st, "kf%d" % i, [128, 512], F32) for i in range(4)]
            tq = [self.sb(st, "tq%d" % i, [128, 512], F32) for i in range(4)]
            bia = [self.sb(st, "bia%d" % i, [128, 512], F32) for i in range(2)]
            xm = [self.sb(st, "xm%d" % i, [128, 512], BF16) for i in range(2)]
            ya = [self.sb(st, "ya%d" % i, [128, 512], BF16) for i in range(2)]
            ystg = [self.sb(st, "ystg%d" % i, [128, 4, 128], BF16) for i in range(2)]
            banks = [self.ps(st, "bk%d" % i, [128, 512]) for i in range(6)]
            bbf = [self.ps(st, "bbf%d" % i, [128, 1024], BF16) for i in range(2)]
            for cb in range(2):
                for k0 in range(0, nt, 8):
                    k1 = min(nt, k0 + 8)
                    self.dma("sp", vt[:, k0:k1, :], self.HTOK[0][s0 + k0 * 128:s0 + k1 * 128, cb * 512:(cb + 1) * 512].rearrange("(kc p) c -> p kc c", p=128), writes=[vt])
                for o in range(2):
                    self.dma("sp", bia[o][:], I["hy_bias"][l, o, cb * 512:(cb + 1) * 512].partition_broadcast(128), writes=[bia[o]])
                for o in range(2):
                    for j in range(nt):
                        fr = ft[(j % 2) * 2]; fi = ft[(j % 2) * 2 + 1]
                        kr = kf[(j % 2) * 2]; ki = kf[(j % 2) * 2 + 1]
                        br = banks[(j % 2) * 2]; bi = banks[(j % 2) * 2 + 1]
                        self.dma("sp", fr[:], Fc[j], writes=[fr]); self.dma("sp", fi[:], Fc[nt + j], writes=[fi])
                        self.dma("sp", kr[:], KF[o, j * 128:(j + 1) * 128, cb * 512:(cb + 1) * 512], writes=[kr])
                        self.dma("sp", ki[:], KF[o, (nt + j) * 128:(nt + j + 1) * 128, cb * 512:(cb + 1) * 512], writes=[ki])
                        self.mm(br[:, :], [(fr[:, kc, :], vt[:, kc, :]) for kc in range(nt)], [fr, vt], br)
                        self.mm(bi[:, :], [(fi[:, kc, :], vt[:, kc, :]) for kc in range(nt)], [fi, vt], bi)
                        self.op("dve", lambda e, br=br, kr=kr: e.tensor_tensor(out=tq[0][:], in0=kr[:], in1=br[:], op=ALU.mult), reads=[kr, br], writes=[tq[0]])
                        self.op("dve", lambda e, bi=bi, ki=ki: e.tensor_tensor(out=tq[1][:], in0=ki[:], in1=bi[:], op=ALU.mult), reads=[ki, bi], writes=[tq[1]])
                        self.op("dve", lambda e, br=br, ki=ki: e.tensor_tensor(out=tq[2][:], in0=ki[:], in1=br[:], op=ALU.mult), reads=[ki, br], writes=[tq[2]])
                        self.op("dve", lambda e, bi=bi, kr=kr: e.tensor_tensor(out=tq[3][:], in0=kr[:], in1=bi[:], op=ALU.mult), reads=[kr, bi], writes=[tq[3]])
                        self.op("dve", lambda e, j=j: e.tensor_tensor(out=Y[:, j, :], in0=tq[0][:], in1=tq[1][:], op=ALU.subtract), reads=[tq[0], tq[1]], writes=[Y])
                        self.op("dve", lambda e, j=j: e.tensor_tensor(out=Y[:, nt + j, :], in0=tq[2][:], in1=tq[3][:], op=ALU.add), reads=[tq[2], tq[3]], writes=[Y])
                        if j == 0:
                            self.op("dve", lambda e: e.tensor_copy(out=Y[0:1, 0, :], in_=tq[0][0:1, :]), reads=[tq[0]], writes=[Y])
                            self.op("dve", lambda e: e.tensor_copy(out=Y[0:1, nt, :], in_=tq[1][0:1, :]), reads=[tq[1]], writes=[Y])
                    for tt in range(nt):
                        g_ = gt[tt % 2]; bk = banks[4 + tt % 2]; x_ = xm[tt % 2]
                        self.dma("sp", g_[:], Gc[tt], writes=[g_])
                        self.dma("sp", x_[:], self.HTOK[1 + o][s0 + tt * 128:s0 + (tt + 1) * 128, cb * 512:(cb + 1) * 512], writes=[x_])
                        self.mm(bk[:, :], [(g_[:, kc, :], Y[:, kc, :]) for kc in range(2 * nt)], [g_, Y], bk)
                        t_ = tq[tt % 2]
                        self.op("dve", lambda e, tt=tt, t_=t_, o=o: e.tensor_tensor(out=t_[:], in0=vt[:, tt, :], in1=bia[o][:], op=ALU.mult), reads=[vt, bia[o]], writes=[t_])
                        self.op("dve", lambda e, t_=t_, bk=bk: e.tensor_tensor(out=t_[:], in0=t_[:], in1=bk[:], op=ALU.add), reads=[t_, bk], writes=[t_])
                        if o == 0:
                            self.op("dve", lambda e, tt=tt, t_=t_, x_=x_: e.tensor_tensor(out=vt[:, tt, :], in0=t_[:], in1=x_[:], op=ALU.mult), reads=[t_, x_], writes=[vt])
                        else:
                            y_ = ya[tt % 2]; pb = bbf[tt % 2]; sg = ystg[tt % 2]
                            self.op("dve", lambda e, t_=t_, x_=x_, y_=y_: e.tensor_tensor(out=y_[:], in0=t_[:], in1=x_[:], op=ALU.mult), reads=[t_, x_], writes=[y_])
                            for ct in range(4):
                                self.op("pe", lambda e, pb=pb, ct=ct, y_=y_: e.transpose(out=pb[:, ct * 128:(ct + 1) * 128], in_=y_[:, ct * 128:(ct + 1) * 128], identity=self.ident[:]), reads=[y_, self.ident], writes=[pb])
                            self.op("act", lambda e, pb=pb, sg=sg: e.activation(out=sg[:], in_=pb[:, 0:512].rearrange("p (c t) -> p c t", c=4), func=AF.Copy), reads=[pb], writes=[sg])
                            self.dma("sp", self.YT[0][cb * 512:(cb + 1) * 512, s0 + tt * 128:s0 + (tt + 1) * 128].rearrange("(c p) t -> p c t", p=128), sg[:], reads=[sg])
        self.barrier()


def gla(self, l):
    I = self.I
    with ExitStack() as st:
        mats = self.sb(st, "mats", [128, 6, 128], F32); masks = self.sb(st, "masks", [128, 2, 128], F32)
        wa = self.sb(st, "wa", [16, 2, 512], F32); babc = self.sb(st, "babc", [128, 2, 512], F32); gn = self.sb(st, "gn", [128, 256], F32)
        S = self.sb(st, "S", [128, 4, 256], F32); Sb = self.sb(st, "Sb", [128, 4, 256], BF16)
        self.dma("sp", mats[:], I["c_glam"].rearrange("m j i -> j m i"), writes=[mats])
        self.dma("sp", masks[:], I["c_gmask"].rearrange("m j i -> j m i"), writes=[masks])
        self.dma("sp", wa[:], I["gla_wa"][l].rearrange("d r n -> r d n"), writes=[wa])
        for d in range(2):
            self.dma("sp", babc[:, d, :], I["gla_ba"][l, d, :].partition_broadcast(128), writes=[babc])
        self.dma("sp", gn[:], I["gla_norm_g"][l, :].partition_broadcast(128), writes=[gn])
        NB = 2
        lrT = [self.sb(st, "lrT%d" % i, [16, 128], F32) for i in range(NB)]
        qT = [self.sb(st, "qT%d" % i, [128, 4, 128], F32) for i in range(NB)]
        kT = [self.sb(st, "kT%d" % i, [128, 4, 128], F32) for i in range(NB)]
        kt = [self.sb(st, "kt%d" % i, [128, 512], F32) for i in range(NB)]
        vt = [self.sb(st, "vt%d" % i, [128, 1024], BF16) for i in range(NB)]
        of = [self.sb(st, "of%d" % i, [128, 1024], F32) for i in range(NB)]
        gr = [self.sb(st, "gr%d" % i, [128, 1024], F32) for i in range(NB)]
        lx = self.sb(st, "lx", [128, 512], F32); ll = self.sb(st, "ll", [128, 512], F32)
        ex = [self.sb(st, "ex%d" % i, [128, 512], F32) for i in range(2)]
        Q1 = [self.sb(st, "Q1%d" % i, [128, 128], BF16) for i in range(2)]; K1 = [self.sb(st, "K1%d" % i, [128, 128], BF16) for i in range(2)]
        Q2 = [self.sb(st, "Q2%d" % i, [128, 128], BF16) for i in range(2)]; K2 = [self.sb(st, "K2%d" % i, [128, 128], BF16) for i in range(2)]
        am = [self.sb(st, "am%d" % i, [128, 128], BF16) for i in range(2)]
        ot = self.sb(st, "ot", [128, 1024], F32); sq = self.sb(st, "sq", [128, 256], F32)
        ss = self.sb(st, "ss", [128, 4], F32); rs = self.sb(st, "rs", [128, 4], F32)
        sgr = self.sb(st, "sgr", [128, 1024], F32); yb = self.sb(st, "yb", [128, 1024], BF16)
        ystg = [self.sb(st, "ystg%d" % i, [128, 8, 128], BF16) for i in range(2)]
        bL = self.ps(st, "bL", [128, 512]); bE = [self.ps(st, "bE%d" % i, [128, 512]) for i in range(2)]
        bA = self.ps(st, "bA", [128, 512]); bO = [self.ps(st, "bO%d" % i, [128, 512]) for i in range(2)]
        bS = self.ps(st, "bS", [128, 512]); bT = self.ps(st, "bT", [128, 1024], BF16)
        qscale = 128.0 ** -0.5
        it = 0
        for (s0, L, ci, pi) in self.seqs():
            nt = L // 128
            for d in range(2):
                if pi < 0:
                    for h in range(4):
                        self.dma("sp", S[:, h, :], I["sg"][l, d, h], writes=[S])
                else:
                    self.op("dve", lambda e: e.memset(S[:], 0.0), writes=[S])
                self.op("act", lambda e: e.activation(out=Sb[:], in_=S[:], func=AF.Copy), reads=[S], writes=[Sb])
                order = list(range(nt)) if d == 0 else list(range(nt - 1, -1, -1))
                for c in order:
                    tok = s0 + c * 128
                    b_ = it % NB; it += 1
                    self.dma("sp", lrT[b_][:], self.GLRT[d, :, tok:tok + 128], writes=[lrT[b_]])
                    self.dma("sp", qT[b_][:], self.GQT[:, tok:tok + 128].rearrange("(h k) t -> k h t", k=128), writes=[qT[b_]])
                    self.dma("sp", kT[b_][:], self.GKT[:, tok:tok + 128].rearrange("(h k) t -> k h t", k=128), writes=[kT[b_]])
                    self.dma("sp", kt[b_][:], self.GK[tok:tok + 128, :], writes=[kt[b_]])
                    self.dma("sp", vt[b_][:], self.GV[tok:tok + 128, :], writes=[vt[b_]])
                    if d == 1:
                        self.dma("sp", of[b_][:], self.OF[tok:tok + 128, :], writes=[of[b_]])
                        self.dma("sp", gr[b_][:], self.GR[tok:tok + 128, :], writes=[gr[b_]])
                    self.mm(bL[:, :], [(lrT[b_][:, :], wa[:, d, :])], [lrT[b_], wa], bL)
                    self.op("dve", lambda e, d=d: e.tensor_tensor(out=lx[:], in0=bL[:], in1=babc[:, d, :], op=ALU.add), reads=[bL, babc], writes=[lx])
                    self.op("dve", lambda e: e.tensor_scalar(out=lx[:], in0=lx[:], scalar1=-80.0, scalar2=None, op0=ALU.max), reads=[lx], writes=[lx])
                    self.op("act", lambda e: e.activation(out=lx[:], in_=lx[:], func=AF.Exp, scale=-1.0), reads=[lx], writes=[lx])
                    self.op("act", lambda e: e.activation(out=ll[:], in_=lx[:], func=AF.Ln, bias=self.ones[:, 0:1], scale=1.0), reads=[lx, self.ones], writes=[ll])
                    for h in range(4):
                        e_ = bE[h % 2]; x_ = ex[h % 2]; q1 = Q1[h % 2]; k1 = K1[h % 2]; q2 = Q2[h % 2]; k2 = K2[h % 2]; a_ = am[h % 2]
                        hs = slice(h * 128, (h + 1) * 128)
                        self.mm(e_[:, 0:128], [(ll[:, hs], mats[:, 3 * d + 0, :])], [ll, mats], e_)
                        self.mm(e_[:, 128:256], [(ll[:, hs], mats[:, 3 * d + 1, :])], [ll, mats], e_)
                        self.mm(e_[:, 256:384], [(mats[:, 3 * d + 2, :], ll[:, hs])], [ll, mats], e_)
                        self.op("act", lambda e, e_=e_, x_=x_: e.activation(out=x_[:, 0:384], in_=e_[:, 0:384], func=AF.Exp), reads=[e_], writes=[x_])
                        self.op("act", lambda e, e_=e_, x_=x_: e.activation(out=x_[:, 384:512], in_=e_[:, 0:128], func=AF.Exp, scale=-1.0), reads=[e_], writes=[x_])
                        self.op("dve", lambda e, x_=x_, q1=q1, h=h, b_=b_: e.scalar_tensor_tensor(out=q1[:], in0=qT[b_][:, h, :], scalar=qscale, in1=x_[:, 0:128], op0=ALU.mult, op1=ALU.mult), reads=[qT[b_], x_], writes=[q1])
                        self.op("dve", lambda e, x_=x_, k1=k1, h=h, b_=b_: e.tensor_tensor(out=k1[:], in0=kT[b_][:, h, :], in1=x_[:, 384:512], op=ALU.mult), reads=[kT[b_], x_], writes=[k1])
                        self.op("dve", lambda e, x_=x_, q2=q2, h=h, b_=b_: e.scalar_tensor_tensor(out=q2[:], in0=qT[b_][:, h, :], scalar=qscale, in1=x_[:, 128:256], op0=ALU.mult, op1=ALU.mult), reads=[qT[b_], x_], writes=[q2])
                        self.op("dve", lambda e, x_=x_, k2=k2, hs=hs, b_=b_: e.tensor_tensor(out=k2[:], in0=kt[b_][:, hs], in1=x_[:, 256:384], op=ALU.mult), reads=[kt[b_], x_], writes=[k2])
                        self.mm(bA[:, 0:128], [(k1[:, :], q1[:, :])], [k1, q1], bA)
                        self.op("dve", lambda e, a_=a_, d=d: e.tensor_tensor(out=a_[:], in0=bA[:, 0:128], in1=masks[:, d, :], op=ALU.mult), reads=[bA, masks], writes=[a_])
                        ob = bO[h // 2]
                        vs = slice(h * 256, (h + 1) * 256)
                        self.mm(ob[:, (h % 2) * 256:(h % 2) * 256 + 256], [(a_[:, :], vt[b_][:, vs]), (q2[:, :], Sb[:, h, :])], [a_, vt[b_], q2, Sb], ob)
                        self.mm(bS[:, 0:256], [(k2[:, :], vt[b_][:, vs])], [k2, vt[b_]], bS)
                        dc = x_[:, 255:256] if d == 0 else x_[:, 128:129]
                        self.op("dve", lambda e, h=h, dc=dc: e.scalar_tensor_tensor(out=S[:, h, :], in0=S[:, h, :], scalar=dc, in1=bS[:, 0:256], op0=ALU.mult, op1=ALU.add), reads=[S, x_, bS], writes=[S])
                        self.op("act", lambda e, h=h: e.activation(out=Sb[:, h, :], in_=S[:, h, :], func=AF.Copy), reads=[S], writes=[Sb])
                    if d == 0:
                        for hh in range(2):
                            self.op("act", lambda e, hh=hh: e.activation(out=ot[:, hh * 512:(hh + 1) * 512], in_=bO[hh][:], func=AF.Copy), reads=[bO[hh]], writes=[ot])
                        self.dma("sp", self.OF[tok:tok + 128, :], ot[:], reads=[ot])
                    else:
                        for hh in range(2):
                            self.op("dve", lambda e, hh=hh, b_=b_: e.tensor_tensor(out=ot[:, hh * 512:(hh + 1) * 512], in0=of[b_][:, hh * 512:(hh + 1) * 512], in1=bO[hh][:], op=ALU.add), reads=[of[b_], bO[hh]], writes=[ot])
                        for h in range(4):
                            self.op("act", lambda e, h=h: e.activation(out=sq[:], in_=ot[:, h * 256:(h + 1) * 256], func=AF.Square), reads=[ot], writes=[sq])
                            self.op("dve", lambda e, h=h: e.reduce_sum(out=ss[:, h:h + 1], in_=sq[:], axis=mybir.AxisListType.X), reads=[sq], writes=[ss])
                        self.op("act", lambda e: e.activation(out=rs[:], in_=ss[:], func=AF.Sqrt, bias=self.eps6[:, 0:1], scale=1.0 / 256.0), reads=[ss, self.eps6], writes=[rs])
                        self.op("dve", lambda e: e.reciprocal(out=rs[:], in_=rs[:]), reads=[rs], writes=[rs])
                        self.op("act", lambda e, b_=b_: e.activation(out=sgr[:], in_=gr[b_][:], func=AF.Silu), reads=[gr[b_]], writes=[sgr])
                        for h in range(4):
                            self.op("dve", lambda e, h=h: e.scalar_tensor_tensor(out=ot[:, h * 256:(h + 1) * 256], in0=ot[:, h * 256:(h + 1) * 256], scalar=rs[:, h:h + 1], in1=gn[:], op0=ALU.mult, op1=ALU.mult), reads=[ot, rs, gn], writes=[ot])
                        self.op("dve", lambda e: e.tensor_tensor(out=yb[:], in0=ot[:], in1=sgr[:], op=ALU.mult), reads=[ot, sgr], writes=[yb])
                        sg_ = ystg[c % 2]
                        for cc in range(8):
                            self.op("pe", lambda e, cc=cc: e.transpose(out=bT[:, cc * 128:(cc + 1) * 128], in_=yb[:, cc * 128:(cc + 1) * 128], identity=self.ident[:]), reads=[yb, self.ident], writes=[bT])
                        self.op("act", lambda e, sg_=sg_: e.activation(out=sg_[:], in_=bT[:, :].rearrange("p (c t) -> p c t", c=8), func=AF.Copy), reads=[bT], writes=[sg_])
                        self.dma("sp", self.YT[1][:, tok:tok + 128].rearrange("(c p) t -> p c t", p=128), sg_[:], reads=[sg_])
                if pi >= 0:
                    for h in range(4):
                        self.dma("sp", self.O["nsg"][pi, l, d, h], S[:, h, :], reads=[S])
                self.barrier()
    self.barrier()


def diff(self, l):
    I = self.I
    lam_init = 0.8 - 0.6 * math.exp(-0.3 * l)
    with ExitStack() as st:
        dl = self.sb(st, "dl", [128, 256], F32); pr = self.sb(st, "pr", [128, 128], F32); sm = self.sb(st, "sm", [128, 4], F32)
        lam = self.sb(st, "lam", [128, 2], F32); gnc = self.sb(st, "gnc", [128, 128], F32)
        self.dma("sp", dl[:], I["diff_lam"][l].rearrange("a b -> (a b)").partition_broadcast(128), writes=[dl])
        self.op("dve", lambda e: e.tensor_tensor(out=pr[:, 0:64], in0=dl[:, 0:64], in1=dl[:, 64:128], op=ALU.mult), reads=[dl], writes=[pr])
        self.op("dve", lambda e: e.tensor_tensor(out=pr[:, 64:128], in0=dl[:, 128:192], in1=dl[:, 192:256], op=ALU.mult), reads=[dl], writes=[pr])
        self.op("dve", lambda e: e.reduce_sum(out=sm[:, 0:1], in_=pr[:, 0:64], axis=mybir.AxisListType.X), reads=[pr], writes=[sm])
        self.op("dve", lambda e: e.reduce_sum(out=sm[:, 1:2], in_=pr[:, 64:128], axis=mybir.AxisListType.X), reads=[pr], writes=[sm])
        self.op("act", lambda e: e.activation(out=sm[:, 2:4], in_=sm[:, 0:2], func=AF.Exp), reads=[sm], writes=[sm])
        self.op("dve", lambda e: e.tensor_tensor(out=lam[:, 0:1], in0=sm[:, 2:3], in1=sm[:, 3:4], op=ALU.subtract), reads=[sm], writes=[lam])
        self.op("dve", lambda e: e.tensor_scalar(out=lam[:, 1:2], in0=lam[:, 0:1], scalar1=lam_init, scalar2=-1.0, op0=ALU.add, op1=ALU.mult), reads=[lam], writes=[lam])
        self.dma("sp", gnc[:], I["diff_norm_g"][l, :].partition_broadcast(128), writes=[gnc])
        self.op("dve", lambda e: e.tensor_scalar(out=gnc[:], in0=gnc[:], scalar1=(1.0 - lam_init), scalar2=None, op0=ALU.mult), reads=[gnc], writes=[gnc])
        with ExitStack() as s2:
            xin = [self.sb(s2, "xin%d" % i, [128, 1024], F32) for i in range(2)]
            xsw = self.sb(s2, "xsw", [128, 1024], F32); t1 = self.sb(s2, "t1", [128, 1024], F32)
            cs = [self.sb(s2, "cs%d" % i, [128, 1024], F32) for i in range(2)]
            xb = [self.sb(s2, "xb%d" % i, [128, 1024], BF16) for i in range(2)]
            va = [self.sb(s2, "va%d" % i, [128, 8, 129], BF16) for i in range(2)]
            stg = [self.sb(s2, "dstg%d" % i, [128, 8, 128], BF16) for i in range(2)]
            bT = [self.ps(s2, "bT%d" % i, [128, 1024], BF16) for i in range(2)]
            for i in range(2):
                self.op("dve", lambda e, i=i: e.memset(va[i][:, :, 128:129], 1.0), writes=[va[i]])
            cnt = 0
            work = []
            for (s0, L, ci, pi) in self.seqs():
                if pi < 0:
                    for c in range(2):
                        work.append(("k", I["ck"][l, c * 128:(c + 1) * 128, :], None, c * 128))
                        work.append(("v", I["cv"][l, c * 128:(c + 1) * 128, :], None, c * 128))
                for c in range(L // 128):
                    tok = s0 + c * 128
                    pos = tok if pi < 0 else None
                    work.append(("q", self.DQ[tok:tok + 128, :], pos, tok))
                    work.append(("k", self.DK[tok:tok + 128, :], pos, 256 + tok))
                    work.append(("v", self.DV[tok:tok + 128, :], None, 256 + tok))
            lastpos = None
            for (kind, src, pos, dst) in work:
                x = xin[cnt % 2]; b = xb[cnt % 2]; v_ = va[cnt % 2]; sg = stg[cnt % 2]; pb = bT[cnt % 2]; cnt += 1
                self.dma("sp", x[:], src, writes=[x])
                if kind == "v":
                    self.op("act", lambda e, x=x, v_=v_: e.activation(out=v_[:, :, 0:128], in_=x[:, :].rearrange("p (h e) -> p h e", h=8), func=AF.Copy), reads=[x], writes=[v_])
                    self.dma("sp", self.VA[dst:dst + 128, :, :], v_[:], reads=[v_])
                    continue
                if pos is not None:
                    if pos != lastpos:
                        self.dma("sp", cs[0][:], I["c_cosf"][pos:pos + 128, :], writes=[cs[0]])
                        self.dma("sp", cs[1][:], I["c_sinf"][pos:pos + 128, :], writes=[cs[1]])
                        lastpos = pos
                    x4 = x[:, :].rearrange("p (g h n) -> p g h n", h=2, n=16)
                    s4 = xsw[:, :].rearrange("p (g h n) -> p g h n", h=2, n=16)
                    self.op("act", lambda e, x4=x4, s4=s4: e.activation(out=s4[:, :, 0, :], in_=x4[:, :, 1, :], func=AF.Copy), reads=[x], writes=[xsw])
                    self.op("act", lambda e, x4=x4, s4=s4: e.activation(out=s4[:, :, 1, :], in_=x4[:, :, 0, :], func=AF.Copy), reads=[x], writes=[xsw])
                    self.op("dve", lambda e, x=x: e.tensor_tensor(out=t1[:], in0=x[:], in1=cs[0][:], op=ALU.mult), reads=[x, cs[0]], writes=[t1])
                    self.op("dve", lambda e: e.tensor_tensor(out=xsw[:], in0=xsw[:], in1=cs[1][:], op=ALU.mult), reads=[xsw, cs[1]], writes=[xsw])
                    self.op("dve", lambda e, b=b: e.tensor_tensor(out=b[:], in0=t1[:], in1=xsw[:], op=ALU.add), reads=[t1, xsw], writes=[b])
                else:
                    self.op("act", lambda e, x=x, b=b: e.activation(out=b[:], in_=x[:], func=AF.Copy), reads=[x], writes=[b])
                for h in range(8):
                    self.op("pe", lambda e, h=h, b=b, pb=pb: e.transpose(out=pb[:, h * 128:(h + 1) * 128], in_=b[:, h * 128:(h + 1) * 128], identity=self.ident[:]), reads=[b, self.ident], writes=[pb])
                self.op("act", lambda e, pb=pb, sg=sg: e.activation(out=sg[:], in_=pb[:, :].rearrange("p (h t) -> p h t", h=8), func=AF.Copy), reads=[pb], writes=[sg])
                dstT = self.DQT if kind == "q" else self.DKT
                self.dma("sp", dstT[:, :, dst:dst + 128].rearrange("h p t -> p h t"), sg[:], reads=[sg])
        self.barrier()
        with ExitStack() as s2:
            KT = [self.sb(s2, "KT%d" % i, [128, 4352], BF16) for i in range(2)]
            VAh = [self.sb(s2, "VAh%d" % i, [128, 34, 129], BF16) for i in range(2)]
            QT = [[self.sb(s2, "QT%d_%d" % (i, j), [128, 4096], BF16) for j in range(2)] for i in range(2)]
            for i in range(2):
                for j in range(2):
                    self.op("dve", lambda e, i=i, j=j: e.memset(QT[i][j][:], 0.0), writes=[QT[i][j]])
            PT = [self.sb(s2, "PT%d" % i, [128, 512], BF16) for i in range(3)]
            rr = self.sb(s2, "rr", [128, 4], F32); o32 = self.sb(s2, "o32", [128, 128], F32); sq = self.sb(s2, "sq", [128, 128], F32)
            ycb = [self.sb(s2, "ycb%d" % i, [128, 128], BF16) for i in range(2)]
            ystg = [self.sb(s2, "ycs%d" % i, [128, 512], BF16) for i in range(2)]
            accb = [self.ps(s2, "acc%d" % i, [128, 512]) for i in range(4)]
            sbk = [self.ps(s2, "sbk%d" % i, [128, 512]) for i in range(3)]
            bT = self.ps(s2, "bT", [128, 1024], BF16)
            hi = 0; sc = 0
            for (s0, L, ci, pi) in self.seqs():
                nk = L + (256 if pi < 0 else 0)
                kcol0 = 0 if pi < 0 else 256 + s0
                nkc = nk // 128
                nq = min(512, L); nqs = nq // 128
                for h in range(8):
                    kt_ = KT[hi % 2]; va_ = VAh[hi % 2]; qt_ = QT[hi % 2]; hi += 1
                    self.dma("sp", kt_[:, 0:nk], self.DKT[h, :, kcol0:kcol0 + nk], writes=[kt_])
                    for j in range(2):
                        self.dma("sp", qt_[j][64 * j:64 * j + 64, 0:L], self.DQT[h, 64 * j:64 * j + 64, s0:s0 + L], writes=[qt_[j]])
                    for k0 in range(0, nkc, 8):
                        k1 = min(nkc, k0 + 8)
                        self.dma("sp", va_[:, k0:k1, :], self.VA[kcol0 + k0 * 128:kcol0 + k1 * 128, h, :].rearrange("(kc p) e -> p kc e", p=128), writes=[va_])
                    for qb in range(L // nq):
                        units = [(kc, j) for kc in range(nkc) for j in range(2)]
                        slots = {}

                        def emit_qk(ui, units=units, slots=slots, kt_=kt_, qt_=qt_, qb=qb, nq=nq):
                            nonlocal sc
                            kc, j = units[ui]
                            sb_ = sbk[sc % 3]; pt = PT[sc % 3]; sc += 1
                            slots[ui] = pt
                            self.mm(sb_[:, 0:nq], [(kt_[:, kc * 128:(kc + 1) * 128], qt_[j][:, qb * nq:(qb + 1) * nq])], [kt_, qt_[j]], sb_)
                            self.op("act", lambda e, sb_=sb_, pt=pt: e.activation(out=pt[:, 0:nq], in_=sb_[:, 0:nq], func=AF.Exp, scale=0.125), reads=[sb_], writes=[pt])

                        def emit_av(ui, units=units, slots=slots, va_=va_, nqs=nqs, nkc=nkc):
                            kc, j = units[ui]
                            pt = slots.pop(ui)
                            for qs in range(nqs):
                                a = j * nqs + qs
                                ab = accb[a // 2]; co = (a % 2) * 256
                                first = (kc == 0 and a % 2 == 0)
                                self.op("pe", lambda e, ab=ab, co=co, pt=pt, qs=qs, kc=kc, first=first: e.matmul(
                                    ab[:, co:co + 129], lhsT=pt[:, qs * 128:(qs + 1) * 128], rhs=va_[:, kc, :], start=first, stop=(kc == nkc - 1), skip_group_check=True),
                                    reads=[pt, va_], writes=[ab], sig=(qs == nqs - 1))

                        for ui in range(0, len(units), 2):
                            emit_qk(ui)
                            emit_qk(ui + 1)
                            emit_av(ui)
                            emit_av(ui + 1)
                        sg = ystg[qb % 2]
                        for qs in range(nqs):
                            a0 = qs; a1 = nqs + qs
                            A0 = accb[a0 // 2]; c0_ = (a0 % 2) * 256; A1 = accb[a1 // 2]; c1_ = (a1 % 2) * 256
                            yc_ = ycb[qs % 2]
                            self.op("dve", lambda e, A0=A0, c0_=c0_: e.reciprocal(out=rr[:, 0:1], in_=A0[:, c0_ + 128:c0_ + 129]), reads=[A0], writes=[rr])
                            self.op("dve", lambda e, A1=A1, c1_=c1_: e.reciprocal(out=rr[:, 1:2], in_=A1[:, c1_ + 128:c1_ + 129]), reads=[A1], writes=[rr])
                            self.op("dve", lambda e: e.tensor_tensor(out=rr[:, 1:2], in0=rr[:, 1:2], in1=lam[:, 1:2], op=ALU.mult), reads=[rr, lam], writes=[rr])
                            self.op("dve", lambda e, A0=A0, c0_=c0_: e.tensor_scalar(out=o32[:], in0=A0[:, c0_:c0_ + 128], scalar1=rr[:, 0:1], scalar2=None, op0=ALU.mult), reads=[A0, rr], writes=[o32])
                            self.op("dve", lambda e, A1=A1, c1_=c1_: e.scalar_tensor_tensor(out=o32[:], in0=A1[:, c1_:c1_ + 128], scalar=rr[:, 1:2], in1=o32[:], op0=ALU.mult, op1=ALU.add), reads=[A1, rr, o32], writes=[o32])
                            self.op("act", lambda e: e.activation(out=sq[:], in_=o32[:], func=AF.Square), reads=[o32], writes=[sq])
                            self.op("dve", lambda e: e.reduce_sum(out=rr[:, 2:3], in_=sq[:], axis=mybir.AxisListType.X), reads=[sq], writes=[rr])
                            self.op("act", lambda e: e.activation(out=rr[:, 3:4], in_=rr[:, 2:3], func=AF.Sqrt, bias=self.eps6[:, 0:1], scale=1.0 / 128.0), reads=[rr, self.eps6], writes=[rr])
                            self.op("dve", lambda e: e.reciprocal(out=rr[:, 3:4], in_=rr[:, 3:4]), reads=[rr], writes=[rr])
                            self.op("dve", lambda e, yc_=yc_: e.scalar_tensor_tensor(out=yc_[:], in0=o32[:], scalar=rr[:, 3:4], in1=gnc[:], op0=ALU.mult, op1=ALU.mult), reads=[o32, rr, gnc], writes=[yc_])
                            self.op("pe", lambda e, yc_=yc_, qs=qs: e.transpose(out=bT[:, qs * 128:(qs + 1) * 128], in_=yc_[:], identity=self.ident[:]), reads=[yc_, self.ident], writes=[bT])
                        self.op("act", lambda e, sg=sg: e.activation(out=sg[:, 0:nq], in_=bT[:, 0:nq], func=AF.Copy), reads=[bT], writes=[sg])
                        q0 = s0 + qb * nq
                        self.dma("sp", self.YT[2][h * 128:(h + 1) * 128, q0:q0 + nq], sg[:, 0:nq], reads=[sg])
    self.barrier()


def mixer_out(self, l):
    I = self.I
    WB = [I["w_branch_a"][l], I["w_branch_b"][l], I["w_branch_c"][l]]
    WO = I["w_out"][l]
    with ExitStack() as st:
        yT = [self.sb(st, "yT%d" % i, [128, 8, 512], BF16) for i in range(3)]
        yacc = self.sb(st, "yacc", [128, 4, D], F32)
        wbr = [self.sb(st, "wbr%d" % i, [128, 8, 512], BF16) for i in range(2)]
        self.wstage_alloc(st)
        wo = [self.sb(st, "wo%d" % i, [128, 16, 512], BF16) for i in range(2)]
        gt = [self.sb(st, "gt%d" % i, [128, 512], F32) for i in range(3)]
        tmp = [self.sb(st, "tmp%d" % i, [128, 512], F32) for i in range(2)]
        yb = self.sb(st, "yb", [128, D], BF16)
        yT2 = self.sb(st, "yT2", [128, 16, 512], BF16)
        V = [self.sb(st, "V%d" % i, [128, D], F32) for i in range(3)]
        xt = [self.sb(st, "xt%d" % i, [128, D], F32) for i in range(2)]
        small = (self.sb(st, "stats", [128, 4, 6], F32), self.sb(st, "mv", [128, 2], F32), self.sb(st, "rstd", [128, 1], F32))
        banks = [self.ps(st, "bk%d" % i, [128, 512]) for i in range(6)]
        bbf = [self.ps(st, "bbf%d" % i, [128, 1024], BF16) for i in range(2)]
        cnt = 0
        for g in self.groups():
            ci = self.gci(g)
            tok0 = g * 512
            for br in range(3):
                self.dma("sp", yT[br][:], self.YT[br][:, tok0:tok0 + 512].rearrange("(kc p) t -> p kc t", p=128), writes=[yT[br]])
            wi = 0
            for br in range(3):
                for db in range(4):
                    w = wbr[wi % 2]; wi += 1
                    self.wload(w, lambda k0, k1, w=w: w[:, k0:k1, :], WB[br], db * 512, 512, 8)
                    for tt in range(4):
                        bk = banks[cnt % 4]; g_ = gt[cnt % 3]; tm = tmp[cnt % 2]; cnt += 1
                        tok = tok0 + tt * 128
                        self.dma("sp", g_[:], self.GATES[tok:tok + 128, br * D + db * 512: br * D + (db + 1) * 512], writes=[g_])
                        self.mm(bk[:, :], [(yT[br][:, kc, tt * 128:(tt + 1) * 128], w[:, kc, :]) for kc in range(8)], [yT[br], w], bk)
                        if br == 0:
                            self.op("dve", lambda e, bk=bk, g_=g_, tt=tt, db=db: e.tensor_tensor(out=yacc[:, tt, db * 512:(db + 1) * 512], in0=g_[:], in1=bk[:], op=ALU.mult), reads=[g_, bk], writes=[yacc])
                        else:
                            self.op("dve", lambda e, bk=bk, g_=g_, tm=tm: e.tensor_tensor(out=tm[:], in0=g_[:], in1=bk[:], op=ALU.mult), reads=[g_, bk], writes=[tm])
                            self.op("dve", lambda e, tm=tm, tt=tt, db=db: e.tensor_tensor(out=yacc[:, tt, db * 512:(db + 1) * 512], in0=yacc[:, tt, db * 512:(db + 1) * 512], in1=tm[:], op=ALU.add), reads=[tm, yacc], writes=[yacc])
            for tt in range(4):
                self.op("act", lambda e, tt=tt: e.activation(out=yb[:], in_=yacc[:, tt, :], func=AF.Copy), reads=[yacc], writes=[yb])
                for half in range(2):
                    pb = bbf[half]
                    for i in range(8):
                        kc = half * 8 + i
                        self.op("pe", lambda e, pb=pb, i=i, kc=kc: e.transpose(out=pb[:, i * 128:(i + 1) * 128], in_=yb[:, kc * 128:(kc + 1) * 128], identity=self.ident[:]), reads=[yb, self.ident], writes=[pb])
                    self.op("act", lambda e, pb=pb, half=half, tt=tt: e.activation(out=yT2[:, half * 8:half * 8 + 8, tt * 128:(tt + 1) * 128], in_=pb[:, :].rearrange("p (k t) -> p k t", k=8), func=AF.Copy), reads=[pb], writes=[yT2])
            self.load_vec(V[0], ci, 5)
            self.load_ln(l, 1, V[1], V[2])
            for db in range(4):
                w = wo[db % 2]
                self.wload(w, lambda k0, k1, w=w: w[:, k0:k1, :], WO, db * 512, 512, 16)
                for tt in range(4):
                    bk = banks[4 + tt % 2]
                    self.mm(bk[:, :], [(yT2[:, kc, tt * 128:(tt + 1) * 128], w[:, kc, :]) for kc in range(16)], [yT2, w], bk)
                    self.op("act", lambda e, bk=bk, tt=tt, db=db: e.activation(out=yacc[:, tt, db * 512:(db + 1) * 512], in_=bk[:], func=AF.Copy), reads=[bk], writes=[yacc])
            for tt in range(4):
                tok = tok0 + tt * 128
                self.epilogue(l, 1, tok, lambda db, tt=tt: yacc[:, tt, db * 512:(db + 1) * 512], [yacc] * 4, xt[tt % 2], _V(yacc, tt), V[0], V[1], V[2], False, small)
    self.barrier()


def phase_mixer(self, l):
    self.mixer_alloc()
    sub = self.cfg.get("mix")
    def on(p):
        return sub is None or p in sub
    if on("in"):
        self.mixer_in(l)
    if on("hyf"):
        for L in sorted({s[1] for s in self.seqs()}):
            self.hy_filters(l, L)
    if on("hyp"):
        self.hy_prep(l)
    if on("hyc"):
        self.hy_conv(l)
    if on("gla"):
        self.gla(l)
    if on("diff"):
        self.diff(l)
    if on("out"):
        self.mixer_out(l)


for _f in (mixer_alloc, mixer_in, range_reduce, hy_filters, hy_prep, hy_conv, gla, diff, mixer_out, phase_mixer):
    setattr(MK, _f.__name__, _f)
MK.seqs = _seqs


def _shapes():
    sh = {
        "xs": ((TS, D), F32), "xp": ((TP, D), F32), "cvec": ((2, D), F32),
        "ck": ((DEPTH, 256, 1024), F32), "cv": ((DEPTH, 256, 1024), F32), "sg": ((DEPTH, 2, 4, 128, 256), F32),
        "w_mod": ((DEPTH, D, 9 * D), F32), "b_mod": ((DEPTH, 9 * D), F32), "ln_g": ((DEPTH, 3, D), F32), "ln_b": ((DEPTH, 3, D), F32),
        "ffn_w1": ((DEPTH, 2, D, FF), F32), "ffn_w3": ((DEPTH, 2, D, FF), F32), "ffn_w2": ((DEPTH, 2, FF, D), F32),
        "w_in": ((DEPTH, D, NCOL), F32), "hy_conv_w": ((DEPTH, 3, 3072), F32), "hy_conv_b": ((DEPTH, 3072), F32),
        "hy_w1": ((DEPTH, 17, 64), F32), "hy_b1": ((DEPTH, 64), F32), "hy_freq": ((DEPTH, 2, 64), F32),
        "hy_w2": ((DEPTH, 64, 64), F32), "hy_b2": ((DEPTH, 64), F32), "hy_w3": ((DEPTH, 64, 4096), F32),
        "hy_decay": ((DEPTH, 4096), F32), "hy_bias": ((DEPTH, 2, 1024), F32), "gla_wa": ((DEPTH, 2, 16, 512), F32),
        "gla_ba": ((DEPTH, 2, 512), F32), "gla_norm_g": ((DEPTH, 256), F32), "diff_lam": ((DEPTH, 4, 64), F32),
        "diff_norm_g": ((DEPTH, 128), F32), "w_branch_a": ((DEPTH, 1024, D), F32), "w_branch_b": ((DEPTH, 1024, D), F32),
        "w_branch_c": ((DEPTH, 1024, D), F32), "w_out": ((DEPTH, D, D), F32),
    }
    for k, v in _consts().items():
        sh[k] = (v.shape, BF16 if v.dtype == ml_dtypes.bfloat16 else F32)
    return sh


def build(cfg=None):
    cfg = cfg or {}
    nc = bass.Bass("TRN2", target_bir_lowering=False)
    k = MK(nc, _shapes(), cfg)
    k.out("ys", [TS, D])
    k.out("yp", [TP, D])
    k.out("nck", [2, DEPTH, 256, 1024])
    k.out("ncv", [2, DEPTH, 256, 1024])
    k.out("nsg", [2, DEPTH, 2, 4, 128, 256])
    phases = cfg.get("phases")
    with nc.allow_non_contiguous_dma(reason="small strided parameter loads"):
        with ExitStack() as st:
            k.phase_init(st)
            for l in range(cfg.get("depth", DEPTH)):
                phases = cfg.get("phases%d" % l, cfg.get("phases"))

                def on(p, phases=phases):
                    return phases is None or p in phases
                if on("mod"):
                    k.phase_mod(l)
                if on("ffn1"):
                    k.phase_ffn(l, 0)
                if on("mixer"):
                    k.phase_mixer(l)
                if on("ffn2"):
                    k.phase_ffn(l, 1)
            k.barrier()
    k.es.close()
    return nc, k


def _in_maps(inputs):
    c = _consts()
    maps = []
    f = lambda a: np.ascontiguousarray(np.asarray(a, dtype=np.float32))
    shared = {n: f(inputs[n]) for n in WNAMES}
    for b in range(8):
        m = dict(shared)
        m.update(c)
        m["xs"] = f(inputs["x_sample"][b])
        m["xp"] = f(inputs["x_prompt"][2 * b:2 * b + 2]).reshape(TP, D)
        m["cvec"] = f(np.stack([np.asarray(inputs["c"][b]), np.asarray(inputs["c_ctx"])]))
        m["ck"] = f(inputs["cache_k"][b]).reshape(DEPTH, 256, 1024)
        m["cv"] = f(inputs["cache_v"][b]).reshape(DEPTH, 256, 1024)
        m["sg"] = f(inputs["state_gla"][b])
        maps.append(m)
    return maps


def kernel(**inputs):
    nc, k = build()
    res = run_bass_kernel_spmd(nc, _in_maps(inputs), core_ids=list(range(8)))
    r = res.results
    ys = np.stack([r[b]["ys"] for b in range(8)])
    yp = np.concatenate([r[b]["yp"].reshape(2, 256, D) for b in range(8)])
    nck = np.concatenate([r[b]["nck"].reshape(2, DEPTH, 256, 8, 2, 64) for b in range(8)])
    ncv = np.concatenate([r[b]["ncv"].reshape(2, DEPTH, 256, 8, 128) for b in range(8)])
    nsg = np.concatenate([r[b]["nsg"] for b in range(8)])
    return (yp, ys, nck, ncv, nsg)
```

```python
import math
from contextlib import ExitStack
import numpy as np
import ml_dtypes
import concourse.bass as bass
import concourse.mybir as mybir
from concourse.bass_utils import run_bass_kernel_spmd

F32 = mybir.dt.float32
BF16 = mybir.dt.bfloat16
AF = mybir.ActivationFunctionType
ALU = mybir.AluOpType

D = 2048
FF = 5504
DEPTH = 2
NCOL = 15392
TS = 4096
TP = 512
T = TS + TP
ALPHA = (2 * DEPTH) ** 0.25
MAGIC = 12582912.0


class Buf:
    __slots__ = ("w", "r", "name")

    def __init__(self, name=""):
        self.w = {}
        self.r = {}
        self.name = name


class TT:
    __slots__ = ("t", "b")

    def __init__(self, t, b):
        self.t = t
        self.b = b

    def __getitem__(self, k):
        return self.t[k]


class KB:
    NDSEM = 36
    QSEM = {"sp": (0, 24), "pool": (24, 28), "cv": (28, 36)}

    def __init__(self, nc):
        self.nc = nc
        self.es = ExitStack()
        self.eng = {"pe": nc.tensor, "act": nc.scalar, "dve": nc.vector, "pool": nc.gpsimd, "sp": nc.sync}
        self.tick = {e: 0 for e in ("pe", "act", "dve", "pool")}
        self.sem = {e: self.es.enter_context(nc.semaphore("tick_" + e)) for e in self.tick}
        self.dsem = [self.es.enter_context(nc.semaphore("dma%d" % i)) for i in range(self.NDSEM)]
        self.dcnt = [0] * self.NDSEM
        self.qnext = {}
        self.waited = {}
        self.ninst = 0
        self.uid = 0
        self.wsi = 0
        self.wc = {}

    def sb(self, stack, name, shape, dt):
        self.uid += 1
        t = stack.enter_context(self.nc.sbuf_tensor("%s_%d" % (name, self.uid), list(shape), dt))
        return TT(t, Buf(name))

    def ps(self, stack, name, shape, dt=F32):
        self.uid += 1
        t = stack.enter_context(self.nc.psum_tensor("%s_%d" % (name, self.uid), list(shape), dt))
        return TT(t, Buf(name))

    def _semh(self, key):
        return self.sem[key] if isinstance(key, str) else self.dsem[key[1]]

    def _wait(self, engine, deps):
        e = self.eng[engine]
        for key, val in deps.items():
            wk = (engine, key)
            if self.waited.get(wk, 0) >= val:
                continue
            self.waited[wk] = val
            e.wait_ge(self._semh(key), val)
            self.ninst += 1

    @staticmethod
    def _merge(d, s):
        for k, v in s.items():
            if d.get(k, 0) < v:
                d[k] = v

    def _deps(self, reads, writes):
        deps = {}
        for b in reads:
            self._merge(deps, b.w)
        for b in writes:
            self._merge(deps, b.w)
            self._merge(deps, b.r)
        return deps

    @staticmethod
    def _bufs(xs):
        return [getattr(x, "b", x) for x in xs if x is not None]

    def op(self, engine, fn, reads=(), writes=(), sig=True):
        reads = self._bufs(reads)
        writes = self._bufs(writes)
        deps = self._deps(reads, writes)
        if engine == "pe":
            deps.pop("pe", None)
        self._wait(engine, deps)
        ins = fn(self.eng[engine])
        self.ninst += 1
        if sig:
            self.tick[engine] += 1
            ins.then_inc(self.sem[engine], 1)
            tok = {engine: self.tick[engine]}
        else:
            tok = {engine: self.tick[engine] + 1}
        for b in reads:
            self._merge(b.r, tok)
        for b in writes:
            self._merge(b.w, tok)
            b.r = {}
        return ins

    def dma(self, queue, out, in_, reads=(), writes=()):
        reads = self._bufs(reads)
        writes = self._bufs(writes)
        deps = self._deps(reads, writes)
        lo, hi = self.QSEM[queue]
        s = self.qnext.get(queue, lo)
        self.qnext[queue] = lo + (s + 1 - lo) % (hi - lo)
        if self.dcnt[s]:
            self._merge(deps, {("d", s): 16 * self.dcnt[s]})
        engine = "pool" if queue == "cv" else queue
        self._wait(engine, deps)
        ins = self.eng[engine].dma_start(out=out, in_=in_)
        self.ninst += 1
        self.dcnt[s] += 1
        ins.then_inc(self.dsem[s], 16)
        tok = {("d", s): 16 * self.dcnt[s]}
        for b in reads:
            self._merge(b.r, tok)
        for b in writes:
            self._merge(b.w, tok)
            b.r = {}
        return tok

    def barrier(self, full=False):
        deps = {e: v for e, v in self.tick.items() if v}
        cvlo, cvhi = self.QSEM["cv"]
        for i, c in enumerate(self.dcnt):
            if c and (full or not (cvlo <= i < cvhi)):
                deps[("d", i)] = 16 * c
        for e in ("sp", "pe", "act", "dve", "pool"):
            self._wait(e, dict(deps))


_CONST_CACHE = {}


def _bf(a):
    return np.ascontiguousarray(a.astype(ml_dtypes.bfloat16))


def _dft_consts(L):
    N = 2 * L
    t = np.arange(L, dtype=np.int64)
    f = np.arange(L, dtype=np.int64)
    ang = (np.outer(t, f) % N).astype(np.float64) * (2.0 * math.pi / N)
    C = np.cos(ang)
    S = -np.sin(ang)
    sgn = np.where(t % 2 == 0, 1.0, -1.0)
    Ffwd = np.concatenate([C, S], axis=1)
    Ffwd[:, L] = sgn
    Gre = (2.0 / N) * C.T
    Gre[0, :] = 1.0 / N
    Gim = (2.0 / N) * S.T
    Gim[0, :] = sgn / N
    Ginv = np.concatenate([Gre, Gim], axis=0)
    nt = L // 128
    Ft = Ffwd.reshape(nt, 128, 2 * nt, 128).transpose(2, 1, 0, 3)
    Gt = Ginv.reshape(2 * nt, 128, nt, 128).transpose(2, 1, 0, 3)
    nyq = sgn.reshape(nt, 128).T
    return _bf(Ft), _bf(Gt), _bf(nyq)


def _feats(L):
    t = np.linspace(0.0, 1.0, L, dtype=np.float32)
    w = (np.float32(2.0 * math.pi / L) * np.arange(L, dtype=np.float32))
    f = np.linspace(1e-4, 7, 8, dtype=np.float32)
    feats = np.concatenate([t[:, None], np.cos(w[:, None] * f), -np.sin(w[:, None] * f)], -1).astype(np.float32)
    featsT = np.ascontiguousarray(feats.T)
    ntl = np.ascontiguousarray((-t).reshape(L // 128, 128).T)
    return featsT, ntl.astype(np.float32)


def _rope_tables():
    L = TS
    rows = L // 64
    r = np.repeat(np.arange(rows, dtype=np.float32), 64)
    col = np.tile(np.arange(64, dtype=np.float32), rows)
    nf = 16
    inv = (10000.0 ** (-np.arange(nf, dtype=np.float32) / nf)).astype(np.float32)
    ang = np.stack([r[:, None] * inv, col[:, None] * inv], axis=1)
    cos = np.cos(ang).astype(np.float32)
    sin = np.sin(ang).astype(np.float32)
    c64 = np.stack([cos, cos], axis=2).reshape(L, 64)
    s64 = np.stack([-sin, sin], axis=2).reshape(L, 64)
    return np.ascontiguousarray(np.tile(c64, (1, 16))), np.ascontiguousarray(np.tile(s64, (1, 16)))


def _gla_mats():
    j = np.arange(128)[:, None]
    i = np.arange(128)[None, :]
    s = -1.0 / 16.0
    m = np.zeros((6, 128, 128), np.float32)
    m[0] = s * ((j <= i).astype(np.float32) - (j <= 63).astype(np.float32))
    m[1] = s * (j <= i)
    m[2] = s * (j > i)
    m[3] = s * ((j >= i).astype(np.float32) - (j >= 64).astype(np.float32))
    m[4] = s * (j >= i)
    m[5] = s * (j < i)
    masks = np.zeros((2, 128, 128), np.float32)
    masks[0] = (j <= i)
    masks[1] = (j >= i)
    return m, masks


def _consts():
    if _CONST_CACHE:
        return _CONST_CACHE
    c = _CONST_CACHE
    c["c_f4096"], c["c_g4096"], c["c_nyq4096"] = _dft_consts(4096)
    c["c_f256"], c["c_g256"], c["c_nyq256"] = _dft_consts(256)
    c["c_featsT4096"], c["c_ntl4096"] = _feats(4096)
    c["c_featsT256"], c["c_ntl256"] = _feats(256)
    c["c_cosf"], c["c_sinf"] = _rope_tables()
    c["c_glam"], c["c_gmask"] = _gla_mats()
    c["c_ident"] = _bf(np.eye(128, dtype=np.float32))
    return c


WNAMES = ["w_mod", "b_mod", "ln_g", "ln_b", "ffn_w1", "ffn_w3", "ffn_w2", "w_in", "hy_conv_w", "hy_conv_b",
          "hy_w1", "hy_b1", "hy_freq", "hy_w2", "hy_b2", "hy_w3", "hy_decay", "hy_bias", "gla_wa", "gla_ba",
          "gla_norm_g", "diff_lam", "diff_norm_g", "w_branch_a", "w_branch_b", "w_branch_c", "w_out"]


class MK(KB):
    def __init__(self, nc, shapes, cfg):
        super().__init__(nc)
        self.cfg = cfg
        self.I = {}
        for name, (shape, dt) in shapes.items():
            self.I[name] = nc.dram_tensor(name, list(shape), dt, kind="ExternalInput").ap()
        self.O = {}
        self.dbg = cfg.get("dbg", ())

    def out(self, name, shape, dt=F32):
        self.O[name] = self.nc.dram_tensor(name, list(shape), dt, kind="ExternalOutput").ap()
        return self.O[name]

    def scratch(self, name, shape, dt=F32):
        if name in self.dbg:
            return self.out(name, shape, dt)
        return self.nc.dram_tensor(name, list(shape), dt).ap()

    def mm(self, psum_ap, pairs, reads, pw):
        n = len(pairs)
        for i, (l, r) in enumerate(pairs):
            self.op("pe", lambda e, l=l, r=r, i=i: e.matmul(psum_ap, lhsT=l, rhs=r, start=(i == 0), stop=(i == n - 1)),
                    reads=reads, writes=[pw], sig=(i == n - 1))

    def wstage_alloc(self, st, n=3):
        pass

    def wload_cast(self, dst, dst_ap_fn, src2d, c0, ncols, nk, rows0=0):
        for k0 in range(0, nk, 8):
            k1 = min(nk, k0 + 8)
            src = src2d[rows0 + k0 * 128: rows0 + k1 * 128, c0:c0 + ncols].rearrange("(kc p) n -> p kc n", p=128)
            self.dma("pool", dst_ap_fn(k0, k1), src, writes=[dst])

    def wconv(self, key, src2d, c0, ncols, nk, rows0=0):
        self.uid += 1
        t = self.nc.dram_tensor("wc_%d" % self.uid, [128, nk, ncols], BF16).ap()
        b = Buf("wc")
        for k0 in range(0, nk, 8):
            k1 = min(nk, k0 + 8)
            src = src2d[rows0 + k0 * 128: rows0 + k1 * 128, c0:c0 + ncols].rearrange("(kc p) n -> p kc n", p=128)
            self.dma("cv", t[:, k0:k1, :], src, writes=[b])
        self.wc[key] = (t, b)

    def wload(self, dst, dst_ap_fn, src2d, c0, ncols, nk, rows0=0, key=None):
        t, b = self.wc[key]
        self.dma("sp", dst_ap_fn(0, nk), t[:, :, :], reads=[b], writes=[dst])

    def groups(self):
        return self.cfg.get("groups", list(range(9)))

    def gci(self, g):
        return 0 if g < 8 else 1

    def phase_init(self, st):
        nc = self.nc
        self.ident = self.sb(st, "ident", [128, 128], BF16)
        self.dma("sp", self.ident[:], self.I["c_ident"][:, :], writes=[self.ident])
        self.ones = self.sb(st, "ones", [128, 128], F32)
        self.op("dve", lambda e: e.memset(self.ones[:], 1.0), writes=[self.ones])
        self.eps5 = self.sb(st, "eps5", [128, 1], F32)
        self.op("dve", lambda e: e.memset(self.eps5[:], 1e-5), writes=[self.eps5])
        self.eps6 = self.sb(st, "eps6", [128, 1], F32)
        self.op("dve", lambda e: e.memset(self.eps6[:], 1e-6), writes=[self.eps6])
        self.X = self.scratch("X", [T, D])
        self.MODBS = [self.scratch("MODB" if i == 0 else "MODB%d" % i, [2, 128, 9 * D]) for i in range(DEPTH)]
        self.MODB = self.MODBS[0]
        for i in range(0, TS, 1024):
            self.dma("sp", self.X[i:i + 1024, :], self.I["xs"][i:i + 1024, :])
        self.dma("sp", self.X[TS:T, :], self.I["xp"][:, :])
        self.barrier()


    def phase_convert(self, l):
        I = self.I
        fblocks = [(i * 256, 256) for i in range(21)] + [(21 * 256, 128)]

        def ffn(fi):
            for (c0, nc_) in fblocks:
                self.wconv((l, 'w1', fi, c0), I["ffn_w1"][l, fi], c0, nc_, 16)
                self.wconv((l, 'w3', fi, c0), I["ffn_w3"][l, fi], c0, nc_, 16)
            for db in range(4):
                for ch in range(11):
                    f0 = ch * 4
                    nf = min(4, 43 - f0)
                    self.wconv((l, 'w2', fi, db, f0), I["ffn_w2"][l, fi], db * 512, 512, nf, rows0=f0 * 128)
        ffn(0)
        W = I["w_in"][l]
        for c0 in list(range(0, 4096, 512)):
            self.wconv((l, 'win', c0), W, c0, 512, 16)
        self.wconv((l, 'win', 6144), W, 6144, 32, 16)
        for c0 in [4096 + i * 512 for i in range(4)] + [6176 + i * 512 for i in range(6)] + [9248 + i * 512 for i in range(12)]:
            self.wconv((l, 'win', c0), W, c0, 512, 16)
        for br, n in enumerate(("w_branch_a", "w_branch_b", "w_branch_c")):
            for db in range(4):
                self.wconv((l, 'wbr', br, db), I[n][l], db * 512, 512, 8)
        for db in range(4):
            self.wconv((l, 'wo', db), I["w_out"][l], db * 512, 512, 16)
        ffn(1)

    def phase_mod(self, l):
        I = self.I
        self.MODB = self.MODBS[l]
        with ExitStack() as st:
            cv = self.sb(st, "cv", [128, 2, 16], F32)
            sc = self.sb(st, "sc", [128, 2, 16], F32)
            lhs = self.sb(st, "modlhs", [128, 2, 16, 128], BF16)
            self.dma("sp", cv[:], I["cvec"].rearrange("c (kc p) -> p c kc", p=128), writes=[cv])
            self.op("act", lambda e: e.activation(out=sc[:], in_=cv[:], func=AF.Silu), reads=[cv], writes=[sc])
            for ci in range(2):
                for kc in range(16):
                    self.op("dve", lambda e, ci=ci, kc=kc: e.tensor_scalar(
                        out=lhs[:, ci, kc, :], in0=self.ones[:, :], scalar1=sc[:, ci, kc:kc + 1], scalar2=None, op0=ALU.mult),
                        reads=[self.ones, sc], writes=[lhs])
            self.wstage_alloc(st)
            wb = [self.sb(st, "modw%d" % i, [128, 16, 512], BF16) for i in range(2)]
            bb = [self.sb(st, "modb%d" % i, [128, 512], F32) for i in range(2)]
            res = [self.sb(st, "modr%d" % i, [128, 512], F32) for i in range(4)]
            banks = [self.ps(st, "modp%d" % i, [128, 512]) for i in range(4)]
            wsrc = I["w_mod"][l]
            for nb in range(36):
                w = wb[nb % 2]
                b = bb[nb % 2]
                self.wload_cast(w, lambda k0, k1, w=w: w[:, k0:k1, :], wsrc, nb * 512, 512, 16)
                self.dma("sp", b[:], I["b_mod"][l, nb * 512:(nb + 1) * 512].partition_broadcast(128), writes=[b])
                for ci in range(2):
                    bk = banks[(nb * 2 + ci) % 4]
                    r = res[(nb * 2 + ci) % 4]
                    self.mm(bk[:, :], [(lhs[:, ci, kc, :], w[:, kc, :]) for kc in range(16)], [lhs, w], bk)
                    self.op("dve", lambda e, bk=bk, r=r, b=b: e.tensor_tensor(out=r[:], in0=bk[:], in1=b[:], op=ALU.add),
                            reads=[bk, b], writes=[r])
                    self.dma("sp", self.MODB[ci, :, nb * 512:(nb + 1) * 512], r[:], reads=[r])
        self.barrier()

    def load_vec(self, dst, ci, idx):
        self.dma("sp", dst[:], self.MODB[ci, :, idx * D:(idx + 1) * D], writes=[dst])

    def prologue(self, g, xt, hb, hT, V0, V1, banks_bf):
        for tt in range(4):
            tok = g * 512 + tt * 128
            x = xt[tt % 2]
            self.dma("sp", x[:], self.X[tok:tok + 128, :], writes=[x])
            self.op("dve", lambda e, x=x: e.tensor_tensor(out=x[:], in0=x[:], in1=V0[:], op=ALU.mult), reads=[x, V0], writes=[x])
            self.op("dve", lambda e, x=x: e.tensor_tensor(out=hb[:], in0=x[:], in1=V1[:], op=ALU.add), reads=[x, V1], writes=[hb])
            for half in range(2):
                pb = banks_bf[half]
                for i in range(8):
                    kc = half * 8 + i
                    self.op("pe", lambda e, pb=pb, i=i, kc=kc: e.transpose(
                        out=pb[:, i * 128:(i + 1) * 128], in_=hb[:, kc * 128:(kc + 1) * 128], identity=self.ident[:]),
                        reads=[hb, self.ident], writes=[pb])
                self.op("act", lambda e, pb=pb, half=half, tt=tt: e.activation(
                    out=hT[:, half * 8:half * 8 + 8, tt * 128:(tt + 1) * 128],
                    in_=pb[:, :].rearrange("p (k t) -> p k t", k=8), func=AF.Copy),
                    reads=[pb], writes=[hT])

    def epilogue(self, l, i, tok, osrc, osrc_bufs, xt, tbuf, V0, V1, V2, half_gate, small):
        stats, mv, rstd = small
        self.dma("sp", xt[:], self.X[tok:tok + 128, :], writes=[xt])
        for db in range(4):
            sl = slice(db * 512, (db + 1) * 512)
            self.op("dve", lambda e, db=db, sl=sl: e.scalar_tensor_tensor(
                out=tbuf[:, sl], in0=osrc(db), scalar=(0.5 if half_gate else 1.0), in1=V0[:, sl], op0=ALU.mult, op1=ALU.mult),
                reads=[osrc_bufs[db], V0], writes=[tbuf])
        self.op("dve", lambda e: e.scalar_tensor_tensor(out=tbuf[:], in0=xt[:], scalar=ALPHA, in1=tbuf[:], op0=ALU.mult, op1=ALU.add),
                reads=[xt, tbuf], writes=[tbuf])
        for c in range(4):
            self.op("dve", lambda e, c=c: e.bn_stats(out=stats[:, c, :], in_=tbuf[:, c * 512:(c + 1) * 512]), reads=[tbuf], writes=[stats])
        self.op("dve", lambda e: e.bn_aggr(out=mv[:], in_=stats[:]), reads=[stats], writes=[mv])
        self.op("act", lambda e: e.activation(out=rstd[:], in_=mv[:, 1:2], func=AF.Sqrt, bias=self.eps5[:, 0:1], scale=1.0),
                reads=[mv, self.eps5], writes=[rstd])
        self.op("dve", lambda e: e.reciprocal(out=rstd[:], in_=rstd[:]), reads=[rstd], writes=[rstd])
        self.op("dve", lambda e: e.tensor_scalar(out=tbuf[:], in0=tbuf[:], scalar1=mv[:, 0:1], scalar2=rstd[:, 0:1],
                                                 op0=ALU.subtract, op1=ALU.mult), reads=[tbuf, mv, rstd], writes=[tbuf])
        self.op("dve", lambda e: e.tensor_tensor(out=tbuf[:], in0=tbuf[:], in1=V1[:], op=ALU.mult), reads=[tbuf, V1], writes=[tbuf])
        self.op("dve", lambda e: e.tensor_tensor(out=xt[:], in0=tbuf[:], in1=V2[:], op=ALU.add), reads=[tbuf, V2], writes=[xt])
        self.dma("sp", self.X[tok:tok + 128, :], xt[:], reads=[xt])
        if l == DEPTH - 1 and i == 2:
            if tok < TS:
                self.dma("sp", self.O["ys"][tok:tok + 128, :], xt[:], reads=[xt])
            else:
                self.dma("sp", self.O["yp"][tok - TS:tok - TS + 128, :], xt[:], reads=[xt])

    def load_ln(self, l, i, V1, V2):
        self.dma("sp", V1[:], self.I["ln_g"][l, i, :].partition_broadcast(128), writes=[V1])
        self.dma("sp", V2[:], self.I["ln_b"][l, i, :].partition_broadcast(128), writes=[V2])

    def phase_ffn(self, l, fi):
        I = self.I
        sub = 0 if fi == 0 else 2
        w1 = I["ffn_w1"][l, fi]
        w3 = I["ffn_w3"][l, fi]
        w2 = I["ffn_w2"][l, fi]
        fblocks = [(i * 256, 256) for i in range(21)] + [(21 * 256, 128)]
        with ExitStack() as st:
            hT = self.sb(st, "hT", [128, 16, 512], BF16)
            gT = self.sb(st, "gT", [128, 43, 512], BF16)
            wA = [self.sb(st, "w1b%d" % i, [128, 16, 256], BF16) for i in range(2)]
            wB = [self.sb(st, "w3b%d" % i, [128, 16, 256], BF16) for i in range(2)]
            wC = [self.sb(st, "w2b%d" % i, [128, 4, 512], BF16) for i in range(2)]
            self.wstage_alloc(st)
            V = [self.sb(st, "V%d" % i, [128, D], F32) for i in range(3)]
            xt = [self.sb(st, "xt%d" % i, [128, D], F32) for i in range(2)]
            hb = self.sb(st, "hb", [128, D], BF16)
            tmp = [self.sb(st, "tmp%d" % i, [128, 512], F32) for i in range(2)]
            tb = self.sb(st, "tb", [128, 4, D], F32)
            small = (self.sb(st, "stats", [128, 4, 6], F32), self.sb(st, "mv", [128, 2], F32), self.sb(st, "rstd", [128, 1], F32))
            banks = [self.ps(st, "bk%d" % i, [128, 512]) for i in range(6)]
            bbf = [self.ps(st, "bbf%d" % i, [128, 1024], BF16) for i in range(2)]
            for g in self.groups():
                ci = self.gci(g)
                self.load_vec(V[0], ci, 3 * sub + 1)
                self.load_vec(V[1], ci, 3 * sub + 0)
                self.op("dve", lambda e: e.tensor_scalar(out=V[0][:], in0=V[0][:], scalar1=1.0, scalar2=None, op0=ALU.add),
                        reads=[V[0]], writes=[V[0]])
                self.prologue(g, xt, hb, hT, V[0], V[1], bbf)
                fidx = 0
                for bi, (c0, nc_) in enumerate(fblocks):
                    a = wA[bi % 2]
                    b = wB[bi % 2]
                    self.wload(a, lambda k0, k1, a=a, nc_=nc_: a[:, k0:k1, 0:nc_], w1, c0, nc_, 16, key=(l, 'w1', fi, c0))
                    self.wload(b, lambda k0, k1, b=b, nc_=nc_: b[:, k0:k1, 0:nc_], w3, c0, nc_, 16, key=(l, 'w3', fi, c0))
                    for ft in range(nc_ // 128):
                        pa = banks[(fidx % 2) * 2]
                        pb = banks[(fidx % 2) * 2 + 1]
                        tm = tmp[fidx % 2]
                        self.mm(pa[:, :], [(a[:, kc, ft * 128:(ft + 1) * 128], hT[:, kc, :]) for kc in range(16)], [a, hT], pa)
                        self.mm(pb[:, :], [(b[:, kc, ft * 128:(ft + 1) * 128], hT[:, kc, :]) for kc in range(16)], [b, hT], pb)
                        self.op("act", lambda e, pa=pa, tm=tm: e.activation(out=tm[:], in_=pa[:], func=AF.Silu), reads=[pa], writes=[tm])
                        self.op("dve", lambda e, pb=pb, tm=tm, fidx=fidx: e.tensor_tensor(out=gT[:, fidx, :], in0=tm[:], in1=pb[:], op=ALU.mult),
                                reads=[tm, pb], writes=[gT])
                        fidx += 1
                self.load_vec(V[0], ci, 3 * sub + 2)
                self.load_ln(l, sub, V[1], V[2])
                nch = 11
                wi = 0
                for db in range(4):
                    for ch in range(nch):
                        f0 = ch * 4
                        nf = min(4, 43 - f0)
                        w = wC[wi % 2]
                        wi += 1
                        self.wload(w, lambda k0, k1, w=w: w[:, k0:k1, :], w2, db * 512, 512, nf, rows0=f0 * 128, key=(l, 'w2', fi, db, f0))
                        for tt in range(4):
                            pk = banks[2 + tt]
                            for fi_ in range(nf):
                                fc = f0 + fi_
                                self.op("pe", lambda e, pk=pk, w=w, fi_=fi_, fc=fc, tt=tt: e.matmul(
                                    pk[:, :], lhsT=gT[:, fc, tt * 128:(tt + 1) * 128], rhs=w[:, fi_, :], start=(fc == 0), stop=(fc == 42)),
                                    reads=[gT, w], writes=[pk], sig=(fi_ == nf - 1))
                    for tt in range(4):
                        pk = banks[2 + tt]
                        self.op("act", lambda e, pk=pk, tt=tt, db=db: e.activation(out=tb[:, tt, db * 512:(db + 1) * 512], in_=pk[:], func=AF.Copy),
                                reads=[pk], writes=[tb])
                for tt in range(4):
                    tok = g * 512 + tt * 128
                    x = xt[tt % 2]
                    self.epilogue(l, sub, tok, lambda db, tt=tt: tb[:, tt, db * 512:(db + 1) * 512], [tb] * 4, x, _V(tb, tt), V[0], V[1], V[2], True, small)
        self.barrier()


class _V:
    def __init__(self, t, tt):
        self.t = t
        self.tt = tt
        self.b = t.b

    def __getitem__(self, k):
        if isinstance(k, tuple):
            return self.t.t[(k[0], self.tt) + tuple(k[1:])]
        return self.t.t[k, self.tt]


SEQS = [(0, TS, 0, -1), (TS, 256, 1, 0), (TS + 256, 256, 1, 1)]


def _seqs(self):
    return [s for s in SEQS if (s[0] // 512) in self.groups()]


def mixer_alloc(self):
    if hasattr(self, "ZHT"):
        return
    S = self.scratch
    self.ZHT = S("ZHT", [3072, T]); self.GQT = S("GQT", [512, T]); self.GKT = S("GKT", [512, T])
    self.GK = S("GK", [T, 512]); self.GV = S("GV", [T, 1024], BF16); self.GR = S("GR", [T, 1024])
    self.GLRT = S("GLRT", [2, 16, T]); self.DQ = S("DQ", [T, 1024]); self.DK = S("DK", [T, 1024]); self.DV = S("DV", [T, 1024])
    self.GATES = S("GATES", [T, 6144])
    self.HTOK = [S("HTOK%d" % i, [T, 1024], BF16) for i in range(3)]
    self.KF = {4096: S("KF4096", [2, 8192, 1024]), 256: S("KF256", [2, 512, 1024])}
    self.YT = [S("Y%sT" % n, [1024, T], BF16) for n in "ABC"]
    self.OF = S("OF", [T, 1024])
    self.DQT = S("DQT", [8, 128, T], BF16); self.DKT = S("DKT", [8, 128, 256 + T], BF16); self.VA = S("VA", [256 + T, 8, 129], BF16)


def mixer_in(self, l):
    I = self.I
    W = I["w_in"][l]
    feat = [(c0, 512, self.ZHT, c0) for c0 in range(0, 3072, 512)] + [(3072, 512, self.GQT, 0), (3584, 512, self.GKT, 0)]
    tokb = [(3584, self.GK, 0, AF.Copy, F32, None)]
    tokb += [(4096 + i * 512, self.GV, i * 512, AF.Copy, BF16, None) for i in range(2)]
    tokb += [(5120 + i * 512, self.GR, i * 512, AF.Copy, F32, None) for i in range(2)]
    tokb += [(6176 + i * 512, self.DQ, i * 512, AF.Copy, F32, None) for i in range(2)]
    tokb += [(7200 + i * 512, self.DK, i * 512, AF.Copy, F32, "nck") for i in range(2)]
    tokb += [(8224 + i * 512, self.DV, i * 512, AF.Copy, F32, "ncv") for i in range(2)]
    tokb += [(9248 + i * 512, self.GATES, i * 512, AF.Sigmoid, F32, None) for i in range(12)]
    with ExitStack() as st:
        hT = self.sb(st, "hT", [128, 16, 512], BF16)
        wb = [self.sb(st, "wi%d" % i, [128, 16, 512], BF16) for i in range(2)]
        self.wstage_alloc(st)
        V = [self.sb(st, "V%d" % i, [128, D], F32) for i in range(2)]
        xt = [self.sb(st, "xt%d" % i, [128, D], F32) for i in range(2)]
        hb = self.sb(st, "hb", [128, D], BF16)
        stf = [self.sb(st, "stf%d" % i, [128, 512], F32) for i in range(4)]
        stb = [self.sb(st, "stb%d" % i, [128, 512], BF16) for i in range(2)]
        banks = [self.ps(st, "bk%d" % i, [128, 512]) for i in range(4)]
        bbf = [self.ps(st, "bbf%d" % i, [128, 1024], BF16) for i in range(2)]
        cnt = 0
        for g in self.groups():
            ci = self.gci(g)
            tok0 = g * 512
            self.load_vec(V[0], ci, 4)
            self.load_vec(V[1], ci, 3)
            self.op("dve", lambda e: e.tensor_scalar(out=V[0][:], in0=V[0][:], scalar1=1.0, scalar2=None, op0=ALU.add), reads=[V[0]], writes=[V[0]])
            self.prologue(g, xt, hb, hT, V[0], V[1], bbf)
            wi = 0
            for (c0, ncl, dst, r0) in feat:
                w = wb[wi % 2]; wi += 1
                self.wload(w, lambda k0, k1, w=w: w[:, k0:k1, :], W, c0, 512, 16, key=(l, 'win', c0))
                for ct in range(4):
                    bk = banks[cnt % 4]; sg = stf[cnt % 4]; cnt += 1
                    self.mm(bk[:, :], [(w[:, kc, ct * 128:(ct + 1) * 128], hT[:, kc, :]) for kc in range(16)], [w, hT], bk)
                    eng = "act" if cnt % 2 else "dve"
                    if eng == "act":
                        self.op("act", lambda e, bk=bk, sg=sg: e.activation(out=sg[:], in_=bk[:], func=AF.Copy), reads=[bk], writes=[sg])
                    else:
                        self.op("dve", lambda e, bk=bk, sg=sg: e.tensor_copy(out=sg[:], in_=bk[:]), reads=[bk], writes=[sg])
                    self.dma("sp", dst[r0 + ct * 128:r0 + (ct + 1) * 128, tok0:tok0 + 512], sg[:], reads=[sg])
            w = wb[wi % 2]; wi += 1
            self.wload(w, lambda k0, k1, w=w: w[:, k0:k1, 0:32], W, 6144, 32, 16, key=(l, 'win', 6144))
            for dd in range(2):
                bk = banks[cnt % 4]; sg = stf[cnt % 4]; cnt += 1
                self.mm(bk[0:16, :], [(w[:, kc, dd * 16:(dd + 1) * 16], hT[:, kc, :]) for kc in range(16)], [w, hT], bk)
                self.op("act", lambda e, bk=bk, sg=sg: e.activation(out=sg[0:16, :], in_=bk[0:16, :], func=AF.Copy), reads=[bk], writes=[sg])
                self.dma("sp", self.GLRT[dd, :, tok0:tok0 + 512], sg[0:16, :], reads=[sg])
            for (c0, dst, dc0, fn, dt, oname) in tokb:
                w = wb[wi % 2]; wi += 1
                self.wload(w, lambda k0, k1, w=w: w[:, k0:k1, :], W, c0, 512, 16, key=(l, 'win', c0))
                for tt in range(4):
                    bk = banks[cnt % 4]
                    sg = stf[cnt % 4] if dt == F32 else stb[cnt % 2]
                    cnt += 1
                    self.mm(bk[:, :], [(hT[:, kc, tt * 128:(tt + 1) * 128], w[:, kc, :]) for kc in range(16)], [w, hT], bk)
                    if fn == AF.Copy and cnt % 2 == 0:
                        self.op("dve", lambda e, bk=bk, sg=sg: e.tensor_copy(out=sg[:], in_=bk[:]), reads=[bk], writes=[sg])
                    else:
                        self.op("act", lambda e, bk=bk, sg=sg, fn=fn: e.activation(out=sg[:], in_=bk[:], func=fn), reads=[bk], writes=[sg])
                    tok = tok0 + tt * 128
                    self.dma("sp", dst[tok:tok + 128, dc0:dc0 + 512], sg[:], reads=[sg])
                    if oname is not None and tok >= TS:
                        pi = (tok - TS) // 256
                        t0 = (tok - TS) % 256
                        self.dma("sp", self.O[oname][pi, l, t0:t0 + 128, dc0:dc0 + 512], sg[:], reads=[sg])
    self.barrier()


def range_reduce(self, a, tmp, reads):
    self.op("dve", lambda e: e.tensor_scalar(out=tmp, in0=a, scalar1=1.0 / (2 * math.pi), scalar2=MAGIC, op0=ALU.mult, op1=ALU.add), reads=reads, writes=reads)
    self.op("dve", lambda e: e.tensor_scalar(out=tmp, in0=tmp, scalar1=-MAGIC, scalar2=-2 * math.pi, op0=ALU.add, op1=ALU.mult), reads=reads, writes=reads)
    self.op("dve", lambda e: e.tensor_tensor(out=a, in0=a, in1=tmp, op=ALU.add), reads=reads, writes=reads)


def hy_filters(self, l, L):
    I = self.I
    nt = L // 128
    KF = self.KF[L]
    Fc = I["c_f%d" % L]; NY = I["c_nyq%d" % L]
    with ExitStack() as st:
        fT = self.sb(st, "fT", [17, L], F32)
        w1 = self.sb(st, "hw1", [17, 64], F32); w2 = self.sb(st, "hw2", [64, 64], F32); w3 = self.sb(st, "hw3", [64, 4096], F32)
        pc = self.sb(st, "hpc", [64, 4], F32)
        ntl = self.sb(st, "ntl", [128, nt], F32)
        nyq = self.sb(st, "nyq", [128, nt], BF16)
        hd1 = self.sb(st, "hd1", [64, L], F32); hd2 = self.sb(st, "hd2", [64, L], F32)
        a_ = self.sb(st, "ha", [64, 512], F32); tm_ = self.sb(st, "htm", [64, 512], F32)
        banks = [self.ps(st, "bk%d" % i, [128, 512]) for i in range(5)]
        self.dma("sp", fT[:], I["c_featsT%d" % L][:, :], writes=[fT])
        self.dma("sp", w1[:], I["hy_w1"][l], writes=[w1]); self.dma("sp", w2[:], I["hy_w2"][l], writes=[w2]); self.dma("sp", w3[:], I["hy_w3"][l], writes=[w3])
        self.dma("sp", pc[:, 0:1], I["hy_b1"][l].rearrange("(m o) -> m o", o=1), writes=[pc])
        self.dma("sp", pc[:, 1:2], I["hy_b2"][l].rearrange("(m o) -> m o", o=1), writes=[pc])
        self.dma("sp", pc[:, 2:4], I["hy_freq"][l].rearrange("a m -> m a"), writes=[pc])
        self.dma("sp", ntl[:], I["c_ntl%d" % L][:, :], writes=[ntl])
        self.dma("sp", nyq[:], NY[:, :], writes=[nyq])
        n = min(512, L)
        for (wsrc, src, dst, bcol, fcol, kk) in ((w1, fT, hd1, 0, 2, 17), (w2, hd1, hd2, 1, 3, 64)):
            for tb in range(L // n):
                bk = banks[tb % 2]
                self.mm(bk[0:64, 0:n], [(wsrc[0:kk, :], src[0:kk, tb * n:(tb + 1) * n])], [wsrc, src], bk)
                self.op("dve", lambda e, bk=bk, bcol=bcol, fcol=fcol: e.tensor_scalar(
                    out=a_[:, 0:n], in0=bk[0:64, 0:n], scalar1=pc[:, bcol:bcol + 1], scalar2=pc[:, fcol:fcol + 1], op0=ALU.add, op1=ALU.mult),
                    reads=[bk, pc], writes=[a_])
                self.op("dve", lambda e: e.tensor_scalar(out=tm_[:, 0:n], in0=a_[:, 0:n], scalar1=1.0 / (2 * math.pi), scalar2=MAGIC, op0=ALU.mult, op1=ALU.add), reads=[a_], writes=[tm_])
                self.op("dve", lambda e: e.tensor_scalar(out=tm_[:, 0:n], in0=tm_[:, 0:n], scalar1=-MAGIC, scalar2=-2 * math.pi, op0=ALU.add, op1=ALU.mult), reads=[tm_], writes=[tm_])
                self.op("dve", lambda e: e.tensor_tensor(out=a_[:, 0:n], in0=a_[:, 0:n], in1=tm_[:, 0:n], op=ALU.add), reads=[a_, tm_], writes=[a_])
                self.op("act", lambda e, dst=dst, tb=tb: e.activation(out=dst[:, tb * n:(tb + 1) * n], in_=a_[:, 0:n], func=AF.Sin), reads=[a_], writes=[dst])
        spl = self.sb(st, "spl", [128, nt, 512], BF16); smi = self.sb(st, "smi", [128, nt, 512], BF16)
        dec = [self.sb(st, "dec%d" % i, [128, 512], F32) for i in range(2)]
        ex = [self.sb(st, "ex%d" % i, [128, 512], F32) for i in range(2)]
        hh = [self.sb(st, "hh%d" % i, [128, 512], F32) for i in range(2)]
        ft = [self.sb(st, "ft%d" % i, [128, nt, 128], BF16) for i in range(2)]
        stg = [self.sb(st, "kst%d" % i, [128, 512], F32) for i in range(2)]
        fcnt = 0
        for o in range(2):
            for cb in range(2):
                for dr in range(2):
                    c0 = o * 2048 + dr * 1024 + cb * 512
                    self.dma("sp", dec[dr][:], I["hy_decay"][l, c0:c0 + 512].partition_broadcast(128), writes=[dec[dr]])
                    self.op("act", lambda e, dr=dr: e.activation(out=dec[dr][:], in_=dec[dr][:], func=AF.Abs), reads=[dec[dr]], writes=[dec[dr]])
                for tt in range(nt):
                    for dr in range(2):
                        c0 = o * 2048 + dr * 1024 + cb * 512
                        bk = banks[dr]
                        self.mm(bk[:, :], [(hd2[:, tt * 128:(tt + 1) * 128], w3[:, c0:c0 + 512])], [hd2, w3], bk)
                        self.op("act", lambda e, dr=dr, tt=tt: e.activation(out=ex[dr][:], in_=dec[dr][:], func=AF.Exp, scale=ntl[:, tt:tt + 1]), reads=[dec[dr], ntl], writes=[ex[dr]])
                        self.op("dve", lambda e, dr=dr, bk=bk: e.tensor_tensor(out=hh[dr][:], in0=ex[dr][:], in1=bk[:], op=ALU.mult), reads=[ex[dr], bk], writes=[hh[dr]])
                    if tt == 0:
                        self.op("dve", lambda e: e.memset(hh[1][0:1, :], 0.0), writes=[hh[1]])
                    self.op("dve", lambda e, tt=tt: e.tensor_tensor(out=spl[:, tt, :], in0=hh[0][:], in1=hh[1][:], op=ALU.add), reads=hh, writes=[spl])
                    self.op("dve", lambda e, tt=tt: e.tensor_tensor(out=smi[:, tt, :], in0=hh[0][:], in1=hh[1][:], op=ALU.subtract), reads=hh, writes=[smi])
                for fti in range(2 * nt):
                    src = spl if fti < nt else smi
                    f = ft[fcnt % 2]; sg = stg[fcnt % 2]; bk = banks[2 + fcnt % 2]; fcnt += 1
                    self.dma("sp", f[:], Fc[fti], writes=[f])
                    self.mm(bk[:, :], [(f[:, kc, :], src[:, kc, :]) for kc in range(nt)], [f, src], bk)
                    self.op("act", lambda e, bk=bk, sg=sg: e.activation(out=sg[:], in_=bk[:], func=AF.Copy), reads=[bk], writes=[sg])
                    if fti == nt:
                        b4 = banks[4]
                        self.mm(b4[0:1, :], [(nyq[:, kc:kc + 1], spl[:, kc, :]) for kc in range(nt)], [nyq, spl], b4)
                        self.op("act", lambda e, sg=sg, b4=b4: e.activation(out=sg[0:1, :], in_=b4[0:1, :], func=AF.Copy), reads=[b4], writes=[sg])
                    self.dma("sp", KF[o, fti * 128:(fti + 1) * 128, cb * 512:(cb + 1) * 512], sg[:], reads=[sg])
    self.barrier()


def hy_prep(self, l):
    I = self.I
    for (s0, L, ci, pi) in self.seqs():
        nt = L // 128
        with ExitStack() as st:
            cT = self.sb(st, "cT", [128, 4, L], BF16)
            u = [self.sb(st, "u%d" % i, [128, L + 2], F32) for i in range(2)]
            acc = self.sb(st, "acc", [128, L], F32)
            wc = [self.sb(st, "wc%d" % i, [128, 4], F32) for i in range(2)]
            stg = [self.sb(st, "pst%d" % i, [128, 512], BF16) for i in range(2)]
            bbf = [self.ps(st, "bbf%d" % i, [128, 1024], BF16) for i in range(2)]
            for i in range(2):
                self.op("dve", lambda e, i=i: e.memset(u[i][:, 0:1], 0.0), writes=[u[i]])
                self.op("dve", lambda e, i=i: e.memset(u[i][:, L + 1:L + 2], 0.0), writes=[u[i]])
            cnt = 0
            for r in range(3):
                for cb in range(2):
                    for ct in range(4):
                        ch0 = r * 1024 + cb * 512 + ct * 128
                        uu = u[cnt % 2]; w = wc[cnt % 2]; cnt += 1
                        self.dma("sp", uu[:, 1:L + 1], self.ZHT[ch0:ch0 + 128, s0:s0 + L], writes=[uu])
                        self.dma("sp", w[:, 0:3], I["hy_conv_w"][l, :, ch0:ch0 + 128].rearrange("k c -> c k"), writes=[w])
                        self.dma("sp", w[:, 3:4], I["hy_conv_b"][l, ch0:ch0 + 128].rearrange("(c o) -> c o", o=1), writes=[w])
                        self.op("dve", lambda e, uu=uu, w=w: e.tensor_scalar(out=acc[:], in0=uu[:, 1:L + 1], scalar1=w[:, 1:2], scalar2=w[:, 3:4], op0=ALU.mult, op1=ALU.add), reads=[uu, w], writes=[acc])
                        self.op("dve", lambda e, uu=uu, w=w: e.scalar_tensor_tensor(out=acc[:], in0=uu[:, 0:L], scalar=w[:, 0:1], in1=acc[:], op0=ALU.mult, op1=ALU.add), reads=[uu, w, acc], writes=[acc])
                        self.op("dve", lambda e, uu=uu, w=w, ct=ct: e.scalar_tensor_tensor(out=cT[:, ct, :], in0=uu[:, 2:L + 2], scalar=w[:, 2:3], in1=acc[:], op0=ALU.mult, op1=ALU.add), reads=[uu, w, acc], writes=[cT])
                    for tt in range(nt):
                        pb = bbf[tt % 2]; sg = stg[tt % 2]
                        for ct in range(4):
                            self.op("pe", lambda e, pb=pb, ct=ct, tt=tt: e.transpose(out=pb[:, ct * 128:(ct + 1) * 128], in_=cT[:, ct, tt * 128:(tt + 1) * 128], identity=self.ident[:]), reads=[cT, self.ident], writes=[pb])
                        self.op("act", lambda e, pb=pb, sg=sg: e.activation(out=sg[:], in_=pb[:, 0:512], func=AF.Copy), reads=[pb], writes=[sg])
                        self.dma("sp", self.HTOK[r][s0 + tt * 128:s0 + (tt + 1) * 128, cb * 512:(cb + 1) * 512], sg[:], reads=[sg])
        self.barrier()


def hy_conv(self, l):
    I = self.I
    for (s0, L, ci, pi) in self.seqs():
        nt = L // 128
        KF = self.KF[L]; Fc = I["c_f%d" % L]; Gc = I["c_g%d" % L]
        with ExitStack() as st:
            vt = self.sb(st, "vt", [128, nt, 512], BF16)
            Y = self.sb(st, "Y", [128, 2 * nt, 512], BF16)
            ft = [self.sb(st, "ft%d" % i, [128, nt, 128], BF16) for i in range(4)]
            gt = [self.sb(st, "gt%d" % i, [128, 2 * nt, 128], BF16) for i in range(2)]
            kf = [self.sb(st, "kf%d" % i, [128, 512], F32) for i in range(4)]
            tq = [self.sb(st, "tq%d" % i, [128, 512], F32) for i in range(4)]
            bia = [self.sb(st, "bia%d" % i, [128, 512], F32) for i in range(2)]
            xm = [self.sb(st, "xm%d" % i, [128, 512], BF16) for i in range(2)]
            ya = [self.sb(st, "ya%d" % i, [128, 512], BF16) for i in range(2)]
            ystg = [self.sb(st, "ystg%d" % i, [128, 4, 128], BF16) for i in range(2)]
            banks = [self.ps(st, "bk%d" % i, [128, 512]) for i in range(6)]
            bbf = [self.ps(st, "bbf%d" % i, [128, 1024], BF16) for i in range(2)]
            for cb in range(2):
                for k0 in range(0, nt, 8):
                    k1 = min(nt, k0 + 8)
                    self.dma("sp", vt[:, k0:k1, :], self.HTOK[0][s0 + k0 * 128:s0 + k1 * 128, cb * 512:(cb + 1) * 512].rearrange("(kc p) c -> p kc c", p=128), writes=[vt])
                for o in range(2):
                    self.dma("sp", bia[o][:], I["hy_bias"][l, o, cb * 512:(cb + 1) * 512].partition_broadcast(128), writes=[bia[o]])
                for o in range(2):
                    for j in range(nt):
                        fr = ft[(j % 2) * 2]; fi = ft[(j % 2) * 2 + 1]
                        kr = kf[(j % 2) * 2]; ki = kf[(j % 2) * 2 + 1]
                        br = banks[(j % 2) * 2]; bi = banks[(j % 2) * 2 + 1]
                        self.dma("sp", fr[:], Fc[j], writes=[fr]); self.dma("sp", fi[:], Fc[nt + j], writes=[fi])
                        self.dma("sp", kr[:], KF[o, j * 128:(j + 1) * 128, cb * 512:(cb + 1) * 512], writes=[kr])
                        self.dma("sp", ki[:], KF[o, (nt + j) * 128:(nt + j + 1) * 128, cb * 512:(cb + 1) * 512], writes=[ki])
                        self.mm(br[:, :], [(fr[:, kc, :], vt[:, kc, :]) for kc in range(nt)], [fr, vt], br)
                        self.mm(bi[:, :], [(fi[:, kc, :], vt[:, kc, :]) for kc in range(nt)], [fi, vt], bi)
                        self.op("dve", lambda e, br=br, kr=kr: e.tensor_tensor(out=tq[0][:], in0=kr[:], in1=br[:], op=ALU.mult), reads=[kr, br], writes=[tq[0]])
                        self.op("dve", lambda e, bi=bi, ki=ki: e.tensor_tensor(out=tq[1][:], in0=ki[:], in1=bi[:], op=ALU.mult), reads=[ki, bi], writes=[tq[1]])
                        self.op("dve", lambda e, br=br, ki=ki: e.tensor_tensor(out=tq[2][:], in0=ki[:], in1=br[:], op=ALU.mult), reads=[ki, br], writes=[tq[2]])
                        self.op("dve", lambda e, bi=bi, kr=kr: e.tensor_tensor(out=tq[3][:], in0=kr[:], in1=bi[:], op=ALU.mult), reads=[kr, bi], writes=[tq[3]])
                        self.op("dve", lambda e, j=j: e.tensor_tensor(out=Y[:, j, :], in0=tq[0][:], in1=tq[1][:], op=ALU.subtract), reads=[tq[0], tq[1]], writes=[Y])
                        self.op("dve", lambda e, j=j: e.tensor_tensor(out=Y[:, nt + j, :], in0=tq[2][:], in1=tq[3][:], op=ALU.add), reads=[tq[2], tq[3]], writes=[Y])
                        if j == 0:
                            self.op("dve", lambda e: e.tensor_copy(out=Y[0:1, 0, :], in_=tq[0][0:1, :]), reads=[tq[0]], writes=[Y])
                            self.op("dve", lambda e: e.tensor_copy(out=Y[0:1, nt, :], in_=tq[1][0:1, :]), reads=[tq[1]], writes=[Y])
                    for tt in range(nt):
                        g_ = gt[tt % 2]; bk = banks[4 + tt % 2]; x_ = xm[tt % 2]
                        self.dma("sp", g_[:], Gc[tt], writes=[g_])
                        self.dma("sp", x_[:], self.HTOK[1 + o][s0 + tt * 128:s0 + (tt + 1) * 128, cb * 512:(cb + 1) * 512], writes=[x_])
                        self.mm(bk[:, :], [(g_[:, kc, :], Y[:, kc, :]) for kc in range(2 * nt)], [g_, Y], bk)
                        t_ = tq[tt % 2]
                        self.op("dve", lambda e, tt=tt, t_=t_, o=o: e.tensor_tensor(out=t_[:], in0=vt[:, tt, :], in1=bia[o][:], op=ALU.mult), reads=[vt, bia[o]], writes=[t_])
                        self.op("dve", lambda e, t_=t_, bk=bk: e.tensor_tensor(out=t_[:], in0=t_[:], in1=bk[:], op=ALU.add), reads=[t_, bk], writes=[t_])
                        if o == 0:
                            self.op("dve", lambda e, tt=tt, t_=t_, x_=x_: e.tensor_tensor(out=vt[:, tt, :], in0=t_[:], in1=x_[:], op=ALU.mult), reads=[t_, x_], writes=[vt])
                        else:
                            y_ = ya[tt % 2]; pb = bbf[tt % 2]; sg = ystg[tt % 2]
                            self.op("dve", lambda e, t_=t_, x_=x_, y_=y_: e.tensor_tensor(out=y_[:], in0=t_[:], in1=x_[:], op=ALU.mult), reads=[t_, x_], writes=[y_])
                            for ct in range(4):
                                self.op("pe", lambda e, pb=pb, ct=ct, y_=y_: e.transpose(out=pb[:, ct * 128:(ct + 1) * 128], in_=y_[:, ct * 128:(ct + 1) * 128], identity=self.ident[:]), reads=[y_, self.ident], writes=[pb])
                            self.op("act", lambda e, pb=pb, sg=sg: e.activation(out=sg[:], in_=pb[:, 0:512].rearrange("p (c t) -> p c t", c=4), func=AF.Copy), reads=[pb], writes=[sg])
                            self.dma("sp", self.YT[0][cb * 512:(cb + 1) * 512, s0 + tt * 128:s0 + (tt + 1) * 128].rearrange("(c p) t -> p c t", p=128), sg[:], reads=[sg])
        self.barrier()


def gla(self, l):
    I = self.I
    with ExitStack() as st:
        mats = self.sb(st, "mats", [128, 6, 128], F32); masks = self.sb(st, "masks", [128, 2, 128], F32)
        wa = self.sb(st, "wa", [16, 2, 512], F32); babc = self.sb(st, "babc", [128, 2, 512], F32); gn = self.sb(st, "gn", [128, 256], F32)
        S = self.sb(st, "S", [128, 4, 256], F32); Sb = self.sb(st, "Sb", [128, 4, 256], BF16)
        self.dma("sp", mats[:], I["c_glam"].rearrange("m j i -> j m i"), writes=[mats])
        self.dma("sp", masks[:], I["c_gmask"].rearrange("m j i -> j m i"), writes=[masks])
        self.dma("sp", wa[:], I["gla_wa"][l].rearrange("d r n -> r d n"), writes=[wa])
        for d in range(2):
            self.dma("sp", babc[:, d, :], I["gla_ba"][l, d, :].partition_broadcast(128), writes=[babc])
        self.dma("sp", gn[:], I["gla_norm_g"][l, :].partition_broadcast(128), writes=[gn])
        NB = 2
        lrT = [self.sb(st, "lrT%d" % i, [16, 128], F32) for i in range(NB)]
        qT = [self.sb(st, "qT%d" % i, [128, 4, 128], F32) for i in range(NB)]
        kT = [self.sb(st, "kT%d" % i, [128, 4, 128], F32) for i in range(NB)]
        kt = [self.sb(st, "kt%d" % i, [128, 512], F32) for i in range(NB)]
        vt = [self.sb(st, "vt%d" % i, [128, 1024], BF16) for i in range(NB)]
        of = [self.sb(st, "of%d" % i, [128, 1024], F32) for i in range(NB)]
        gr = [self.sb(st, "gr%d" % i, [128, 1024], F32) for i in range(NB)]
        lx = self.sb(st, "lx", [128, 512], F32); ll = self.sb(st, "ll", [128, 512], F32)
        ex = [self.sb(st, "ex%d" % i, [128, 512], F32) for i in range(2)]
        Q1 = [self.sb(st, "Q1%d" % i, [128, 128], BF16) for i in range(2)]; K1 = [self.sb(st, "K1%d" % i, [128, 128], BF16) for i in range(2)]
        Q2 = [self.sb(st, "Q2%d" % i, [128, 128], BF16) for i in range(2)]; K2 = [self.sb(st, "K2%d" % i, [128, 128], BF16) for i in range(2)]
        am = [self.sb(st, "am%d" % i, [128, 128], BF16) for i in range(2)]
        ot = self.sb(st, "ot", [128, 1024], F32); sq = self.sb(st, "sq", [128, 256], F32)
        ss = self.sb(st, "ss", [128, 4], F32); rs = self.sb(st, "rs", [128, 4], F32)
        sgr = self.sb(st, "sgr", [128, 1024], F32); yb = self.sb(st, "yb", [128, 1024], BF16)
        ystg = [self.sb(st, "ystg%d" % i, [128, 8, 128], BF16) for i in range(2)]
        bL = self.ps(st, "bL", [128, 512]); bE = [self.ps(st, "bE%d" % i, [128, 512]) for i in range(2)]
        bA = self.ps(st, "bA", [128, 512]); bO = [self.ps(st, "bO%d" % i, [128, 512]) for i in range(2)]
        bS = self.ps(st, "bS", [128, 512]); bT = self.ps(st, "bT", [128, 1024], BF16)
        qscale = 128.0 ** -0.5
        it = 0
        for (s0, L, ci, pi) in self.seqs():
            nt = L // 128
            for d in range(2):
                if pi < 0:
                    for h in range(4):
                        self.dma("sp", S[:, h, :], I["sg"][l, d, h], writes=[S])
                else:
                    self.op("dve", lambda e: e.memset(S[:], 0.0), writes=[S])
                self.op("act", lambda e: e.activation(out=Sb[:], in_=S[:], func=AF.Copy), reads=[S], writes=[Sb])
                order = list(range(nt)) if d == 0 else list(range(nt - 1, -1, -1))
                for c in order:
                    tok = s0 + c * 128
                    b_ = it % NB; it += 1
                    self.dma("sp", lrT[b_][:], self.GLRT[d, :, tok:tok + 128], writes=[lrT[b_]])
                    self.dma("sp", qT[b_][:], self.GQT[:, tok:tok + 128].rearrange("(h k) t -> k h t", k=128), writes=[qT[b_]])
                    self.dma("sp", kT[b_][:], self.GKT[:, tok:tok + 128].rearrange("(h k) t -> k h t", k=128), writes=[kT[b_]])
                    self.dma("sp", kt[b_][:], self.GK[tok:tok + 128, :], writes=[kt[b_]])
                    self.dma("sp", vt[b_][:], self.GV[tok:tok + 128, :], writes=[vt[b_]])
                    if d == 1:
                        self.dma("sp", of[b_][:], self.OF[tok:tok + 128, :], writes=[of[b_]])
                        self.dma("sp", gr[b_][:], self.GR[tok:tok + 128, :], writes=[gr[b_]])
                    self.mm(bL[:, :], [(lrT[b_][:, :], wa[:, d, :])], [lrT[b_], wa], bL)
                    self.op("dve", lambda e, d=d: e.tensor_tensor(out=lx[:], in0=bL[:], in1=babc[:, d, :], op=ALU.add), reads=[bL, babc], writes=[lx])
                    self.op("dve", lambda e: e.tensor_scalar(out=lx[:], in0=lx[:], scalar1=-80.0, scalar2=None, op0=ALU.max), reads=[lx], writes=[lx])
                    self.op("act", lambda e: e.activation(out=lx[:], in_=lx[:], func=AF.Exp, scale=-1.0), reads=[lx], writes=[lx])
                    self.op("act", lambda e: e.activation(out=ll[:], in_=lx[:], func=AF.Ln, bias=self.ones[:, 0:1], scale=1.0), reads=[lx, self.ones], writes=[ll])
                    for h in range(4):
                        e_ = bE[h % 2]; x_ = ex[h % 2]; q1 = Q1[h % 2]; k1 = K1[h % 2]; q2 = Q2[h % 2]; k2 = K2[h % 2]; a_ = am[h % 2]
                        hs = slice(h * 128, (h + 1) * 128)
                        self.mm(e_[:, 0:128], [(ll[:, hs], mats[:, 3 * d + 0, :])], [ll, mats], e_)
                        self.mm(e_[:, 128:256], [(ll[:, hs], mats[:, 3 * d + 1, :])], [ll, mats], e_)
                        self.mm(e_[:, 256:384], [(mats[:, 3 * d + 2, :], ll[:, hs])], [ll, mats], e_)
                        self.op("act", lambda e, e_=e_, x_=x_: e.activation(out=x_[:, 0:384], in_=e_[:, 0:384], func=AF.Exp), reads=[e_], writes=[x_])
                        self.op("act", lambda e, e_=e_, x_=x_: e.activation(out=x_[:, 384:512], in_=e_[:, 0:128], func=AF.Exp, scale=-1.0), reads=[e_], writes=[x_])
                        self.op("dve", lambda e, x_=x_, q1=q1, h=h, b_=b_: e.scalar_tensor_tensor(out=q1[:], in0=qT[b_][:, h, :], scalar=qscale, in1=x_[:, 0:128], op0=ALU.mult, op1=ALU.mult), reads=[qT[b_], x_], writes=[q1])
                        self.op("dve", lambda e, x_=x_, k1=k1, h=h, b_=b_: e.tensor_tensor(out=k1[:], in0=kT[b_][:, h, :], in1=x_[:, 384:512], op=ALU.mult), reads=[kT[b_], x_], writes=[k1])
                        self.op("dve", lambda e, x_=x_, q2=q2, h=h, b_=b_: e.scalar_tensor_tensor(out=q2[:], in0=qT[b_][:, h, :], scalar=qscale, in1=x_[:, 128:256], op0=ALU.mult, op1=ALU.mult), reads=[qT[b_], x_], writes=[q2])
                        self.op("dve", lambda e, x_=x_, k2=k2, hs=hs, b_=b_: e.tensor_tensor(out=k2[:], in0=kt[b_][:, hs], in1=x_[:, 256:384], op=ALU.mult), reads=[kt[b_], x_], writes=[k2])
                        self.mm(bA[:, 0:128], [(k1[:, :], q1[:, :])], [k1, q1], bA)
                        self.op("dve", lambda e, a_=a_, d=d: e.tensor_tensor(out=a_[:], in0=bA[:, 0:128], in1=masks[:, d, :], op=ALU.mult), reads=[bA, masks], writes=[a_])
                        ob = bO[h // 2]
                        vs = slice(h * 256, (h + 1) * 256)
                        self.mm(ob[:, (h % 2) * 256:(h % 2) * 256 + 256], [(a_[:, :], vt[b_][:, vs]), (q2[:, :], Sb[:, h, :])], [a_, vt[b_], q2, Sb], ob)
                        self.mm(bS[:, 0:256], [(k2[:, :], vt[b_][:, vs])], [k2, vt[b_]], bS)
                        dc = x_[:, 255:256] if d == 0 else x_[:, 128:129]
                        self.op("dve", lambda e, h=h, dc=dc: e.scalar_tensor_tensor(out=S[:, h, :], in0=S[:, h, :], scalar=dc, in1=bS[:, 0:256], op0=ALU.mult, op1=ALU.add), reads=[S, x_, bS], writes=[S])
                        self.op("act", lambda e, h=h: e.activation(out=Sb[:, h, :], in_=S[:, h, :], func=AF.Copy), reads=[S], writes=[Sb])
                    if d == 0:
                        for hh in range(2):
                            self.op("act", lambda e, hh=hh: e.activation(out=ot[:, hh * 512:(hh + 1) * 512], in_=bO[hh][:], func=AF.Copy), reads=[bO[hh]], writes=[ot])
                        self.dma("sp", self.OF[tok:tok + 128, :], ot[:], reads=[ot])
                    else:
                        for hh in range(2):
                            self.op("dve", lambda e, hh=hh, b_=b_: e.tensor_tensor(out=ot[:, hh * 512:(hh + 1) * 512], in0=of[b_][:, hh * 512:(hh + 1) * 512], in1=bO[hh][:], op=ALU.add), reads=[of[b_], bO[hh]], writes=[ot])
                        for h in range(4):
                            self.op("act", lambda e, h=h: e.activation(out=sq[:], in_=ot[:, h * 256:(h + 1) * 256], func=AF.Square), reads=[ot], writes=[sq])
                            self.op("dve", lambda e, h=h: e.reduce_sum(out=ss[:, h:h + 1], in_=sq[:], axis=mybir.AxisListType.X), reads=[sq], writes=[ss])
                        self.op("act", lambda e: e.activation(out=rs[:], in_=ss[:], func=AF.Sqrt, bias=self.eps6[:, 0:1], scale=1.0 / 256.0), reads=[ss, self.eps6], writes=[rs])
                        self.op("dve", lambda e: e.reciprocal(out=rs[:], in_=rs[:]), reads=[rs], writes=[rs])
                        self.op("act", lambda e, b_=b_: e.activation(out=sgr[:], in_=gr[b_][:], func=AF.Silu), reads=[gr[b_]], writes=[sgr])
                        for h in range(4):
                            self.op("dve", lambda e, h=h: e.scalar_tensor_tensor(out=ot[:, h * 256:(h + 1) * 256], in0=ot[:, h * 256:(h + 1) * 256], scalar=rs[:, h:h + 1], in1=gn[:], op0=ALU.mult, op1=ALU.mult), reads=[ot, rs, gn], writes=[ot])
                        self.op("dve", lambda e: e.tensor_tensor(out=yb[:], in0=ot[:], in1=sgr[:], op=ALU.mult), reads=[ot, sgr], writes=[yb])
                        sg_ = ystg[c % 2]
                        for cc in range(8):
                            self.op("pe", lambda e, cc=cc: e.transpose(out=bT[:, cc * 128:(cc + 1) * 128], in_=yb[:, cc * 128:(cc + 1) * 128], identity=self.ident[:]), reads=[yb, self.ident], writes=[bT])
                        self.op("act", lambda e, sg_=sg_: e.activation(out=sg_[:], in_=bT[:, :].rearrange("p (c t) -> p c t", c=8), func=AF.Copy), reads=[bT], writes=[sg_])
                        self.dma("sp", self.YT[1][:, tok:tok + 128].rearrange("(c p) t -> p c t", p=128), sg_[:], reads=[sg_])
                if pi >= 0:
                    for h in range(4):
                        self.dma("sp", self.O["nsg"][pi, l, d, h], S[:, h, :], reads=[S])
                self.barrier()
    self.barrier()


def diff(self, l):
    I = self.I
    lam_init = 0.8 - 0.6 * math.exp(-0.3 * l)
    with ExitStack() as st:
        dl = self.sb(st, "dl", [128, 256], F32); pr = self.sb(st, "pr", [128, 128], F32); sm = self.sb(st, "sm", [128, 4], F32)
        lam = self.sb(st, "lam", [128, 2], F32); gnc = self.sb(st, "gnc", [128, 128], F32)
        self.dma("sp", dl[:], I["diff_lam"][l].rearrange("a b -> (a b)").partition_broadcast(128), writes=[dl])
        self.op("dve", lambda e: e.tensor_tensor(out=pr[:, 0:64], in0=dl[:, 0:64], in1=dl[:, 64:128], op=ALU.mult), reads=[dl], writes=[pr])
        self.op("dve", lambda e: e.tensor_tensor(out=pr[:, 64:128], in0=dl[:, 128:192], in1=dl[:, 192:256], op=ALU.mult), reads=[dl], writes=[pr])
        self.op("dve", lambda e: e.reduce_sum(out=sm[:, 0:1], in_=pr[:, 0:64], axis=mybir.AxisListType.X), reads=[pr], writes=[sm])
        self.op("dve", lambda e: e.reduce_sum(out=sm[:, 1:2], in_=pr[:, 64:128], axis=mybir.AxisListType.X), reads=[pr], writes=[sm])
        self.op("act", lambda e: e.activation(out=sm[:, 2:4], in_=sm[:, 0:2], func=AF.Exp), reads=[sm], writes=[sm])
        self.op("dve", lambda e: e.tensor_tensor(out=lam[:, 0:1], in0=sm[:, 2:3], in1=sm[:, 3:4], op=ALU.subtract), reads=[sm], writes=[lam])
        self.op("dve", lambda e: e.tensor_scalar(out=lam[:, 1:2], in0=lam[:, 0:1], scalar1=lam_init, scalar2=-1.0, op0=ALU.add, op1=ALU.mult), reads=[lam], writes=[lam])
        self.dma("sp", gnc[:], I["diff_norm_g"][l, :].partition_broadcast(128), writes=[gnc])
        self.op("dve", lambda e: e.tensor_scalar(out=gnc[:], in0=gnc[:], scalar1=(1.0 - lam_init), scalar2=None, op0=ALU.mult), reads=[gnc], writes=[gnc])
        with ExitStack() as s2:
            xin = [self.sb(s2, "xin%d" % i, [128, 1024], F32) for i in range(2)]
            xsw = self.sb(s2, "xsw", [128, 1024], F32); t1 = self.sb(s2, "t1", [128, 1024], F32)
            cs = [self.sb(s2, "cs%d" % i, [128, 1024], F32) for i in range(2)]
            xb = [self.sb(s2, "xb%d" % i, [128, 1024], BF16) for i in range(2)]
            va = [self.sb(s2, "va%d" % i, [128, 8, 129], BF16) for i in range(2)]
            stg = [self.sb(s2, "dstg%d" % i, [128, 8, 128], BF16) for i in range(2)]
            bT = [self.ps(s2, "bT%d" % i, [128, 1024], BF16) for i in range(2)]
            for i in range(2):
                self.op("dve", lambda e, i=i: e.memset(va[i][:, :, 128:129], 1.0), writes=[va[i]])
            cnt = 0
            work = []
            for (s0, L, ci, pi) in self.seqs():
                if pi < 0:
                    for c in range(2):
                        work.append(("k", I["ck"][l, c * 128:(c + 1) * 128, :], None, c * 128))
                        work.append(("v", I["cv"][l, c * 128:(c + 1) * 128, :], None, c * 128))
                for c in range(L // 128):
                    tok = s0 + c * 128
                    pos = tok if pi < 0 else None
                    work.append(("q", self.DQ[tok:tok + 128, :], pos, tok))
                    work.append(("k", self.DK[tok:tok + 128, :], pos, 256 + tok))
                    work.append(("v", self.DV[tok:tok + 128, :], None, 256 + tok))
            lastpos = None
            for (kind, src, pos, dst) in work:
                x = xin[cnt % 2]; b = xb[cnt % 2]; v_ = va[cnt % 2]; sg = stg[cnt % 2]; pb = bT[cnt % 2]; cnt += 1
                self.dma("sp", x[:], src, writes=[x])
                if kind == "v":
                    self.op("act", lambda e, x=x, v_=v_: e.activation(out=v_[:, :, 0:128], in_=x[:, :].rearrange("p (h e) -> p h e", h=8), func=AF.Copy), reads=[x], writes=[v_])
                    self.dma("sp", self.VA[dst:dst + 128, :, :], v_[:], reads=[v_])
                    continue
                if pos is not None:
                    if pos != lastpos:
                        self.dma("sp", cs[0][:], I["c_cosf"][pos:pos + 128, :], writes=[cs[0]])
                        self.dma("sp", cs[1][:], I["c_sinf"][pos:pos + 128, :], writes=[cs[1]])
                        lastpos = pos
                    x4 = x[:, :].rearrange("p (g h n) -> p g h n", h=2, n=16)
                    s4 = xsw[:, :].rearrange("p (g h n) -> p g h n", h=2, n=16)
                    self.op("act", lambda e, x4=x4, s4=s4: e.activation(out=s4[:, :, 0, :], in_=x4[:, :, 1, :], func=AF.Copy), reads=[x], writes=[xsw])
                    self.op("act", lambda e, x4=x4, s4=s4: e.activation(out=s4[:, :, 1, :], in_=x4[:, :, 0, :], func=AF.Copy), reads=[x], writes=[xsw])
                    self.op("dve", lambda e, x=x: e.tensor_tensor(out=t1[:], in0=x[:], in1=cs[0][:], op=ALU.mult), reads=[x, cs[0]], writes=[t1])
                    self.op("dve", lambda e: e.tensor_tensor(out=xsw[:], in0=xsw[:], in1=cs[1][:], op=ALU.mult), reads=[xsw, cs[1]], writes=[xsw])
                    self.op("dve", lambda e, b=b: e.tensor_tensor(out=b[:], in0=t1[:], in1=xsw[:], op=ALU.add), reads=[t1, xsw], writes=[b])
                else:
                    self.op("act", lambda e, x=x, b=b: e.activation(out=b[:], in_=x[:], func=AF.Copy), reads=[x], writes=[b])
                for h in range(8):
                    self.op("pe", lambda e, h=h, b=b, pb=pb: e.transpose(out=pb[:, h * 128:(h + 1) * 128], in_=b[:, h * 128:(h + 1) * 128], identity=self.ident[:]), reads=[b, self.ident], writes=[pb])
                self.op("act", lambda e, pb=pb, sg=sg: e.activation(out=sg[:], in_=pb[:, :].rearrange("p (h t) -> p h t", h=8), func=AF.Copy), reads=[pb], writes=[sg])
                dstT = self.DQT if kind == "q" else self.DKT
                self.dma("sp", dstT[:, :, dst:dst + 128].rearrange("h p t -> p h t"), sg[:], reads=[sg])
        self.barrier()
        with ExitStack() as s2:
            KT = [self.sb(s2, "KT%d" % i, [128, 4352], BF16) for i in range(2)]
            VAh = [self.sb(s2, "VAh%d" % i, [128, 34, 129], BF16) for i in range(2)]
            QT = [[self.sb(s2, "QT%d_%d" % (i, j), [128, 4096], BF16) for j in range(2)] for i in range(2)]
            for i in range(2):
                for j in range(2):
                    self.op("dve", lambda e, i=i, j=j: e.memset(QT[i][j][:], 0.0), writes=[QT[i][j]])
            PT = [self.sb(s2, "PT%d" % i, [128, 512], BF16) for i in range(3)]
            rr = self.sb(s2, "rr", [128, 4], F32); o32 = self.sb(s2, "o32", [128, 128], F32); sq = self.sb(s2, "sq", [128, 128], F32)
            ycb = [self.sb(s2, "ycb%d" % i, [128, 128], BF16) for i in range(2)]
            ystg = [self.sb(s2, "ycs%d" % i, [128, 512], BF16) for i in range(2)]
            accb = [self.ps(s2, "acc%d" % i, [128, 512]) for i in range(4)]
            sbk = [self.ps(s2, "sbk%d" % i, [128, 512]) for i in range(3)]
            bT = self.ps(s2, "bT", [128, 1024], BF16)
            hi = 0; sc = 0
            for (s0, L, ci, pi) in self.seqs():
                nk = L + (256 if pi < 0 else 0)
                kcol0 = 0 if pi < 0 else 256 + s0
                nkc = nk // 128
                nq = min(512, L); nqs = nq // 128
                for h in range(8):
                    kt_ = KT[hi % 2]; va_ = VAh[hi % 2]; qt_ = QT[hi % 2]; hi += 1
                    self.dma("sp", kt_[:, 0:nk], self.DKT[h, :, kcol0:kcol0 + nk], writes=[kt_])
                    for j in range(2):
                        self.dma("sp", qt_[j][64 * j:64 * j + 64, 0:L], self.DQT[h, 64 * j:64 * j + 64, s0:s0 + L], writes=[qt_[j]])
                    for k0 in range(0, nkc, 8):
                        k1 = min(nkc, k0 + 8)
                        self.dma("sp", va_[:, k0:k1, :], self.VA[kcol0 + k0 * 128:kcol0 + k1 * 128, h, :].rearrange("(kc p) e -> p kc e", p=128), writes=[va_])
                    for qb in range(L // nq):
                        units = [(kc, j) for kc in range(nkc) for j in range(2)]
                        slots = {}

                        def emit_qk(ui, units=units, slots=slots, kt_=kt_, qt_=qt_, qb=qb, nq=nq):
                            nonlocal sc
                            kc, j = units[ui]
                            sb_ = sbk[sc % 3]; pt = PT[sc % 3]; sc += 1
                            slots[ui] = pt
                            self.mm(sb_[:, 0:nq], [(kt_[:, kc * 128:(kc + 1) * 128], qt_[j][:, qb * nq:(qb + 1) * nq])], [kt_, qt_[j]], sb_)
                            self.op("act", lambda e, sb_=sb_, pt=pt: e.activation(out=pt[:, 0:nq], in_=sb_[:, 0:nq], func=AF.Exp, scale=0.125), reads=[sb_], writes=[pt])

                        def emit_av(ui, units=units, slots=slots, va_=va_, nqs=nqs, nkc=nkc):
                            kc, j = units[ui]
                            pt = slots.pop(ui)
                            for qs in range(nqs):
                                a = j * nqs + qs
                                ab = accb[a // 2]; co = (a % 2) * 256
                                first = (kc == 0 and a % 2 == 0)
                                self.op("pe", lambda e, ab=ab, co=co, pt=pt, qs=qs, kc=kc, first=first: e.matmul(
                                    ab[:, co:co + 129], lhsT=pt[:, qs * 128:(qs + 1) * 128], rhs=va_[:, kc, :], start=first, stop=(kc == nkc - 1), skip_group_check=True),
                                    reads=[pt, va_], writes=[ab], sig=(qs == nqs - 1))

                        for ui in range(0, len(units), 2):
                            emit_qk(ui)
                            emit_qk(ui + 1)
                            emit_av(ui)
                            emit_av(ui + 1)
                        sg = ystg[qb % 2]
                        for qs in range(nqs):
                            a0 = qs; a1 = nqs + qs
                            A0 = accb[a0 // 2]; c0_ = (a0 % 2) * 256; A1 = accb[a1 // 2]; c1_ = (a1 % 2) * 256
                            yc_ = ycb[qs % 2]
                            self.op("dve", lambda e, A0=A0, c0_=c0_: e.reciprocal(out=rr[:, 0:1], in_=A0[:, c0_ + 128:c0_ + 129]), reads=[A0], writes=[rr])
                            self.op("dve", lambda e, A1=A1, c1_=c1_: e.reciprocal(out=rr[:, 1:2], in_=A1[:, c1_ + 128:c1_ + 129]), reads=[A1], writes=[rr])
                            self.op("dve", lambda e: e.tensor_tensor(out=rr[:, 1:2], in0=rr[:, 1:2], in1=lam[:, 1:2], op=ALU.mult), reads=[rr, lam], writes=[rr])
                            self.op("dve", lambda e, A0=A0, c0_=c0_: e.tensor_scalar(out=o32[:], in0=A0[:, c0_:c0_ + 128], scalar1=rr[:, 0:1], scalar2=None, op0=ALU.mult), reads=[A0, rr], writes=[o32])
                            self.op("dve", lambda e, A1=A1, c1_=c1_: e.scalar_tensor_tensor(out=o32[:], in0=A1[:, c1_:c1_ + 128], scalar=rr[:, 1:2], in1=o32[:], op0=ALU.mult, op1=ALU.add), reads=[A1, rr, o32], writes=[o32])
                            self.op("act", lambda e: e.activation(out=sq[:], in_=o32[:], func=AF.Square), reads=[o32], writes=[sq])
                            self.op("dve", lambda e: e.reduce_sum(out=rr[:, 2:3], in_=sq[:], axis=mybir.AxisListType.X), reads=[sq], writes=[rr])
                            self.op("act", lambda e: e.activation(out=rr[:, 3:4], in_=rr[:, 2:3], func=AF.Sqrt, bias=self.eps6[:, 0:1], scale=1.0 / 128.0), reads=[rr, self.eps6], writes=[rr])
                            self.op("dve", lambda e: e.reciprocal(out=rr[:, 3:4], in_=rr[:, 3:4]), reads=[rr], writes=[rr])
                            self.op("dve", lambda e, yc_=yc_: e.scalar_tensor_tensor(out=yc_[:], in0=o32[:], scalar=rr[:, 3:4], in1=gnc[:], op0=ALU.mult, op1=ALU.mult), reads=[o32, rr, gnc], writes=[yc_])
                            self.op("pe", lambda e, yc_=yc_, qs=qs: e.transpose(out=bT[:, qs * 128:(qs + 1) * 128], in_=yc_[:], identity=self.ident[:]), reads=[yc_, self.ident], writes=[bT])
                        self.op("act", lambda e, sg=sg: e.activation(out=sg[:, 0:nq], in_=bT[:, 0:nq], func=AF.Copy), reads=[bT], writes=[sg])
                        q0 = s0 + qb * nq
                        self.dma("sp", self.YT[2][h * 128:(h + 1) * 128, q0:q0 + nq], sg[:, 0:nq], reads=[sg])
    self.barrier()


def mixer_out(self, l):
    I = self.I
    WB = [I["w_branch_a"][l], I["w_branch_b"][l], I["w_branch_c"][l]]
    WO = I["w_out"][l]
    with ExitStack() as st:
        yT = [self.sb(st, "yT%d" % i, [128, 8, 512], BF16) for i in range(3)]
        yacc = self.sb(st, "yacc", [128, 4, D], F32)
        wbr = [self.sb(st, "wbr%d" % i, [128, 8, 512], BF16) for i in range(2)]
        self.wstage_alloc(st)
        wo = [self.sb(st, "wo%d" % i, [128, 16, 512], BF16) for i in range(2)]
        gt = [self.sb(st, "gt%d" % i, [128, 512], F32) for i in range(3)]
        tmp = [self.sb(st, "tmp%d" % i, [128, 512], F32) for i in range(2)]
        yb = self.sb(st, "yb", [128, D], BF16)
        yT2 = self.sb(st, "yT2", [128, 16, 512], BF16)
        V = [self.sb(st, "V%d" % i, [128, D], F32) for i in range(3)]
        xt = [self.sb(st, "xt%d" % i, [128, D], F32) for i in range(2)]
        small = (self.sb(st, "stats", [128, 4, 6], F32), self.sb(st, "mv", [128, 2], F32), self.sb(st, "rstd", [128, 1], F32))
        banks = [self.ps(st, "bk%d" % i, [128, 512]) for i in range(6)]
        bbf = [self.ps(st, "bbf%d" % i, [128, 1024], BF16) for i in range(2)]
        cnt = 0
        for g in self.groups():
            ci = self.gci(g)
            tok0 = g * 512
            for br in range(3):
                self.dma("sp", yT[br][:], self.YT[br][:, tok0:tok0 + 512].rearrange("(kc p) t -> p kc t", p=128), writes=[yT[br]])
            wi = 0
            for br in range(3):
                for db in range(4):
                    w = wbr[wi % 2]; wi += 1
                    self.wload(w, lambda k0, k1, w=w: w[:, k0:k1, :], WB[br], db * 512, 512, 8, key=(l, 'wbr', br, db))
                    for tt in range(4):
                        bk = banks[cnt % 4]; g_ = gt[cnt % 3]; tm = tmp[cnt % 2]; cnt += 1
                        tok = tok0 + tt * 128
                        self.dma("sp", g_[:], self.GATES[tok:tok + 128, br * D + db * 512: br * D + (db + 1) * 512], writes=[g_])
                        self.mm(bk[:, :], [(yT[br][:, kc, tt * 128:(tt + 1) * 128], w[:, kc, :]) for kc in range(8)], [yT[br], w], bk)
                        if br == 0:
                            self.op("dve", lambda e, bk=bk, g_=g_, tt=tt, db=db: e.tensor_tensor(out=yacc[:, tt, db * 512:(db + 1) * 512], in0=g_[:], in1=bk[:], op=ALU.mult), reads=[g_, bk], writes=[yacc])
                        else:
                            self.op("dve", lambda e, bk=bk, g_=g_, tm=tm: e.tensor_tensor(out=tm[:], in0=g_[:], in1=bk[:], op=ALU.mult), reads=[g_, bk], writes=[tm])
                            self.op("dve", lambda e, tm=tm, tt=tt, db=db: e.tensor_tensor(out=yacc[:, tt, db * 512:(db + 1) * 512], in0=yacc[:, tt, db * 512:(db + 1) * 512], in1=tm[:], op=ALU.add), reads=[tm, yacc], writes=[yacc])
            for tt in range(4):
                self.op("act", lambda e, tt=tt: e.activation(out=yb[:], in_=yacc[:, tt, :], func=AF.Copy), reads=[yacc], writes=[yb])
                for half in range(2):
                    pb = bbf[half]
                    for i in range(8):
                        kc = half * 8 + i
                        self.op("pe", lambda e, pb=pb, i=i, kc=kc: e.transpose(out=pb[:, i * 128:(i + 1) * 128], in_=yb[:, kc * 128:(kc + 1) * 128], identity=self.ident[:]), reads=[yb, self.ident], writes=[pb])
                    self.op("act", lambda e, pb=pb, half=half, tt=tt: e.activation(out=yT2[:, half * 8:half * 8 + 8, tt * 128:(tt + 1) * 128], in_=pb[:, :].rearrange("p (k t) -> p k t", k=8), func=AF.Copy), reads=[pb], writes=[yT2])
            self.load_vec(V[0], ci, 5)
            self.load_ln(l, 1, V[1], V[2])
            for db in range(4):
                w = wo[db % 2]
                self.wload(w, lambda k0, k1, w=w: w[:, k0:k1, :], WO, db * 512, 512, 16, key=(l, 'wo', db))
                for tt in range(4):
                    bk = banks[4 + tt % 2]
                    self.mm(bk[:, :], [(yT2[:, kc, tt * 128:(tt + 1) * 128], w[:, kc, :]) for kc in range(16)], [yT2, w], bk)
                    self.op("act", lambda e, bk=bk, tt=tt, db=db: e.activation(out=yacc[:, tt, db * 512:(db + 1) * 512], in_=bk[:], func=AF.Copy), reads=[bk], writes=[yacc])
            for tt in range(4):
                tok = tok0 + tt * 128
                self.epilogue(l, 1, tok, lambda db, tt=tt: yacc[:, tt, db * 512:(db + 1) * 512], [yacc] * 4, xt[tt % 2], _V(yacc, tt), V[0], V[1], V[2], False, small)
    self.barrier()


def phase_mixer(self, l):
    self.mixer_alloc()
    sub = self.cfg.get("mix")
    def on(p):
        return sub is None or p in sub
    if on("in"):
        self.mixer_in(l)
    if on("hyf"):
        for L in sorted({s[1] for s in self.seqs()}):
            self.hy_filters(l, L)
    if on("hyp"):
        self.hy_prep(l)
    if on("hyc"):
        self.hy_conv(l)
    if on("gla"):
        self.gla(l)
    if on("diff"):
        self.diff(l)
    if on("out"):
        self.mixer_out(l)


for _f in (mixer_alloc, mixer_in, range_reduce, hy_filters, hy_prep, hy_conv, gla, diff, mixer_out, phase_mixer):
    setattr(MK, _f.__name__, _f)
MK.seqs = _seqs


def _shapes():
    sh = {
        "xs": ((TS, D), F32), "xp": ((TP, D), F32), "cvec": ((2, D), F32),
        "ck": ((DEPTH, 256, 1024), F32), "cv": ((DEPTH, 256, 1024), F32), "sg": ((DEPTH, 2, 4, 128, 256), F32),
        "w_mod": ((DEPTH, D, 9 * D), F32), "b_mod": ((DEPTH, 9 * D), F32), "ln_g": ((DEPTH, 3, D), F32), "ln_b": ((DEPTH, 3, D), F32),
        "ffn_w1": ((DEPTH, 2, D, FF), F32), "ffn_w3": ((DEPTH, 2, D, FF), F32), "ffn_w2": ((DEPTH, 2, FF, D), F32),
        "w_in": ((DEPTH, D, NCOL), F32), "hy_conv_w": ((DEPTH, 3, 3072), F32), "hy_conv_b": ((DEPTH, 3072), F32),
        "hy_w1": ((DEPTH, 17, 64), F32), "hy_b1": ((DEPTH, 64), F32), "hy_freq": ((DEPTH, 2, 64), F32),
        "hy_w2": ((DEPTH, 64, 64), F32), "hy_b2": ((DEPTH, 64), F32), "hy_w3": ((DEPTH, 64, 4096), F32),
        "hy_decay": ((DEPTH, 4096), F32), "hy_bias": ((DEPTH, 2, 1024), F32), "gla_wa": ((DEPTH, 2, 16, 512), F32),
        "gla_ba": ((DEPTH, 2, 512), F32), "gla_norm_g": ((DEPTH, 256), F32), "diff_lam": ((DEPTH, 4, 64), F32),
        "diff_norm_g": ((DEPTH, 128), F32), "w_branch_a": ((DEPTH, 1024, D), F32), "w_branch_b": ((DEPTH, 1024, D), F32),
        "w_branch_c": ((DEPTH, 1024, D), F32), "w_out": ((DEPTH, D, D), F32),
    }
    for k, v in _consts().items():
        sh[k] = (v.shape, BF16 if v.dtype == ml_dtypes.bfloat16 else F32)
    return sh


def build(cfg=None):
    cfg = cfg or {}
    nc = bass.Bass("TRN2", target_bir_lowering=False)
    k = MK(nc, _shapes(), cfg)
    k.out("ys", [TS, D])
    k.out("yp", [TP, D])
    k.out("nck", [2, DEPTH, 256, 1024])
    k.out("ncv", [2, DEPTH, 256, 1024])
    k.out("nsg", [2, DEPTH, 2, 4, 128, 256])
    phases = cfg.get("phases")
    with nc.allow_non_contiguous_dma(reason="small strided parameter loads"):
        with ExitStack() as st:
            k.phase_init(st)
            nl = cfg.get("depth", DEPTH)
            for l in range(nl):
                k.phase_mod(l)
            for l in range(nl):
                k.phase_convert(l)
            for l in range(nl):
                phases = cfg.get("phases%d" % l, cfg.get("phases"))
                k.MODB = k.MODBS[l]

                def on(p, phases=phases):
                    return phases is None or p in phases
                if on("ffn1"):
                    k.phase_ffn(l, 0)
                if on("mixer"):
                    k.phase_mixer(l)
                if on("ffn2"):
                    k.phase_ffn(l, 1)
            k.barrier(full=True)
    k.es.close()
    return nc, k


def _in_maps(inputs):
    c = _consts()
    maps = []
    f = lambda a: np.ascontiguousarray(np.asarray(a, dtype=np.float32))
    shared = {n: f(inputs[n]) for n in WNAMES}
    for b in range(8):
        m = dict(shared)
        m.update(c)
        m["xs"] = f(inputs["x_sample"][b])
        m["xp"] = f(inputs["x_prompt"][2 * b:2 * b + 2]).reshape(TP, D)
        m["cvec"] = f(np.stack([np.asarray(inputs["c"][b]), np.asarray(inputs["c_ctx"])]))
        m["ck"] = f(inputs["cache_k"][b]).reshape(DEPTH, 256, 1024)
        m["cv"] = f(inputs["cache_v"][b]).reshape(DEPTH, 256, 1024)
        m["sg"] = f(inputs["state_gla"][b])
        maps.append(m)
    return maps


def kernel(**inputs):
    nc, k = build()
    res = run_bass_kernel_spmd(nc, _in_maps(inputs), core_ids=list(range(8)))
    r = res.results
    ys = np.stack([r[b]["ys"] for b in range(8)])
    yp = np.concatenate([r[b]["yp"].reshape(2, 256, D) for b in range(8)])
    nck = np.concatenate([r[b]["nck"].reshape(2, DEPTH, 256, 8, 2, 64) for b in range(8)])
    ncv = np.concatenate([r[b]["ncv"].reshape(2, DEPTH, 256, 8, 128) for b in range(8)])
    nsg = np.concatenate([r[b]["nsg"] for b in range(8)])
    return (yp, ys, nck, ncv, nsg)
```

```python
import math
from contextlib import ExitStack
import numpy as np
import ml_dtypes
import concourse.bass as bass
import concourse.mybir as mybir
from concourse.bass_utils import run_bass_kernel_spmd

F32 = mybir.dt.float32
BF16 = mybir.dt.bfloat16
AF = mybir.ActivationFunctionType
ALU = mybir.AluOpType

D = 2048
FF = 5504
DEPTH = 2
NCOL = 15392
TS = 4096
TP = 512
T = TS + TP
ALPHA = (2 * DEPTH) ** 0.25
MAGIC = 12582912.0


class Buf:
    __slots__ = ("w", "r", "name")

    def __init__(self, name=""):
        self.w = {}
        self.r = {}
        self.name = name


class TT:
    __slots__ = ("t", "b")

    def __init__(self, t, b):
        self.t = t
        self.b = b

    def __getitem__(self, k):
        return self.t[k]


class KB:
    NDSEM = 36
    QSEM = {"sp": (0, 24), "pool": (24, 28), "cv": (28, 36)}

    def __init__(self, nc):
        self.nc = nc
        self.es = ExitStack()
        self.eng = {"pe": nc.tensor, "act": nc.scalar, "dve": nc.vector, "pool": nc.gpsimd, "sp": nc.sync}
        self.tick = {e: 0 for e in ("pe", "act", "dve", "pool")}
        self.sem = {e: self.es.enter_context(nc.semaphore("tick_" + e)) for e in self.tick}
        self.dsem = [self.es.enter_context(nc.semaphore("dma%d" % i)) for i in range(self.NDSEM)]
        self.dcnt = [0] * self.NDSEM
        self.qnext = {}
        self.waited = {}
        self.ninst = 0
        self.uid = 0
        self.wsi = 0
        self.wc = {}

    def sb(self, stack, name, shape, dt):
        self.uid += 1
        t = stack.enter_context(self.nc.sbuf_tensor("%s_%d" % (name, self.uid), list(shape), dt))
        return TT(t, Buf(name))

    def ps(self, stack, name, shape, dt=F32):
        self.uid += 1
        t = stack.enter_context(self.nc.psum_tensor("%s_%d" % (name, self.uid), list(shape), dt))
        return TT(t, Buf(name))

    def _semh(self, key):
        return self.sem[key] if isinstance(key, str) else self.dsem[key[1]]

    def _wait(self, engine, deps):
        e = self.eng[engine]
        for key, val in deps.items():
            wk = (engine, key)
            if self.waited.get(wk, 0) >= val:
                continue
            self.waited[wk] = val
            e.wait_ge(self._semh(key), val)
            self.ninst += 1

    @staticmethod
    def _merge(d, s):
        for k, v in s.items():
            if d.get(k, 0) < v:
                d[k] = v

    def _deps(self, reads, writes):
        deps = {}
        for b in reads:
            self._merge(deps, b.w)
        for b in writes:
            self._merge(deps, b.w)
            self._merge(deps, b.r)
        return deps

    @staticmethod
    def _bufs(xs):
        return [getattr(x, "b", x) for x in xs if x is not None]

    def op(self, engine, fn, reads=(), writes=(), sig=True):
        reads = self._bufs(reads)
        writes = self._bufs(writes)
        deps = self._deps(reads, writes)
        if engine == "pe":
            deps.pop("pe", None)
        self._wait(engine, deps)
        ins = fn(self.eng[engine])
        self.ninst += 1
        if sig:
            self.tick[engine] += 1
            ins.then_inc(self.sem[engine], 1)
            tok = {engine: self.tick[engine]}
        else:
            tok = {engine: self.tick[engine] + 1}
        for b in reads:
            self._merge(b.r, tok)
        for b in writes:
            self._merge(b.w, tok)
            b.r = {}
        return ins

    def dma(self, queue, out, in_, reads=(), writes=()):
        reads = self._bufs(reads)
        writes = self._bufs(writes)
        deps = self._deps(reads, writes)
        lo, hi = self.QSEM[queue]
        s = self.qnext.get(queue, lo)
        self.qnext[queue] = lo + (s + 1 - lo) % (hi - lo)
        if self.dcnt[s]:
            self._merge(deps, {("d", s): 16 * self.dcnt[s]})
        engine = "pool" if queue == "cv" else queue
        self._wait(engine, deps)
        ins = self.eng[engine].dma_start(out=out, in_=in_)
        self.ninst += 1
        self.dcnt[s] += 1
        ins.then_inc(self.dsem[s], 16)
        tok = {("d", s): 16 * self.dcnt[s]}
        for b in reads:
            self._merge(b.r, tok)
        for b in writes:
            self._merge(b.w, tok)
            b.r = {}
        return tok

    def barrier(self, full=False):
        deps = {e: v for e, v in self.tick.items() if v}
        cvlo, cvhi = self.QSEM["cv"]
        for i, c in enumerate(self.dcnt):
            if c and (full or not (cvlo <= i < cvhi)):
                deps[("d", i)] = 16 * c
        for e in ("sp", "pe", "act", "dve", "pool"):
            self._wait(e, dict(deps))


_CONST_CACHE = {}


def _bf(a):
    return np.ascontiguousarray(a.astype(ml_dtypes.bfloat16))


def _dft_consts(L):
    N = 2 * L
    t = np.arange(L, dtype=np.int64)
    f = np.arange(L, dtype=np.int64)
    ang = (np.outer(t, f) % N).astype(np.float64) * (2.0 * math.pi / N)
    C = np.cos(ang)
    S = -np.sin(ang)
    sgn = np.where(t % 2 == 0, 1.0, -1.0)
    Ffwd = np.concatenate([C, S], axis=1)
    Ffwd[:, L] = sgn
    Gre = (2.0 / N) * C.T
    Gre[0, :] = 1.0 / N
    Gim = (2.0 / N) * S.T
    Gim[0, :] = sgn / N
    Ginv = np.concatenate([Gre, Gim], axis=0)
    nt = L // 128
    Ft = Ffwd.reshape(nt, 128, 2 * nt, 128).transpose(2, 1, 0, 3)
    Gt = Ginv.reshape(2 * nt, 128, nt, 128).transpose(2, 1, 0, 3)
    nyq = sgn.reshape(nt, 128).T
    return _bf(Ft), _bf(Gt), _bf(nyq)


def _feats(L):
    t = np.linspace(0.0, 1.0, L, dtype=np.float32)
    w = (np.float32(2.0 * math.pi / L) * np.arange(L, dtype=np.float32))
    f = np.linspace(1e-4, 7, 8, dtype=np.float32)
    feats = np.concatenate([t[:, None], np.cos(w[:, None] * f), -np.sin(w[:, None] * f)], -1).astype(np.float32)
    featsT = np.ascontiguousarray(feats.T)
    ntl = np.ascontiguousarray((-t).reshape(L // 128, 128).T)
    return featsT, ntl.astype(np.float32)


def _rope_tables():
    L = TS
    rows = L // 64
    r = np.repeat(np.arange(rows, dtype=np.float32), 64)
    col = np.tile(np.arange(64, dtype=np.float32), rows)
    nf = 16
    inv = (10000.0 ** (-np.arange(nf, dtype=np.float32) / nf)).astype(np.float32)
    ang = np.stack([r[:, None] * inv, col[:, None] * inv], axis=1)
    cos = np.cos(ang).astype(np.float32)
    sin = np.sin(ang).astype(np.float32)
    c64 = np.stack([cos, cos], axis=2).reshape(L, 64)
    s64 = np.stack([-sin, sin], axis=2).reshape(L, 64)
    return np.ascontiguousarray(np.tile(c64, (1, 16))), np.ascontiguousarray(np.tile(s64, (1, 16)))


def _gla_mats():
    j = np.arange(128)[:, None]
    i = np.arange(128)[None, :]
    s = -1.0 / 16.0
    m = np.zeros((6, 128, 128), np.float32)
    m[0] = s * ((j <= i).astype(np.float32) - (j <= 63).astype(np.float32))
    m[1] = s * (j <= i)
    m[2] = s * (j > i)
    m[3] = s * ((j >= i).astype(np.float32) - (j >= 64).astype(np.float32))
    m[4] = s * (j >= i)
    m[5] = s * (j < i)
    masks = np.zeros((2, 128, 128), np.float32)
    masks[0] = (j <= i)
    masks[1] = (j >= i)
    return m, masks


def _consts():
    if _CONST_CACHE:
        return _CONST_CACHE
    c = _CONST_CACHE
    c["c_f4096"], c["c_g4096"], c["c_nyq4096"] = _dft_consts(4096)
    c["c_f256"], c["c_g256"], c["c_nyq256"] = _dft_consts(256)
    c["c_featsT4096"], c["c_ntl4096"] = _feats(4096)
    c["c_featsT256"], c["c_ntl256"] = _feats(256)
    c["c_cosf"], c["c_sinf"] = _rope_tables()
    c["c_glam"], c["c_gmask"] = _gla_mats()
    c["c_ident"] = _bf(np.eye(128, dtype=np.float32))
    return c


WNAMES = ["w_mod", "b_mod", "ln_g", "ln_b", "ffn_w1", "ffn_w3", "ffn_w2", "w_in", "hy_conv_w", "hy_conv_b",
          "hy_w1", "hy_b1", "hy_freq", "hy_w2", "hy_b2", "hy_w3", "hy_decay", "hy_bias", "gla_wa", "gla_ba",
          "gla_norm_g", "diff_lam", "diff_norm_g", "w_branch_a", "w_branch_b", "w_branch_c", "w_out"]


class MK(KB):
    def __init__(self, nc, shapes, cfg):
        super().__init__(nc)
        self.cfg = cfg
        self.I = {}
        for name, (shape, dt) in shapes.items():
            self.I[name] = nc.dram_tensor(name, list(shape), dt, kind="ExternalInput").ap()
        self.O = {}
        self.dbg = cfg.get("dbg", ())

    def out(self, name, shape, dt=F32):
        self.O[name] = self.nc.dram_tensor(name, list(shape), dt, kind="ExternalOutput").ap()
        return self.O[name]

    def scratch(self, name, shape, dt=F32):
        if name in self.dbg:
            return self.out(name, shape, dt)
        return self.nc.dram_tensor(name, list(shape), dt).ap()

    def mm(self, psum_ap, pairs, reads, pw):
        n = len(pairs)
        for i, (l, r) in enumerate(pairs):
            self.op("pe", lambda e, l=l, r=r, i=i: e.matmul(psum_ap, lhsT=l, rhs=r, start=(i == 0), stop=(i == n - 1)),
                    reads=reads, writes=[pw], sig=(i == n - 1))

    def wstage_alloc(self, st, n=3):
        pass

    def wload_cast(self, dst, dst_ap_fn, src2d, c0, ncols, nk, rows0=0):
        for k0 in range(0, nk, 8):
            k1 = min(nk, k0 + 8)
            src = src2d[rows0 + k0 * 128: rows0 + k1 * 128, c0:c0 + ncols].rearrange("(kc p) n -> p kc n", p=128)
            self.dma("pool", dst_ap_fn(k0, k1), src, writes=[dst])

    def wconv(self, key, src2d, c0, ncols, nk, rows0=0):
        self.uid += 1
        t = self.nc.dram_tensor("wc_%d" % self.uid, [128, nk, ncols], BF16).ap()
        b = Buf("wc")
        for k0 in range(0, nk, 8):
            k1 = min(nk, k0 + 8)
            src = src2d[rows0 + k0 * 128: rows0 + k1 * 128, c0:c0 + ncols].rearrange("(kc p) n -> p kc n", p=128)
            self.dma("cv", t[:, k0:k1, :], src, writes=[b])
        self.wc[key] = (t, b)

    def wload(self, dst, dst_ap_fn, src2d, c0, ncols, nk, rows0=0, key=None):
        t, b = self.wc[key]
        self.dma("sp", dst_ap_fn(0, nk), t[:, :, :], reads=[b], writes=[dst])

    def groups(self):
        return self.cfg.get("groups", list(range(9)))

    def gci(self, g):
        return 0 if g < 8 else 1

    def phase_init(self, st):
        nc = self.nc
        self.ident = self.sb(st, "ident", [128, 128], BF16)
        self.dma("sp", self.ident[:], self.I["c_ident"][:, :], writes=[self.ident])
        self.ones = self.sb(st, "ones", [128, 128], F32)
        self.op("dve", lambda e: e.memset(self.ones[:], 1.0), writes=[self.ones])
        self.eps5 = self.sb(st, "eps5", [128, 1], F32)
        self.op("dve", lambda e: e.memset(self.eps5[:], 1e-5), writes=[self.eps5])
        self.eps6 = self.sb(st, "eps6", [128, 1], F32)
        self.op("dve", lambda e: e.memset(self.eps6[:], 1e-6), writes=[self.eps6])
        self.X = self.scratch("X", [T, D])
        self.MODBS = [self.scratch("MODB" if i == 0 else "MODB%d" % i, [2, 128, 9 * D]) for i in range(DEPTH)]
        self.MODB = self.MODBS[0]
        for i in range(0, TS, 1024):
            self.dma("sp", self.X[i:i + 1024, :], self.I["xs"][i:i + 1024, :])
        self.dma("sp", self.X[TS:T, :], self.I["xp"][:, :])
        self.barrier()


    def phase_convert(self, l):
        I = self.I
        fblocks = [(i * 256, 256) for i in range(21)] + [(21 * 256, 128)]

        def ffn(fi):
            for (c0, nc_) in fblocks:
                self.wconv((l, 'w1', fi, c0), I["ffn_w1"][l, fi], c0, nc_, 16)
                self.wconv((l, 'w3', fi, c0), I["ffn_w3"][l, fi], c0, nc_, 16)
            for db in range(4):
                for ch in range(11):
                    f0 = ch * 4
                    nf = min(4, 43 - f0)
                    self.wconv((l, 'w2', fi, db, f0), I["ffn_w2"][l, fi], db * 512, 512, nf, rows0=f0 * 128)
        ffn(0)
        W = I["w_in"][l]
        for c0 in list(range(0, 4096, 512)):
            self.wconv((l, 'win', c0), W, c0, 512, 16)
        self.wconv((l, 'win', 6144), W, 6144, 32, 16)
        for c0 in [4096 + i * 512 for i in range(4)] + [6176 + i * 512 for i in range(6)] + [9248 + i * 512 for i in range(12)]:
            self.wconv((l, 'win', c0), W, c0, 512, 16)
        for br, n in enumerate(("w_branch_a", "w_branch_b", "w_branch_c")):
            for db in range(4):
                self.wconv((l, 'wbr', br, db), I[n][l], db * 512, 512, 8)
        for db in range(4):
            self.wconv((l, 'wo', db), I["w_out"][l], db * 512, 512, 16)
        ffn(1)

    def phase_mod(self, l):
        I = self.I
        self.MODB = self.MODBS[l]
        with ExitStack() as st:
            cv = self.sb(st, "cv", [128, 2, 16], F32)
            sc = self.sb(st, "sc", [128, 2, 16], F32)
            lhs = self.sb(st, "modlhs", [128, 2, 16, 128], BF16)
            self.dma("sp", cv[:], I["cvec"].rearrange("c (kc p) -> p c kc", p=128), writes=[cv])
            self.op("act", lambda e: e.activation(out=sc[:], in_=cv[:], func=AF.Silu), reads=[cv], writes=[sc])
            for ci in range(2):
                for kc in range(16):
                    self.op("dve", lambda e, ci=ci, kc=kc: e.tensor_scalar(
                        out=lhs[:, ci, kc, :], in0=self.ones[:, :], scalar1=sc[:, ci, kc:kc + 1], scalar2=None, op0=ALU.mult),
                        reads=[self.ones, sc], writes=[lhs])
            self.wstage_alloc(st)
            wb = [self.sb(st, "modw%d" % i, [128, 16, 512], BF16) for i in range(3)]
            bb = [self.sb(st, "modb%d" % i, [128, 512], F32) for i in range(2)]
            res = [self.sb(st, "modr%d" % i, [128, 512], F32) for i in range(4)]
            banks = [self.ps(st, "modp%d" % i, [128, 512]) for i in range(4)]
            wsrc = I["w_mod"][l]
            for nb in range(36):
                w = wb[nb % 3]
                b = bb[nb % 2]
                self.wload_cast(w, lambda k0, k1, w=w: w[:, k0:k1, :], wsrc, nb * 512, 512, 16)
                self.dma("sp", b[:], I["b_mod"][l, nb * 512:(nb + 1) * 512].partition_broadcast(128), writes=[b])
                for ci in range(2):
                    bk = banks[(nb * 2 + ci) % 4]
                    r = res[(nb * 2 + ci) % 4]
                    self.mm(bk[:, :], [(lhs[:, ci, kc, :], w[:, kc, :]) for kc in range(16)], [lhs, w], bk)
                    self.op("dve", lambda e, bk=bk, r=r, b=b: e.tensor_tensor(out=r[:], in0=bk[:], in1=b[:], op=ALU.add),
                            reads=[bk, b], writes=[r])
                    self.dma("sp", self.MODB[ci, :, nb * 512:(nb + 1) * 512], r[:], reads=[r])
        self.barrier()

    def load_vec(self, dst, ci, idx):
        self.dma("sp", dst[:], self.MODB[ci, :, idx * D:(idx + 1) * D], writes=[dst])

    def prologue(self, g, xt, hb, hT, V0, V1, banks_bf):
        for tt in range(4):
            tok = g * 512 + tt * 128
            x = xt[tt % 2]
            self.dma("sp", x[:], self.X[tok:tok + 128, :], writes=[x])
            self.op("dve", lambda e, x=x: e.tensor_tensor(out=x[:], in0=x[:], in1=V0[:], op=ALU.mult), reads=[x, V0], writes=[x])
            self.op("dve", lambda e, x=x: e.tensor_tensor(out=hb[:], in0=x[:], in1=V1[:], op=ALU.add), reads=[x, V1], writes=[hb])
            for half in range(2):
                pb = banks_bf[half]
                for i in range(8):
                    kc = half * 8 + i
                    self.op("pe", lambda e, pb=pb, i=i, kc=kc: e.transpose(
                        out=pb[:, i * 128:(i + 1) * 128], in_=hb[:, kc * 128:(kc + 1) * 128], identity=self.ident[:]),
                        reads=[hb, self.ident], writes=[pb])
                self.op("act", lambda e, pb=pb, half=half, tt=tt: e.activation(
                    out=hT[:, half * 8:half * 8 + 8, tt * 128:(tt + 1) * 128],
                    in_=pb[:, :].rearrange("p (k t) -> p k t", k=8), func=AF.Copy),
                    reads=[pb], writes=[hT])

    def epilogue(self, l, i, tok, osrc, osrc_bufs, xt, tbuf, V0, V1, V2, half_gate, small):
        stats, mv, rstd = small
        self.dma("sp", xt[:], self.X[tok:tok + 128, :], writes=[xt])
        for db in range(4):
            sl = slice(db * 512, (db + 1) * 512)
            self.op("dve", lambda e, db=db, sl=sl: e.scalar_tensor_tensor(
                out=tbuf[:, sl], in0=osrc(db), scalar=(0.5 if half_gate else 1.0), in1=V0[:, sl], op0=ALU.mult, op1=ALU.mult),
                reads=[osrc_bufs[db], V0], writes=[tbuf])
        self.op("dve", lambda e: e.scalar_tensor_tensor(out=tbuf[:], in0=xt[:], scalar=ALPHA, in1=tbuf[:], op0=ALU.mult, op1=ALU.add),
                reads=[xt, tbuf], writes=[tbuf])
        for c in range(4):
            self.op("dve", lambda e, c=c: e.bn_stats(out=stats[:, c, :], in_=tbuf[:, c * 512:(c + 1) * 512]), reads=[tbuf], writes=[stats])
        self.op("dve", lambda e: e.bn_aggr(out=mv[:], in_=stats[:]), reads=[stats], writes=[mv])
        self.op("act", lambda e: e.activation(out=rstd[:], in_=mv[:, 1:2], func=AF.Sqrt, bias=self.eps5[:, 0:1], scale=1.0),
                reads=[mv, self.eps5], writes=[rstd])
        self.op("dve", lambda e: e.reciprocal(out=rstd[:], in_=rstd[:]), reads=[rstd], writes=[rstd])
        self.op("dve", lambda e: e.tensor_scalar(out=tbuf[:], in0=tbuf[:], scalar1=mv[:, 0:1], scalar2=rstd[:, 0:1],
                                                 op0=ALU.subtract, op1=ALU.mult), reads=[tbuf, mv, rstd], writes=[tbuf])
        self.op("dve", lambda e: e.tensor_tensor(out=tbuf[:], in0=tbuf[:], in1=V1[:], op=ALU.mult), reads=[tbuf, V1], writes=[tbuf])
        self.op("dve", lambda e: e.tensor_tensor(out=xt[:], in0=tbuf[:], in1=V2[:], op=ALU.add), reads=[tbuf, V2], writes=[xt])
        self.dma("sp", self.X[tok:tok + 128, :], xt[:], reads=[xt])
        if l == DEPTH - 1 and i == 2:
            if tok < TS:
                self.dma("sp", self.O["ys"][tok:tok + 128, :], xt[:], reads=[xt])
            else:
                self.dma("sp", self.O["yp"][tok - TS:tok - TS + 128, :], xt[:], reads=[xt])

    def load_ln(self, l, i, V1, V2):
        self.dma("sp", V1[:], self.I["ln_g"][l, i, :].partition_broadcast(128), writes=[V1])
        self.dma("sp", V2[:], self.I["ln_b"][l, i, :].partition_broadcast(128), writes=[V2])

    def phase_ffn(self, l, fi):
        I = self.I
        sub = 0 if fi == 0 else 2
        w1 = I["ffn_w1"][l, fi]
        w3 = I["ffn_w3"][l, fi]
        w2 = I["ffn_w2"][l, fi]
        fblocks = [(i * 256, 256) for i in range(21)] + [(21 * 256, 128)]
        with ExitStack() as st:
            hT = self.sb(st, "hT", [128, 16, 512], BF16)
            gT = self.sb(st, "gT", [128, 43, 512], BF16)
            wA = [self.sb(st, "w1b%d" % i, [128, 16, 256], BF16) for i in range(3)]
            wB = [self.sb(st, "w3b%d" % i, [128, 16, 256], BF16) for i in range(3)]
            wC = [self.sb(st, "w2b%d" % i, [128, 4, 512], BF16) for i in range(4)]
            self.wstage_alloc(st)
            V = [self.sb(st, "V%d" % i, [128, D], F32) for i in range(3)]
            xt = [self.sb(st, "xt%d" % i, [128, D], F32) for i in range(2)]
            hb = self.sb(st, "hb", [128, D], BF16)
            tmp = [self.sb(st, "tmp%d" % i, [128, 512], F32) for i in range(2)]
            tb = self.sb(st, "tb", [128, 4, D], F32)
            small = (self.sb(st, "stats", [128, 4, 6], F32), self.sb(st, "mv", [128, 2], F32), self.sb(st, "rstd", [128, 1], F32))
            banks = [self.ps(st, "bk%d" % i, [128, 512]) for i in range(6)]
            bbf = [self.ps(st, "bbf%d" % i, [128, 1024], BF16) for i in range(2)]
            for g in self.groups():
                ci = self.gci(g)
                self.load_vec(V[0], ci, 3 * sub + 1)
                self.load_vec(V[1], ci, 3 * sub + 0)
                self.op("dve", lambda e: e.tensor_scalar(out=V[0][:], in0=V[0][:], scalar1=1.0, scalar2=None, op0=ALU.add),
                        reads=[V[0]], writes=[V[0]])
                self.prologue(g, xt, hb, hT, V[0], V[1], bbf)
                fidx = 0
                for bi, (c0, nc_) in enumerate(fblocks):
                    a = wA[bi % 3]
                    b = wB[bi % 3]
                    self.wload(a, lambda k0, k1, a=a, nc_=nc_: a[:, k0:k1, 0:nc_], w1, c0, nc_, 16, key=(l, 'w1', fi, c0))
                    self.wload(b, lambda k0, k1, b=b, nc_=nc_: b[:, k0:k1, 0:nc_], w3, c0, nc_, 16, key=(l, 'w3', fi, c0))
                    for ft in range(nc_ // 128):
                        pa = banks[(fidx % 2) * 2]
                        pb = banks[(fidx % 2) * 2 + 1]
                        tm = tmp[fidx % 2]
                        self.mm(pa[:, :], [(a[:, kc, ft * 128:(ft + 1) * 128], hT[:, kc, :]) for kc in range(16)], [a, hT], pa)
                        self.mm(pb[:, :], [(b[:, kc, ft * 128:(ft + 1) * 128], hT[:, kc, :]) for kc in range(16)], [b, hT], pb)
                        self.op("act", lambda e, pa=pa, tm=tm: e.activation(out=tm[:], in_=pa[:], func=AF.Silu), reads=[pa], writes=[tm])
                        self.op("dve", lambda e, pb=pb, tm=tm, fidx=fidx: e.tensor_tensor(out=gT[:, fidx, :], in0=tm[:], in1=pb[:], op=ALU.mult),
                                reads=[tm, pb], writes=[gT])
                        fidx += 1
                self.load_vec(V[0], ci, 3 * sub + 2)
                self.load_ln(l, sub, V[1], V[2])
                nch = 11
                wi = 0
                for db in range(4):
                    for ch in range(nch):
                        f0 = ch * 4
                        nf = min(4, 43 - f0)
                        w = wC[wi % 4]
                        wi += 1
                        self.wload(w, lambda k0, k1, w=w: w[:, k0:k1, :], w2, db * 512, 512, nf, rows0=f0 * 128, key=(l, 'w2', fi, db, f0))
                        for tt in range(4):
                            pk = banks[2 + tt]
                            for fi_ in range(nf):
                                fc = f0 + fi_
                                self.op("pe", lambda e, pk=pk, w=w, fi_=fi_, fc=fc, tt=tt: e.matmul(
                                    pk[:, :], lhsT=gT[:, fc, tt * 128:(tt + 1) * 128], rhs=w[:, fi_, :], start=(fc == 0), stop=(fc == 42)),
                                    reads=[gT, w], writes=[pk], sig=(fi_ == nf - 1))
                    for tt in range(4):
                        pk = banks[2 + tt]
                        self.op("act", lambda e, pk=pk, tt=tt, db=db: e.activation(out=tb[:, tt, db * 512:(db + 1) * 512], in_=pk[:], func=AF.Copy),
                                reads=[pk], writes=[tb])
                for tt in range(4):
                    tok = g * 512 + tt * 128
                    x = xt[tt % 2]
                    self.epilogue(l, sub, tok, lambda db, tt=tt: tb[:, tt, db * 512:(db + 1) * 512], [tb] * 4, x, _V(tb, tt), V[0], V[1], V[2], True, small)
        self.barrier()


class _V:
    def __init__(self, t, tt):
        self.t = t
        self.tt = tt
        self.b = t.b

    def __getitem__(self, k):
        if isinstance(k, tuple):
            return self.t.t[(k[0], self.tt) + tuple(k[1:])]
        return self.t.t[k, self.tt]


SEQS = [(0, TS, 0, -1), (TS, 256, 1, 0), (TS + 256, 256, 1, 1)]


def _seqs(self):
    return [s for s in SEQS if (s[0] // 512) in self.groups()]


def mixer_alloc(self):
    if hasattr(self, "ZHT"):
        return
    S = self.scratch
    self.ZHT = S("ZHT", [3072, T]); self.GQT = S("GQT", [512, T]); self.GKT = S("GKT", [512, T])
    self.GK = S("GK", [T, 512]); self.GV = S("GV", [T, 1024], BF16); self.GR = S("GR", [T, 1024])
    self.GLRT = S("GLRT", [2, 16, T]); self.DQ = S("DQ", [T, 1024]); self.DK = S("DK", [T, 1024]); self.DV = S("DV", [T, 1024])
    self.GATES = S("GATES", [T, 6144])
    self.HTOK = [S("HTOK%d" % i, [T, 1024], BF16) for i in range(3)]
    self.KF = {4096: S("KF4096", [2, 8192, 1024]), 256: S("KF256", [2, 512, 1024])}
    self.YT = [S("Y%sT" % n, [1024, T], BF16) for n in "ABC"]
    self.OF = S("OF", [T, 1024])
    self.DQT = S("DQT", [8, 128, T], BF16); self.DKT = S("DKT", [8, 128, 256 + T], BF16); self.VA = S("VA", [256 + T, 8, 129], BF16)


def mixer_in(self, l):
    I = self.I
    W = I["w_in"][l]
    feat = [(c0, 512, self.ZHT, c0) for c0 in range(0, 3072, 512)] + [(3072, 512, self.GQT, 0), (3584, 512, self.GKT, 0)]
    tokb = [(3584, self.GK, 0, AF.Copy, F32, None)]
    tokb += [(4096 + i * 512, self.GV, i * 512, AF.Copy, BF16, None) for i in range(2)]
    tokb += [(5120 + i * 512, self.GR, i * 512, AF.Copy, F32, None) for i in range(2)]
    tokb += [(6176 + i * 512, self.DQ, i * 512, AF.Copy, F32, None) for i in range(2)]
    tokb += [(7200 + i * 512, self.DK, i * 512, AF.Copy, F32, "nck") for i in range(2)]
    tokb += [(8224 + i * 512, self.DV, i * 512, AF.Copy, F32, "ncv") for i in range(2)]
    tokb += [(9248 + i * 512, self.GATES, i * 512, AF.Sigmoid, F32, None) for i in range(12)]
    with ExitStack() as st:
        hT = self.sb(st, "hT", [128, 16, 512], BF16)
        wb = [self.sb(st, "wi%d" % i, [128, 16, 512], BF16) for i in range(3)]
        self.wstage_alloc(st)
        V = [self.sb(st, "V%d" % i, [128, D], F32) for i in range(2)]
        xt = [self.sb(st, "xt%d" % i, [128, D], F32) for i in range(2)]
        hb = self.sb(st, "hb", [128, D], BF16)
        stf = [self.sb(st, "stf%d" % i, [128, 512], F32) for i in range(4)]
        stb = [self.sb(st, "stb%d" % i, [128, 512], BF16) for i in range(2)]
        banks = [self.ps(st, "bk%d" % i, [128, 512]) for i in range(4)]
        bbf = [self.ps(st, "bbf%d" % i, [128, 1024], BF16) for i in range(2)]
        cnt = 0
        for g in self.groups():
            ci = self.gci(g)
            tok0 = g * 512
            self.load_vec(V[0], ci, 4)
            self.load_vec(V[1], ci, 3)
            self.op("dve", lambda e: e.tensor_scalar(out=V[0][:], in0=V[0][:], scalar1=1.0, scalar2=None, op0=ALU.add), reads=[V[0]], writes=[V[0]])
            self.prologue(g, xt, hb, hT, V[0], V[1], bbf)
            wi = 0
            for (c0, ncl, dst, r0) in feat:
                w = wb[wi % 3]; wi += 1
                self.wload(w, lambda k0, k1, w=w: w[:, k0:k1, :], W, c0, 512, 16, key=(l, 'win', c0))
                for ct in range(4):
                    bk = banks[cnt % 4]; sg = stf[cnt % 4]; cnt += 1
                    self.mm(bk[:, :], [(w[:, kc, ct * 128:(ct + 1) * 128], hT[:, kc, :]) for kc in range(16)], [w, hT], bk)
                    eng = "act" if cnt % 2 else "dve"
                    if eng == "act":
                        self.op("act", lambda e, bk=bk, sg=sg: e.activation(out=sg[:], in_=bk[:], func=AF.Copy), reads=[bk], writes=[sg])
                    else:
                        self.op("dve", lambda e, bk=bk, sg=sg: e.tensor_copy(out=sg[:], in_=bk[:]), reads=[bk], writes=[sg])
                    self.dma("sp", dst[r0 + ct * 128:r0 + (ct + 1) * 128, tok0:tok0 + 512], sg[:], reads=[sg])
            w = wb[wi % 3]; wi += 1
            self.wload(w, lambda k0, k1, w=w: w[:, k0:k1, 0:32], W, 6144, 32, 16, key=(l, 'win', 6144))
            for dd in range(2):
                bk = banks[cnt % 4]; sg = stf[cnt % 4]; cnt += 1
                self.mm(bk[0:16, :], [(w[:, kc, dd * 16:(dd + 1) * 16], hT[:, kc, :]) for kc in range(16)], [w, hT], bk)
                self.op("act", lambda e, bk=bk, sg=sg: e.activation(out=sg[0:16, :], in_=bk[0:16, :], func=AF.Copy), reads=[bk], writes=[sg])
                self.dma("sp", self.GLRT[dd, :, tok0:tok0 + 512], sg[0:16, :], reads=[sg])
            for (c0, dst, dc0, fn, dt, oname) in tokb:
                w = wb[wi % 3]; wi += 1
                self.wload(w, lambda k0, k1, w=w: w[:, k0:k1, :], W, c0, 512, 16, key=(l, 'win', c0))
                for tt in range(4):
                    bk = banks[cnt % 4]
                    sg = stf[cnt % 4] if dt == F32 else stb[cnt % 2]
                    cnt += 1
                    self.mm(bk[:, :], [(hT[:, kc, tt * 128:(tt + 1) * 128], w[:, kc, :]) for kc in range(16)], [w, hT], bk)
                    if fn == AF.Copy and cnt % 2 == 0:
                        self.op("dve", lambda e, bk=bk, sg=sg: e.tensor_copy(out=sg[:], in_=bk[:]), reads=[bk], writes=[sg])
                    else:
                        self.op("act", lambda e, bk=bk, sg=sg, fn=fn: e.activation(out=sg[:], in_=bk[:], func=fn), reads=[bk], writes=[sg])
                    tok = tok0 + tt * 128
                    self.dma("sp", dst[tok:tok + 128, dc0:dc0 + 512], sg[:], reads=[sg])
                    if oname is not None and tok >= TS:
                        pi = (tok - TS) // 256
                        t0 = (tok - TS) % 256
                        self.dma("sp", self.O[oname][pi, l, t0:t0 + 128, dc0:dc0 + 512], sg[:], reads=[sg])
    self.barrier()


def range_reduce(self, a, tmp, reads):
    self.op("dve", lambda e: e.tensor_scalar(out=tmp, in0=a, scalar1=1.0 / (2 * math.pi), scalar2=MAGIC, op0=ALU.mult, op1=ALU.add), reads=reads, writes=reads)
    self.op("dve", lambda e: e.tensor_scalar(out=tmp, in0=tmp, scalar1=-MAGIC, scalar2=-2 * math.pi, op0=ALU.add, op1=ALU.mult), reads=reads, writes=reads)
    self.op("dve", lambda e: e.tensor_tensor(out=a, in0=a, in1=tmp, op=ALU.add), reads=reads, writes=reads)


def hy_filters(self, l, L):
    I = self.I
    nt = L // 128
    KF = self.KF[L]
    Fc = I["c_f%d" % L]; NY = I["c_nyq%d" % L]
    with ExitStack() as st:
        fT = self.sb(st, "fT", [17, L], F32)
        w1 = self.sb(st, "hw1", [17, 64], F32); w2 = self.sb(st, "hw2", [64, 64], F32); w3 = self.sb(st, "hw3", [64, 4096], F32)
        pc = self.sb(st, "hpc", [64, 4], F32)
        ntl = self.sb(st, "ntl", [128, nt], F32)
        nyq = self.sb(st, "nyq", [128, nt], BF16)
        hd1 = self.sb(st, "hd1", [64, L], F32); hd2 = self.sb(st, "hd2", [64, L], F32)
        a_ = self.sb(st, "ha", [64, 512], F32); tm_ = self.sb(st, "htm", [64, 512], F32)
        banks = [self.ps(st, "bk%d" % i, [128, 512]) for i in range(5)]
        self.dma("sp", fT[:], I["c_featsT%d" % L][:, :], writes=[fT])
        self.dma("sp", w1[:], I["hy_w1"][l], writes=[w1]); self.dma("sp", w2[:], I["hy_w2"][l], writes=[w2]); self.dma("sp", w3[:], I["hy_w3"][l], writes=[w3])
        self.dma("sp", pc[:, 0:1], I["hy_b1"][l].rearrange("(m o) -> m o", o=1), writes=[pc])
        self.dma("sp", pc[:, 1:2], I["hy_b2"][l].rearrange("(m o) -> m o", o=1), writes=[pc])
        self.dma("sp", pc[:, 2:4], I["hy_freq"][l].rearrange("a m -> m a"), writes=[pc])
        self.dma("sp", ntl[:], I["c_ntl%d" % L][:, :], writes=[ntl])
        self.dma("sp", nyq[:], NY[:, :], writes=[nyq])
        n = min(512, L)
        for (wsrc, src, dst, bcol, fcol, kk) in ((w1, fT, hd1, 0, 2, 17), (w2, hd1, hd2, 1, 3, 64)):
            for tb in range(L // n):
                bk = banks[tb % 2]
                self.mm(bk[0:64, 0:n], [(wsrc[0:kk, :], src[0:kk, tb * n:(tb + 1) * n])], [wsrc, src], bk)
                self.op("dve", lambda e, bk=bk, bcol=bcol, fcol=fcol: e.tensor_scalar(
                    out=a_[:, 0:n], in0=bk[0:64, 0:n], scalar1=pc[:, bcol:bcol + 1], scalar2=pc[:, fcol:fcol + 1], op0=ALU.add, op1=ALU.mult),
                    reads=[bk, pc], writes=[a_])
                self.op("dve", lambda e: e.tensor_scalar(out=tm_[:, 0:n], in0=a_[:, 0:n], scalar1=1.0 / (2 * math.pi), scalar2=MAGIC, op0=ALU.mult, op1=ALU.add), reads=[a_], writes=[tm_])
                self.op("dve", lambda e: e.tensor_scalar(out=tm_[:, 0:n], in0=tm_[:, 0:n], scalar1=-MAGIC, scalar2=-2 * math.pi, op0=ALU.add, op1=ALU.mult), reads=[tm_], writes=[tm_])
                self.op("dve", lambda e: e.tensor_tensor(out=a_[:, 0:n], in0=a_[:, 0:n], in1=tm_[:, 0:n], op=ALU.add), reads=[a_, tm_], writes=[a_])
                self.op("act", lambda e, dst=dst, tb=tb: e.activation(out=dst[:, tb * n:(tb + 1) * n], in_=a_[:, 0:n], func=AF.Sin), reads=[a_], writes=[dst])
        spl = self.sb(st, "spl", [128, nt, 512], BF16); smi = self.sb(st, "smi", [128, nt, 512], BF16)
        dec = [self.sb(st, "dec%d" % i, [128, 512], F32) for i in range(2)]
        ex = [self.sb(st, "ex%d" % i, [128, 512], F32) for i in range(2)]
        hh = [self.sb(st, "hh%d" % i, [128, 512], F32) for i in range(2)]
        ft = [self.sb(st, "ft%d" % i, [128, nt, 128], BF16) for i in range(2)]
        stg = [self.sb(st, "kst%d" % i, [128, 512], F32) for i in range(2)]
        fcnt = 0
        for o in range(2):
            for cb in range(2):
                for dr in range(2):
                    c0 = o * 2048 + dr * 1024 + cb * 512
                    self.dma("sp", dec[dr][:], I["hy_decay"][l, c0:c0 + 512].partition_broadcast(128), writes=[dec[dr]])
                    self.op("act", lambda e, dr=dr: e.activation(out=dec[dr][:], in_=dec[dr][:], func=AF.Abs), reads=[dec[dr]], writes=[dec[dr]])
                for tt in range(nt):
                    for dr in range(2):
                        c0 = o * 2048 + dr * 1024 + cb * 512
                        bk = banks[dr]
                        self.mm(bk[:, :], [(hd2[:, tt * 128:(tt + 1) * 128], w3[:, c0:c0 + 512])], [hd2, w3], bk)
                        self.op("act", lambda e, dr=dr, tt=tt: e.activation(out=ex[dr][:], in_=dec[dr][:], func=AF.Exp, scale=ntl[:, tt:tt + 1]), reads=[dec[dr], ntl], writes=[ex[dr]])
                        self.op("dve", lambda e, dr=dr, bk=bk: e.tensor_tensor(out=hh[dr][:], in0=ex[dr][:], in1=bk[:], op=ALU.mult), reads=[ex[dr], bk], writes=[hh[dr]])
                    if tt == 0:
                        self.op("dve", lambda e: e.memset(hh[1][0:1, :], 0.0), writes=[hh[1]])
                    self.op("dve", lambda e, tt=tt: e.tensor_tensor(out=spl[:, tt, :], in0=hh[0][:], in1=hh[1][:], op=ALU.add), reads=hh, writes=[spl])
                    self.op("dve", lambda e, tt=tt: e.tensor_tensor(out=smi[:, tt, :], in0=hh[0][:], in1=hh[1][:], op=ALU.subtract), reads=hh, writes=[smi])
                for fti in range(2 * nt):
                    src = spl if fti < nt else smi
                    f = ft[fcnt % 2]; sg = stg[fcnt % 2]; bk = banks[2 + fcnt % 2]; fcnt += 1
                    self.dma("sp", f[:], Fc[fti], writes=[f])
                    self.mm(bk[:, :], [(f[:, kc, :], src[:, kc, :]) for kc in range(nt)], [f, src], bk)
                    self.op("act", lambda e, bk=bk, sg=sg: e.activation(out=sg[:], in_=bk[:], func=AF.Copy), reads=[bk], writes=[sg])
                    if fti == nt:
                        b4 = banks[4]
                        self.mm(b4[0:1, :], [(nyq[:, kc:kc + 1], spl[:, kc, :]) for kc in range(nt)], [nyq, spl], b4)
                        self.op("act", lambda e, sg=sg, b4=b4: e.activation(out=sg[0:1, :], in_=b4[0:1, :], func=AF.Copy), reads=[b4], writes=[sg])
                    self.dma("sp", KF[o, fti * 128:(fti + 1) * 128, cb * 512:(cb + 1) * 512], sg[:], reads=[sg])
    self.barrier()


def hy_prep(self, l):
    I = self.I
    for (s0, L, ci, pi) in self.seqs():
        nt = L // 128
        with ExitStack() as st:
            cT = self.sb(st, "cT", [128, 4, L], BF16)
            u = [self.sb(st, "u%d" % i, [128, L + 2], F32) for i in range(2)]
            acc = self.sb(st, "acc", [128, L], F32)
            wc = [self.sb(st, "wc%d" % i, [128, 4], F32) for i in range(2)]
            stg = [self.sb(st, "pst%d" % i, [128, 512], BF16) for i in range(2)]
            bbf = [self.ps(st, "bbf%d" % i, [128, 1024], BF16) for i in range(2)]
            for i in range(2):
                self.op("dve", lambda e, i=i: e.memset(u[i][:, 0:1], 0.0), writes=[u[i]])
                self.op("dve", lambda e, i=i: e.memset(u[i][:, L + 1:L + 2], 0.0), writes=[u[i]])
            cnt = 0
            for r in range(3):
                for cb in range(2):
                    for ct in range(4):
                        ch0 = r * 1024 + cb * 512 + ct * 128
                        uu = u[cnt % 2]; w = wc[cnt % 2]; cnt += 1
                        self.dma("sp", uu[:, 1:L + 1], self.ZHT[ch0:ch0 + 128, s0:s0 + L], writes=[uu])
                        self.dma("sp", w[:, 0:3], I["hy_conv_w"][l, :, ch0:ch0 + 128].rearrange("k c -> c k"), writes=[w])
                        self.dma("sp", w[:, 3:4], I["hy_conv_b"][l, ch0:ch0 + 128].rearrange("(c o) -> c o", o=1), writes=[w])
                        self.op("dve", lambda e, uu=uu, w=w: e.tensor_scalar(out=acc[:], in0=uu[:, 1:L + 1], scalar1=w[:, 1:2], scalar2=w[:, 3:4], op0=ALU.mult, op1=ALU.add), reads=[uu, w], writes=[acc])
                        self.op("dve", lambda e, uu=uu, w=w: e.scalar_tensor_tensor(out=acc[:], in0=uu[:, 0:L], scalar=w[:, 0:1], in1=acc[:], op0=ALU.mult, op1=ALU.add), reads=[uu, w, acc], writes=[acc])
                        self.op("dve", lambda e, uu=uu, w=w, ct=ct: e.scalar_tensor_tensor(out=cT[:, ct, :], in0=uu[:, 2:L + 2], scalar=w[:, 2:3], in1=acc[:], op0=ALU.mult, op1=ALU.add), reads=[uu, w, acc], writes=[cT])
                    for tt in range(nt):
                        pb = bbf[tt % 2]; sg = stg[tt % 2]
                        for ct in range(4):
                            self.op("pe", lambda e, pb=pb, ct=ct, tt=tt: e.transpose(out=pb[:, ct * 128:(ct + 1) * 128], in_=cT[:, ct, tt * 128:(tt + 1) * 128], identity=self.ident[:]), reads=[cT, self.ident], writes=[pb])
                        self.op("act", lambda e, pb=pb, sg=sg: e.activation(out=sg[:], in_=pb[:, 0:512], func=AF.Copy), reads=[pb], writes=[sg])
                        self.dma("sp", self.HTOK[r][s0 + tt * 128:s0 + (tt + 1) * 128, cb * 512:(cb + 1) * 512], sg[:], reads=[sg])
        self.barrier()


def hy_conv(self, l):
    I = self.I
    for (s0, L, ci, pi) in self.seqs():
        nt = L // 128
        KF = self.KF[L]; Fc = I["c_f%d" % L]; Gc = I["c_g%d" % L]
        with ExitStack() as st:
            vt = self.sb(st, "vt", [128, nt, 512], BF16)
            Y = self.sb(st, "Y", [128, 2 * nt, 512], BF16)
            ft = [self.sb(st, "ft%d" % i, [128, nt, 128], BF16) for i in range(4)]
            gt = [self.sb(st, "gt%d" % i, [128, 2 * nt, 128], BF16) for i in range(2)]
            kf = [self.sb(st, "kf%d" % i, [128, 512], F32) for i in range(4)]
            tq = [self.sb(st, "tq%d" % i, [128, 512], F32) for i in range(4)]
            bia = [self.sb(st, "bia%d" % i, [128, 512], F32) for i in range(2)]
            xm = [self.sb(st, "xm%d" % i, [128, 512], BF16) for i in range(2)]
            ya = [self.sb(st, "ya%d" % i, [128, 512], BF16) for i in range(2)]
            ystg = [self.sb(st, "ystg%d" % i, [128, 4, 128], BF16) for i in range(2)]
            banks = [self.ps(st, "bk%d" % i, [128, 512]) for i in range(6)]
            bbf = [self.ps(st, "bbf%d" % i, [128, 1024], BF16) for i in range(2)]
            for cb in range(2):
                for k0 in range(0, nt, 8):
                    k1 = min(nt, k0 + 8)
                    self.dma("sp", vt[:, k0:k1, :], self.HTOK[0][s0 + k0 * 128:s0 + k1 * 128, cb * 512:(cb + 1) * 512].rearrange("(kc p) c -> p kc c", p=128), writes=[vt])
                for o in range(2):
                    self.dma("sp", bia[o][:], I["hy_bias"][l, o, cb * 512:(cb + 1) * 512].partition_broadcast(128), writes=[bia[o]])
                for o in range(2):
                    for j in range(nt):
                        fr = ft[(j % 2) * 2]; fi = ft[(j % 2) * 2 + 1]
                        kr = kf[(j % 2) * 2]; ki = kf[(j % 2) * 2 + 1]
                        br = banks[(j % 2) * 2]; bi = banks[(j % 2) * 2 + 1]
                        self.dma("sp", fr[:], Fc[j], writes=[fr]); self.dma("sp", fi[:], Fc[nt + j], writes=[fi])
                        self.dma("sp", kr[:], KF[o, j * 128:(j + 1) * 128, cb * 512:(cb + 1) * 512], writes=[kr])
                        self.dma("sp", ki[:], KF[o, (nt + j) * 128:(nt + j + 1) * 128, cb * 512:(cb + 1) * 512], writes=[ki])
                        self.mm(br[:, :], [(fr[:, kc, :], vt[:, kc, :]) for kc in range(nt)], [fr, vt], br)
                        self.mm(bi[:, :], [(fi[:, kc, :], vt[:, kc, :]) for kc in range(nt)], [fi, vt], bi)
                        self.op("dve", lambda e, br=br, kr=kr: e.tensor_tensor(out=tq[0][:], in0=kr[:], in1=br[:], op=ALU.mult), reads=[kr, br], writes=[tq[0]])
                        self.op("dve", lambda e, bi=bi, ki=ki: e.tensor_tensor(out=tq[1][:], in0=ki[:], in1=bi[:], op=ALU.mult), reads=[ki, bi], writes=[tq[1]])
                        self.op("dve", lambda e, br=br, ki=ki: e.tensor_tensor(out=tq[2][:], in0=ki[:], in1=br[:], op=ALU.mult), reads=[ki, br], writes=[tq[2]])
                        self.op("dve", lambda e, bi=bi, kr=kr: e.tensor_tensor(out=tq[3][:], in0=kr[:], in1=bi[:], op=ALU.mult), reads=[kr, bi], writes=[tq[3]])
                        self.op("dve", lambda e, j=j: e.tensor_tensor(out=Y[:, j, :], in0=tq[0][:], in1=tq[1][:], op=ALU.subtract), reads=[tq[0], tq[1]], writes=[Y])
                        self.op("dve", lambda e, j=j: e.tensor_tensor(out=Y[:, nt + j, :], in0=tq[2][:], in1=tq[3][:], op=ALU.add), reads=[tq[2], tq[3]], writes=[Y])
                        if j == 0:
                            self.op("dve", lambda e: e.tensor_copy(out=Y[0:1, 0, :], in_=tq[0][0:1, :]), reads=[tq[0]], writes=[Y])
                            self.op("dve", lambda e: e.tensor_copy(out=Y[0:1, nt, :], in_=tq[1][0:1, :]), reads=[tq[1]], writes=[Y])
                    for tt in range(nt):
                        g_ = gt[tt % 2]; bk = banks[4 + tt % 2]; x_ = xm[tt % 2]
                        self.dma("sp", g_[:], Gc[tt], writes=[g_])
                        self.dma("sp", x_[:], self.HTOK[1 + o][s0 + tt * 128:s0 + (tt + 1) * 128, cb * 512:(cb + 1) * 512], writes=[x_])
                        self.mm(bk[:, :], [(g_[:, kc, :], Y[:, kc, :]) for kc in range(2 * nt)], [g_, Y], bk)
                        t_ = tq[tt % 2]
                        self.op("dve", lambda e, tt=tt, t_=t_, o=o: e.tensor_tensor(out=t_[:], in0=vt[:, tt, :], in1=bia[o][:], op=ALU.mult), reads=[vt, bia[o]], writes=[t_])
                        self.op("dve", lambda e, t_=t_, bk=bk: e.tensor_tensor(out=t_[:], in0=t_[:], in1=bk[:], op=ALU.add), reads=[t_, bk], writes=[t_])
                        if o == 0:
                            self.op("dve", lambda e, tt=tt, t_=t_, x_=x_: e.tensor_tensor(out=vt[:, tt, :], in0=t_[:], in1=x_[:], op=ALU.mult), reads=[t_, x_], writes=[vt])
                        else:
                            y_ = ya[tt % 2]; pb = bbf[tt % 2]; sg = ystg[tt % 2]
                            self.op("dve", lambda e, t_=t_, x_=x_, y_=y_: e.tensor_tensor(out=y_[:], in0=t_[:], in1=x_[:], op=ALU.mult), reads=[t_, x_], writes=[y_])
                            for ct in range(4):
                                self.op("pe", lambda e, pb=pb, ct=ct, y_=y_: e.transpose(out=pb[:, ct * 128:(ct + 1) * 128], in_=y_[:, ct * 128:(ct + 1) * 128], identity=self.ident[:]), reads=[y_, self.ident], writes=[pb])
                            self.op("act", lambda e, pb=pb, sg=sg: e.activation(out=sg[:], in_=pb[:, 0:512].rearrange("p (c t) -> p c t", c=4), func=AF.Copy), reads=[pb], writes=[sg])
                            self.dma("sp", self.YT[0][cb * 512:(cb + 1) * 512, s0 + tt * 128:s0 + (tt + 1) * 128].rearrange("(c p) t -> p c t", p=128), sg[:], reads=[sg])
        self.barrier()


def gla(self, l):
    I = self.I
    with ExitStack() as st:
        mats = self.sb(st, "mats", [128, 6, 128], F32); masks = self.sb(st, "masks", [128, 2, 128], F32)
        wa = self.sb(st, "wa", [16, 2, 512], F32); babc = self.sb(st, "babc", [128, 2, 512], F32); gn = self.sb(st, "gn", [128, 256], F32)
        S = self.sb(st, "S", [128, 4, 256], F32); Sb = self.sb(st, "Sb", [128, 4, 256], BF16)
        self.dma("sp", mats[:], I["c_glam"].rearrange("m j i -> j m i"), writes=[mats])
        self.dma("sp", masks[:], I["c_gmask"].rearrange("m j i -> j m i"), writes=[masks])
        self.dma("sp", wa[:], I["gla_wa"][l].rearrange("d r n -> r d n"), writes=[wa])
        for d in range(2):
            self.dma("sp", babc[:, d, :], I["gla_ba"][l, d, :].partition_broadcast(128), writes=[babc])
        self.dma("sp", gn[:], I["gla_norm_g"][l, :].partition_broadcast(128), writes=[gn])
        NB = 2
        lrT = [self.sb(st, "lrT%d" % i, [16, 128], F32) for i in range(NB)]
        qT = [self.sb(st, "qT%d" % i, [128, 4, 128], F32) for i in range(NB)]
        kT = [self.sb(st, "kT%d" % i, [128, 4, 128], F32) for i in range(NB)]
        kt = [self.sb(st, "kt%d" % i, [128, 512], F32) for i in range(NB)]
        vt = [self.sb(st, "vt%d" % i, [128, 1024], BF16) for i in range(NB)]
        of = [self.sb(st, "of%d" % i, [128, 1024], F32) for i in range(NB)]
        gr = [self.sb(st, "gr%d" % i, [128, 1024], F32) for i in range(NB)]
        lx = self.sb(st, "lx", [128, 512], F32); ll = self.sb(st, "ll", [128, 512], F32)
        ex = [self.sb(st, "ex%d" % i, [128, 512], F32) for i in range(2)]
        Q1 = [self.sb(st, "Q1%d" % i, [128, 128], BF16) for i in range(2)]; K1 = [self.sb(st, "K1%d" % i, [128, 128], BF16) for i in range(2)]
        Q2 = [self.sb(st, "Q2%d" % i, [128, 128], BF16) for i in range(2)]; K2 = [self.sb(st, "K2%d" % i, [128, 128], BF16) for i in range(2)]
        am = [self.sb(st, "am%d" % i, [128, 128], BF16) for i in range(2)]
        ot = self.sb(st, "ot", [128, 1024], F32); sq = self.sb(st, "sq", [128, 256], F32)
        ss = self.sb(st, "ss", [128, 4], F32); rs = self.sb(st, "rs", [128, 4], F32)
        sgr = self.sb(st, "sgr", [128, 1024], F32); yb = self.sb(st, "yb", [128, 1024], BF16)
        ystg = [self.sb(st, "ystg%d" % i, [128, 8, 128], BF16) for i in range(2)]
        bL = self.ps(st, "bL", [128, 512]); bE = [self.ps(st, "bE%d" % i, [128, 512]) for i in range(2)]
        bA = self.ps(st, "bA", [128, 512]); bO = [self.ps(st, "bO%d" % i, [128, 512]) for i in range(2)]
        bS = self.ps(st, "bS", [128, 512]); bT = self.ps(st, "bT", [128, 1024], BF16)
        qscale = 128.0 ** -0.5
        it = 0
        for (s0, L, ci, pi) in self.seqs():
            nt = L // 128
            for d in range(2):
                if pi < 0:
                    for h in range(4):
                        self.dma("sp", S[:, h, :], I["sg"][l, d, h], writes=[S])
                else:
                    self.op("dve", lambda e: e.memset(S[:], 0.0), writes=[S])
                self.op("act", lambda e: e.activation(out=Sb[:], in_=S[:], func=AF.Copy), reads=[S], writes=[Sb])
                order = list(range(nt)) if d == 0 else list(range(nt - 1, -1, -1))
                for c in order:
                    tok = s0 + c * 128
                    b_ = it % NB; it += 1
                    self.dma("sp", lrT[b_][:], self.GLRT[d, :, tok:tok + 128], writes=[lrT[b_]])
                    self.dma("sp", qT[b_][:], self.GQT[:, tok:tok + 128].rearrange("(h k) t -> k h t", k=128), writes=[qT[b_]])
                    self.dma("sp", kT[b_][:], self.GKT[:, tok:tok + 128].rearrange("(h k) t -> k h t", k=128), writes=[kT[b_]])
                    self.dma("sp", kt[b_][:], self.GK[tok:tok + 128, :], writes=[kt[b_]])
                    self.dma("sp", vt[b_][:], self.GV[tok:tok + 128, :], writes=[vt[b_]])
                    if d == 1:
                        self.dma("sp", of[b_][:], self.OF[tok:tok + 128, :], writes=[of[b_]])
                        self.dma("sp", gr[b_][:], self.GR[tok:tok + 128, :], writes=[gr[b_]])
                    self.mm(bL[:, :], [(lrT[b_][:, :], wa[:, d, :])], [lrT[b_], wa], bL)
                    self.op("dve", lambda e, d=d: e.tensor_tensor(out=lx[:], in0=bL[:], in1=babc[:, d, :], op=ALU.add), reads=[bL, babc], writes=[lx])
                    self.op("dve", lambda e: e.tensor_scalar(out=lx[:], in0=lx[:], scalar1=-80.0, scalar2=None, op0=ALU.max), reads=[lx], writes=[lx])
                    self.op("act", lambda e: e.activation(out=lx[:], in_=lx[:], func=AF.Exp, scale=-1.0), reads=[lx], writes=[lx])
                    self.op("act", lambda e: e.activation(out=ll[:], in_=lx[:], func=AF.Ln, bias=self.ones[:, 0:1], scale=1.0), reads=[lx, self.ones], writes=[ll])
                    for h in range(4):
                        e_ = bE[h % 2]; x_ = ex[h % 2]; q1 = Q1[h % 2]; k1 = K1[h % 2]; q2 = Q2[h % 2]; k2 = K2[h % 2]; a_ = am[h % 2]
                        hs = slice(h * 128, (h + 1) * 128)
                        self.mm(e_[:, 0:128], [(ll[:, hs], mats[:, 3 * d + 0, :])], [ll, mats], e_)
                        self.mm(e_[:, 128:256], [(ll[:, hs], mats[:, 3 * d + 1, :])], [ll, mats], e_)
                        self.mm(e_[:, 256:384], [(mats[:, 3 * d + 2, :], ll[:, hs])], [ll, mats], e_)
                        self.op("act", lambda e, e_=e_, x_=x_: e.activation(out=x_[:, 0:384], in_=e_[:, 0:384], func=AF.Exp), reads=[e_], writes=[x_])
                        self.op("act", lambda e, e_=e_, x_=x_: e.activation(out=x_[:, 384:512], in_=e_[:, 0:128], func=AF.Exp, scale=-1.0), reads=[e_], writes=[x_])
                        self.op("dve", lambda e, x_=x_, q1=q1, h=h, b_=b_: e.scalar_tensor_tensor(out=q1[:], in0=qT[b_][:, h, :], scalar=qscale, in1=x_[:, 0:128], op0=ALU.mult, op1=ALU.mult), reads=[qT[b_], x_], writes=[q1])
                        self.op("dve", lambda e, x_=x_, k1=k1, h=h, b_=b_: e.tensor_tensor(out=k1[:], in0=kT[b_][:, h, :], in1=x_[:, 384:512], op=ALU.mult), reads=[kT[b_], x_], writes=[k1])
                        self.op("dve", lambda e, x_=x_, q2=q2, h=h, b_=b_: e.scalar_tensor_tensor(out=q2[:], in0=qT[b_][:, h, :], scalar=qscale, in1=x_[:, 128:256], op0=ALU.mult, op1=ALU.mult), reads=[qT[b_], x_], writes=[q2])
                        self.op("dve", lambda e, x_=x_, k2=k2, hs=hs, b_=b_: e.tensor_tensor(out=k2[:], in0=kt[b_][:, hs], in1=x_[:, 256:384], op=ALU.mult), reads=[kt[b_], x_], writes=[k2])
                        self.mm(bA[:, 0:128], [(k1[:, :], q1[:, :])], [k1, q1], bA)
                        self.op("dve", lambda e, a_=a_, d=d: e.tensor_tensor(out=a_[:], in0=bA[:, 0:128], in1=masks[:, d, :], op=ALU.mult), reads=[bA, masks], writes=[a_])
                        ob = bO[h // 2]
                        vs = slice(h * 256, (h + 1) * 256)
                        self.mm(ob[:, (h % 2) * 256:(h % 2) * 256 + 256], [(a_[:, :], vt[b_][:, vs]), (q2[:, :], Sb[:, h, :])], [a_, vt[b_], q2, Sb], ob)
                        self.mm(bS[:, 0:256], [(k2[:, :], vt[b_][:, vs])], [k2, vt[b_]], bS)
                        dc = x_[:, 255:256] if d == 0 else x_[:, 128:129]
                        self.op("dve", lambda e, h=h, dc=dc: e.scalar_tensor_tensor(out=S[:, h, :], in0=S[:, h, :], scalar=dc, in1=bS[:, 0:256], op0=ALU.mult, op1=ALU.add), reads=[S, x_, bS], writes=[S])
                        self.op("act", lambda e, h=h: e.activation(out=Sb[:, h, :], in_=S[:, h, :], func=AF.Copy), reads=[S], writes=[Sb])
                    if d == 0:
                        for hh in range(2):
                            self.op("act", lambda e, hh=hh: e.activation(out=ot[:, hh * 512:(hh + 1) * 512], in_=bO[hh][:], func=AF.Copy), reads=[bO[hh]], writes=[ot])
                        self.dma("sp", self.OF[tok:tok + 128, :], ot[:], reads=[ot])
                    else:
                        for hh in range(2):
                            self.op("dve", lambda e, hh=hh, b_=b_: e.tensor_tensor(out=ot[:, hh * 512:(hh + 1) * 512], in0=of[b_][:, hh * 512:(hh + 1) * 512], in1=bO[hh][:], op=ALU.add), reads=[of[b_], bO[hh]], writes=[ot])
                        for h in range(4):
                            self.op("act", lambda e, h=h: e.activation(out=sq[:], in_=ot[:, h * 256:(h + 1) * 256], func=AF.Square), reads=[ot], writes=[sq])
                            self.op("dve", lambda e, h=h: e.reduce_sum(out=ss[:, h:h + 1], in_=sq[:], axis=mybir.AxisListType.X), reads=[sq], writes=[ss])
                        self.op("act", lambda e: e.activation(out=rs[:], in_=ss[:], func=AF.Sqrt, bias=self.eps6[:, 0:1], scale=1.0 / 256.0), reads=[ss, self.eps6], writes=[rs])
                        self.op("dve", lambda e: e.reciprocal(out=rs[:], in_=rs[:]), reads=[rs], writes=[rs])
                        self.op("act", lambda e, b_=b_: e.activation(out=sgr[:], in_=gr[b_][:], func=AF.Silu), reads=[gr[b_]], writes=[sgr])
                        for h in range(4):
                            self.op("dve", lambda e, h=h: e.scalar_tensor_tensor(out=ot[:, h * 256:(h + 1) * 256], in0=ot[:, h * 256:(h + 1) * 256], scalar=rs[:, h:h + 1], in1=gn[:], op0=ALU.mult, op1=ALU.mult), reads=[ot, rs, gn], writes=[ot])
                        self.op("dve", lambda e: e.tensor_tensor(out=yb[:], in0=ot[:], in1=sgr[:], op=ALU.mult), reads=[ot, sgr], writes=[yb])
                        sg_ = ystg[c % 2]
                        for cc in range(8):
                            self.op("pe", lambda e, cc=cc: e.transpose(out=bT[:, cc * 128:(cc + 1) * 128], in_=yb[:, cc * 128:(cc + 1) * 128], identity=self.ident[:]), reads=[yb, self.ident], writes=[bT])
                        self.op("act", lambda e, sg_=sg_: e.activation(out=sg_[:], in_=bT[:, :].rearrange("p (c t) -> p c t", c=8), func=AF.Copy), reads=[bT], writes=[sg_])
                        self.dma("sp", self.YT[1][:, tok:tok + 128].rearrange("(c p) t -> p c t", p=128), sg_[:], reads=[sg_])
                if pi >= 0:
                    for h in range(4):
                        self.dma("sp", self.O["nsg"][pi, l, d, h], S[:, h, :], reads=[S])
                self.barrier()
    self.barrier()


def diff(self, l):
    I = self.I
    lam_init = 0.8 - 0.6 * math.exp(-0.3 * l)
    with ExitStack() as st:
        dl = self.sb(st, "dl", [128, 256], F32); pr = self.sb(st, "pr", [128, 128], F32); sm = self.sb(st, "sm", [128, 4], F32)
        lam = self.sb(st, "lam", [128, 2], F32); gnc = self.sb(st, "gnc", [128, 128], F32)
        self.dma("sp", dl[:], I["diff_lam"][l].rearrange("a b -> (a b)").partition_broadcast(128), writes=[dl])
        self.op("dve", lambda e: e.tensor_tensor(out=pr[:, 0:64], in0=dl[:, 0:64], in1=dl[:, 64:128], op=ALU.mult), reads=[dl], writes=[pr])
        self.op("dve", lambda e: e.tensor_tensor(out=pr[:, 64:128], in0=dl[:, 128:192], in1=dl[:, 192:256], op=ALU.mult), reads=[dl], writes=[pr])
        self.op("dve", lambda e: e.reduce_sum(out=sm[:, 0:1], in_=pr[:, 0:64], axis=mybir.AxisListType.X), reads=[pr], writes=[sm])
        self.op("dve", lambda e: e.reduce_sum(out=sm[:, 1:2], in_=pr[:, 64:128], axis=mybir.AxisListType.X), reads=[pr], writes=[sm])
        self.op("act", lambda e: e.activation(out=sm[:, 2:4], in_=sm[:, 0:2], func=AF.Exp), reads=[sm], writes=[sm])
        self.op("dve", lambda e: e.tensor_tensor(out=lam[:, 0:1], in0=sm[:, 2:3], in1=sm[:, 3:4], op=ALU.subtract), reads=[sm], writes=[lam])
        self.op("dve", lambda e: e.tensor_scalar(out=lam[:, 1:2], in0=lam[:, 0:1], scalar1=lam_init, scalar2=-1.0, op0=ALU.add, op1=ALU.mult), reads=[lam], writes=[lam])
        self.dma("sp", gnc[:], I["diff_norm_g"][l, :].partition_broadcast(128), writes=[gnc])
        self.op("dve", lambda e: e.tensor_scalar(out=gnc[:], in0=gnc[:], scalar1=(1.0 - lam_init), scalar2=None, op0=ALU.mult), reads=[gnc], writes=[gnc])
        with ExitStack() as s2:
            xin = [self.sb(s2, "xin%d" % i, [128, 1024], F32) for i in range(2)]
            xsw = self.sb(s2, "xsw", [128, 1024], F32); t1 = self.sb(s2, "t1", [128, 1024], F32)
            cs = [self.sb(s2, "cs%d" % i, [128, 1024], F32) for i in range(2)]
            xb = [self.sb(s2, "xb%d" % i, [128, 1024], BF16) for i in range(2)]
            va = [self.sb(s2, "va%d" % i, [128, 8, 129], BF16) for i in range(2)]
            stg = [self.sb(s2, "dstg%d" % i, [128, 8, 128], BF16) for i in range(2)]
            bT = [self.ps(s2, "bT%d" % i, [128, 1024], BF16) for i in range(2)]
            for i in range(2):
                self.op("dve", lambda e, i=i: e.memset(va[i][:, :, 128:129], 1.0), writes=[va[i]])
            cnt = 0
            work = []
            for (s0, L, ci, pi) in self.seqs():
                if pi < 0:
                    for c in range(2):
                        work.append(("k", I["ck"][l, c * 128:(c + 1) * 128, :], None, c * 128))
                        work.append(("v", I["cv"][l, c * 128:(c + 1) * 128, :], None, c * 128))
                for c in range(L // 128):
                    tok = s0 + c * 128
                    pos = tok if pi < 0 else None
                    work.append(("q", self.DQ[tok:tok + 128, :], pos, tok))
                    work.append(("k", self.DK[tok:tok + 128, :], pos, 256 + tok))
                    work.append(("v", self.DV[tok:tok + 128, :], None, 256 + tok))
            lastpos = None
            for (kind, src, pos, dst) in work:
                x = xin[cnt % 2]; b = xb[cnt % 2]; v_ = va[cnt % 2]; sg = stg[cnt % 2]; pb = bT[cnt % 2]; cnt += 1
                self.dma("sp", x[:], src, writes=[x])
                if kind == "v":
                    self.op("act", lambda e, x=x, v_=v_: e.activation(out=v_[:, :, 0:128], in_=x[:, :].rearrange("p (h e) -> p h e", h=8), func=AF.Copy), reads=[x], writes=[v_])
                    self.dma("sp", self.VA[dst:dst + 128, :, :], v_[:], reads=[v_])
                    continue
                if pos is not None:
                    if pos != lastpos:
                        self.dma("sp", cs[0][:], I["c_cosf"][pos:pos + 128, :], writes=[cs[0]])
                        self.dma("sp", cs[1][:], I["c_sinf"][pos:pos + 128, :], writes=[cs[1]])
                        lastpos = pos
                    x4 = x[:, :].rearrange("p (g h n) -> p g h n", h=2, n=16)
                    s4 = xsw[:, :].rearrange("p (g h n) -> p g h n", h=2, n=16)
                    self.op("act", lambda e, x4=x4, s4=s4: e.activation(out=s4[:, :, 0, :], in_=x4[:, :, 1, :], func=AF.Copy), reads=[x], writes=[xsw])
                    self.op("act", lambda e, x4=x4, s4=s4: e.activation(out=s4[:, :, 1, :], in_=x4[:, :, 0, :], func=AF.Copy), reads=[x], writes=[xsw])
                    self.op("dve", lambda e, x=x: e.tensor_tensor(out=t1[:], in0=x[:], in1=cs[0][:], op=ALU.mult), reads=[x, cs[0]], writes=[t1])
                    self.op("dve", lambda e: e.tensor_tensor(out=xsw[:], in0=xsw[:], in1=cs[1][:], op=ALU.mult), reads=[xsw, cs[1]], writes=[xsw])
                    self.op("dve", lambda e, b=b: e.tensor_tensor(out=b[:], in0=t1[:], in1=xsw[:], op=ALU.add), reads=[t1, xsw], writes=[b])
                else:
                    self.op("act", lambda e, x=x, b=b: e.activation(out=b[:], in_=x[:], func=AF.Copy), reads=[x], writes=[b])
                for h in range(8):
                    self.op("pe", lambda e, h=h, b=b, pb=pb: e.transpose(out=pb[:, h * 128:(h + 1) * 128], in_=b[:, h * 128:(h + 1) * 128], identity=self.ident[:]), reads=[b, self.ident], writes=[pb])
                self.op("act", lambda e, pb=pb, sg=sg: e.activation(out=sg[:], in_=pb[:, :].rearrange("p (h t) -> p h t", h=8), func=AF.Copy), reads=[pb], writes=[sg])
                dstT = self.DQT if kind == "q" else self.DKT
                self.dma("sp", dstT[:, :, dst:dst + 128].rearrange("h p t -> p h t"), sg[:], reads=[sg])
        self.barrier()
        with ExitStack() as s2:
            KT = [self.sb(s2, "KT%d" % i, [128, 4352], BF16) for i in range(2)]
            VAh = [self.sb(s2, "VAh%d" % i, [128, 34, 129], BF16) for i in range(2)]
            QT = [[self.sb(s2, "QT%d_%d" % (i, j), [128, 4096], BF16) for j in range(2)] for i in range(2)]
            for i in range(2):
                for j in range(2):
                    self.op("dve", lambda e, i=i, j=j: e.memset(QT[i][j][:], 0.0), writes=[QT[i][j]])
            PT = [self.sb(s2, "PT%d" % i, [128, 512], BF16) for i in range(3)]
            rr = self.sb(s2, "rr", [128, 4], F32); o32 = self.sb(s2, "o32", [128, 128], F32); sq = self.sb(s2, "sq", [128, 128], F32)
            ycb = [self.sb(s2, "ycb%d" % i, [128, 128], BF16) for i in range(2)]
            ystg = [self.sb(s2, "ycs%d" % i, [128, 512], BF16) for i in range(2)]
            accb = [self.ps(s2, "acc%d" % i, [128, 512]) for i in range(4)]
            sbk = [self.ps(s2, "sbk%d" % i, [128, 512]) for i in range(3)]
            bT = self.ps(s2, "bT", [128, 1024], BF16)
            hi = 0; sc = 0
            for (s0, L, ci, pi) in self.seqs():
                nk = L + (256 if pi < 0 else 0)
                kcol0 = 0 if pi < 0 else 256 + s0
                nkc = nk // 128
                nq = min(512, L); nqs = nq // 128
                for h in range(8):
                    kt_ = KT[hi % 2]; va_ = VAh[hi % 2]; qt_ = QT[hi % 2]; hi += 1
                    self.dma("sp", kt_[:, 0:nk], self.DKT[h, :, kcol0:kcol0 + nk], writes=[kt_])
                    for j in range(2):
                        self.dma("sp", qt_[j][64 * j:64 * j + 64, 0:L], self.DQT[h, 64 * j:64 * j + 64, s0:s0 + L], writes=[qt_[j]])
                    for k0 in range(0, nkc, 8):
                        k1 = min(nkc, k0 + 8)
                        self.dma("sp", va_[:, k0:k1, :], self.VA[kcol0 + k0 * 128:kcol0 + k1 * 128, h, :].rearrange("(kc p) e -> p kc e", p=128), writes=[va_])
                    for qb in range(L // nq):
                        units = [(kc, j) for kc in range(nkc) for j in range(2)]
                        slots = {}

                        def emit_qk(ui, units=units, slots=slots, kt_=kt_, qt_=qt_, qb=qb, nq=nq):
                            nonlocal sc
                            kc, j = units[ui]
                            sb_ = sbk[sc % 3]; pt = PT[sc % 3]; sc += 1
                            slots[ui] = pt
                            self.mm(sb_[:, 0:nq], [(kt_[:, kc * 128:(kc + 1) * 128], qt_[j][:, qb * nq:(qb + 1) * nq])], [kt_, qt_[j]], sb_)
                            self.op("act", lambda e, sb_=sb_, pt=pt: e.activation(out=pt[:, 0:nq], in_=sb_[:, 0:nq], func=AF.Exp, scale=0.125), reads=[sb_], writes=[pt])

                        def emit_av(ui, units=units, slots=slots, va_=va_, nqs=nqs, nkc=nkc):
                            kc, j = units[ui]
                            pt = slots.pop(ui)
                            for qs in range(nqs):
                                a = j * nqs + qs
                                ab = accb[a // 2]; co = (a % 2) * 256
                                first = (kc == 0 and a % 2 == 0)
                                self.op("pe", lambda e, ab=ab, co=co, pt=pt, qs=qs, kc=kc, first=first: e.matmul(
                                    ab[:, co:co + 129], lhsT=pt[:, qs * 128:(qs + 1) * 128], rhs=va_[:, kc, :], start=first, stop=(kc == nkc - 1), skip_group_check=True),
                                    reads=[pt, va_], writes=[ab], sig=(qs == nqs - 1))

                        for ui in range(0, len(units), 2):
                            emit_qk(ui)
                            emit_qk(ui + 1)
                            emit_av(ui)
                            emit_av(ui + 1)
                        sg = ystg[qb % 2]
                        for qs in range(nqs):
                            a0 = qs; a1 = nqs + qs
                            A0 = accb[a0 // 2]; c0_ = (a0 % 2) * 256; A1 = accb[a1 // 2]; c1_ = (a1 % 2) * 256
                            yc_ = ycb[qs % 2]
                            self.op("dve", lambda e, A0=A0, c0_=c0_: e.reciprocal(out=rr[:, 0:1], in_=A0[:, c0_ + 128:c0_ + 129]), reads=[A0], writes=[rr])
                            self.op("dve", lambda e, A1=A1, c1_=c1_: e.reciprocal(out=rr[:, 1:2], in_=A1[:, c1_ + 128:c1_ + 129]), reads=[A1], writes=[rr])
                            self.op("dve", lambda e: e.tensor_tensor(out=rr[:, 1:2], in0=rr[:, 1:2], in1=lam[:, 1:2], op=ALU.mult), reads=[rr, lam], writes=[rr])
                            self.op("dve", lambda e, A0=A0, c0_=c0_: e.tensor_scalar(out=o32[:], in0=A0[:, c0_:c0_ + 128], scalar1=rr[:, 0:1], scalar2=None, op0=ALU.mult), reads=[A0, rr], writes=[o32])
                            self.op("dve", lambda e, A1=A1, c1_=c1_: e.scalar_tensor_tensor(out=o32[:], in0=A1[:, c1_:c1_ + 128], scalar=rr[:, 1:2], in1=o32[:], op0=ALU.mult, op1=ALU.add), reads=[A1, rr, o32], writes=[o32])
                            self.op("act", lambda e: e.activation(out=sq[:], in_=o32[:], func=AF.Square), reads=[o32], writes=[sq])
                            self.op("dve", lambda e: e.reduce_sum(out=rr[:, 2:3], in_=sq[:], axis=mybir.AxisListType.X), reads=[sq], writes=[rr])
                            self.op("act", lambda e: e.activation(out=rr[:, 3:4], in_=rr[:, 2:3], func=AF.Sqrt, bias=self.eps6[:, 0:1], scale=1.0 / 128.0), reads=[rr, self.eps6], writes=[rr])
                            self.op("dve", lambda e: e.reciprocal(out=rr[:, 3:4], in_=rr[:, 3:4]), reads=[rr], writes=[rr])
                            self.op("dve", lambda e, yc_=yc_: e.scalar_tensor_tensor(out=yc_[:], in0=o32[:], scalar=rr[:, 3:4], in1=gnc[:], op0=ALU.mult, op1=ALU.mult), reads=[o32, rr, gnc], writes=[yc_])
                            self.op("pe", lambda e, yc_=yc_, qs=qs: e.transpose(out=bT[:, qs * 128:(qs + 1) * 128], in_=yc_[:], identity=self.ident[:]), reads=[yc_, self.ident], writes=[bT])
                        self.op("act", lambda e, sg=sg: e.activation(out=sg[:, 0:nq], in_=bT[:, 0:nq], func=AF.Copy), reads=[bT], writes=[sg])
                        q0 = s0 + qb * nq
                        self.dma("sp", self.YT[2][h * 128:(h + 1) * 128, q0:q0 + nq], sg[:, 0:nq], reads=[sg])
    self.barrier()


def mixer_out(self, l):
    I = self.I
    WB = [I["w_branch_a"][l], I["w_branch_b"][l], I["w_branch_c"][l]]
    WO = I["w_out"][l]
    with ExitStack() as st:
        yT = [self.sb(st, "yT%d" % i, [128, 8, 512], BF16) for i in range(3)]
        yacc = self.sb(st, "yacc", [128, 4, D], F32)
        wbr = [self.sb(st, "wbr%d" % i, [128, 8, 512], BF16) for i in range(3)]
        self.wstage_alloc(st)
        wo = [self.sb(st, "wo%d" % i, [128, 16, 512], BF16) for i in range(2)]
        gt = [self.sb(st, "gt%d" % i, [128, 512], F32) for i in range(3)]
        tmp = [self.sb(st, "tmp%d" % i, [128, 512], F32) for i in range(2)]
        yb = self.sb(st, "yb", [128, D], BF16)
        yT2 = self.sb(st, "yT2", [128, 16, 512], BF16)
        V = [self.sb(st, "V%d" % i, [128, D], F32) for i in range(3)]
        xt = [self.sb(st, "xt%d" % i, [128, D], F32) for i in range(2)]
        small = (self.sb(st, "stats", [128, 4, 6], F32), self.sb(st, "mv", [128, 2], F32), self.sb(st, "rstd", [128, 1], F32))
        banks = [self.ps(st, "bk%d" % i, [128, 512]) for i in range(6)]
        bbf = [self.ps(st, "bbf%d" % i, [128, 1024], BF16) for i in range(2)]
        cnt = 0
        for g in self.groups():
            ci = self.gci(g)
            tok0 = g * 512
            for br in range(3):
                self.dma("sp", yT[br][:], self.YT[br][:, tok0:tok0 + 512].rearrange("(kc p) t -> p kc t", p=128), writes=[yT[br]])
            wi = 0
            for br in range(3):
                for db in range(4):
                    w = wbr[wi % 3]; wi += 1
                    self.wload(w, lambda k0, k1, w=w: w[:, k0:k1, :], WB[br], db * 512, 512, 8, key=(l, 'wbr', br, db))
                    for tt in range(4):
                        bk = banks[cnt % 4]; g_ = gt[cnt % 3]; tm = tmp[cnt % 2]; cnt += 1
                        tok = tok0 + tt * 128
                        self.dma("sp", g_[:], self.GATES[tok:tok + 128, br * D + db * 512: br * D + (db + 1) * 512], writes=[g_])
                        self.mm(bk[:, :], [(yT[br][:, kc, tt * 128:(tt + 1) * 128], w[:, kc, :]) for kc in range(8)], [yT[br], w], bk)
                        if br == 0:
                            self.op("dve", lambda e, bk=bk, g_=g_, tt=tt, db=db: e.tensor_tensor(out=yacc[:, tt, db * 512:(db + 1) * 512], in0=g_[:], in1=bk[:], op=ALU.mult), reads=[g_, bk], writes=[yacc])
                        else:
                            self.op("dve", lambda e, bk=bk, g_=g_, tm=tm: e.tensor_tensor(out=tm[:], in0=g_[:], in1=bk[:], op=ALU.mult), reads=[g_, bk], writes=[tm])
                            self.op("dve", lambda e, tm=tm, tt=tt, db=db: e.tensor_tensor(out=yacc[:, tt, db * 512:(db + 1) * 512], in0=yacc[:, tt, db * 512:(db + 1) * 512], in1=tm[:], op=ALU.add), reads=[tm, yacc], writes=[yacc])
            for tt in range(4):
                self.op("act", lambda e, tt=tt: e.activation(out=yb[:], in_=yacc[:, tt, :], func=AF.Copy), reads=[yacc], writes=[yb])
                for half in range(2):
                    pb = bbf[half]
                    for i in range(8):
                        kc = half * 8 + i
                        self.op("pe", lambda e, pb=pb, i=i, kc=kc: e.transpose(out=pb[:, i * 128:(i + 1) * 128], in_=yb[:, kc * 128:(kc + 1) * 128], identity=self.ident[:]), reads=[yb, self.ident], writes=[pb])
                    self.op("act", lambda e, pb=pb, half=half, tt=tt: e.activation(out=yT2[:, half * 8:half * 8 + 8, tt * 128:(tt + 1) * 128], in_=pb[:, :].rearrange("p (k t) -> p k t", k=8), func=AF.Copy), reads=[pb], writes=[yT2])
            self.load_vec(V[0], ci, 5)
            self.load_ln(l, 1, V[1], V[2])
            for db in range(4):
                w = wo[db % 2]
                self.wload(w, lambda k0, k1, w=w: w[:, k0:k1, :], WO, db * 512, 512, 16, key=(l, 'wo', db))
                for tt in range(4):
                    bk = banks[4 + tt % 2]
                    self.mm(bk[:, :], [(yT2[:, kc, tt * 128:(tt + 1) * 128], w[:, kc, :]) for kc in range(16)], [yT2, w], bk)
                    self.op("act", lambda e, bk=bk, tt=tt, db=db: e.activation(out=yacc[:, tt, db * 512:(db + 1) * 512], in_=bk[:], func=AF.Copy), reads=[bk], writes=[yacc])
            for tt in range(4):
                tok = tok0 + tt * 128
                self.epilogue(l, 1, tok, lambda db, tt=tt: yacc[:, tt, db * 512:(db + 1) * 512], [yacc] * 4, xt[tt % 2], _V(yacc, tt), V[0], V[1], V[2], False, small)
    self.barrier()


def phase_mixer(self, l):
    self.mixer_alloc()
    sub = self.cfg.get("mix")
    def on(p):
        return sub is None or p in sub
    if on("in"):
        self.mixer_in(l)
    if on("hyf"):
        for L in sorted({s[1] for s in self.seqs()}):
            self.hy_filters(l, L)
    if on("hyp"):
        self.hy_prep(l)
    if on("hyc"):
        self.hy_conv(l)
    if on("gla"):
        self.gla(l)
    if on("diff"):
        self.diff(l)
    if on("out"):
        self.mixer_out(l)


for _f in (mixer_alloc, mixer_in, range_reduce, hy_filters, hy_prep, hy_conv, gla, diff, mixer_out, phase_mixer):
    setattr(MK, _f.__name__, _f)
MK.seqs = _seqs


def _shapes():
    sh = {
        "xs": ((TS, D), F32), "xp": ((TP, D), F32), "cvec": ((2, D), F32),
        "ck": ((DEPTH, 256, 1024), F32), "cv": ((DEPTH, 256, 1024), F32), "sg": ((DEPTH, 2, 4, 128, 256), F32),
        "w_mod": ((DEPTH, D, 9 * D), F32), "b_mod": ((DEPTH, 9 * D), F32), "ln_g": ((DEPTH, 3, D), F32), "ln_b": ((DEPTH, 3, D), F32),
        "ffn_w1": ((DEPTH, 2, D, FF), F32), "ffn_w3": ((DEPTH, 2, D, FF), F32), "ffn_w2": ((DEPTH, 2, FF, D), F32),
        "w_in": ((DEPTH, D, NCOL), F32), "hy_conv_w": ((DEPTH, 3, 3072), F32), "hy_conv_b": ((DEPTH, 3072), F32),
        "hy_w1": ((DEPTH, 17, 64), F32), "hy_b1": ((DEPTH, 64), F32), "hy_freq": ((DEPTH, 2, 64), F32),
        "hy_w2": ((DEPTH, 64, 64), F32), "hy_b2": ((DEPTH, 64), F32), "hy_w3": ((DEPTH, 64, 4096), F32),
        "hy_decay": ((DEPTH, 4096), F32), "hy_bias": ((DEPTH, 2, 1024), F32), "gla_wa": ((DEPTH, 2, 16, 512), F32),
        "gla_ba": ((DEPTH, 2, 512), F32), "gla_norm_g": ((DEPTH, 256), F32), "diff_lam": ((DEPTH, 4, 64), F32),
        "diff_norm_g": ((DEPTH, 128), F32), "w_branch_a": ((DEPTH, 1024, D), F32), "w_branch_b": ((DEPTH, 1024, D), F32),
        "w_branch_c": ((DEPTH, 1024, D), F32), "w_out": ((DEPTH, D, D), F32),
    }
    for k, v in _consts().items():
        sh[k] = (v.shape, BF16 if v.dtype == ml_dtypes.bfloat16 else F32)
    return sh


def build(cfg=None):
    cfg = cfg or {}
    nc = bass.Bass("TRN2", target_bir_lowering=False)
    k = MK(nc, _shapes(), cfg)
    k.out("ys", [TS, D])
    k.out("yp", [TP, D])
    k.out("nck", [2, DEPTH, 256, 1024])
    k.out("ncv", [2, DEPTH, 256, 1024])
    k.out("nsg", [2, DEPTH, 2, 4, 128, 256])
    phases = cfg.get("phases")
    with nc.allow_non_contiguous_dma(reason="small strided parameter loads"):
        with ExitStack() as st:
            k.phase_init(st)
            nl = cfg.get("depth", DEPTH)
            for l in range(nl):
                k.phase_mod(l)
            for l in range(nl):
                k.phase_convert(l)
            for l in range(nl):
                phases = cfg.get("phases%d" % l, cfg.get("phases"))
                k.MODB = k.MODBS[l]

                def on(p, phases=phases):
                    return phases is None or p in phases
                if on("ffn1"):
                    k.phase_ffn(l, 0)
                if on("mixer"):
                    k.phase_mixer(l)
                if on("ffn2"):
                    k.phase_ffn(l, 1)
            k.barrier(full=True)
    k.es.close()
    return nc, k


def _in_maps(inputs):
    c = _consts()
    maps = []
    f = lambda a: np.ascontiguousarray(np.asarray(a, dtype=np.float32))
    shared = {n: f(inputs[n]) for n in WNAMES}
    for b in range(8):
        m = dict(shared)
        m.update(c)
        m["xs"] = f(inputs["x_sample"][b])
        m["xp"] = f(inputs["x_prompt"][2 * b:2 * b + 2]).reshape(TP, D)
        m["cvec"] = f(np.stack([np.asarray(inputs["c"][b]), np.asarray(inputs["c_ctx"])]))
        m["ck"] = f(inputs["cache_k"][b]).reshape(DEPTH, 256, 1024)
        m["cv"] = f(inputs["cache_v"][b]).reshape(DEPTH, 256, 1024)
        m["sg"] = f(inputs["state_gla"][b])
        maps.append(m)
    return maps


def kernel(**inputs):
    nc, k = build()
    res = run_bass_kernel_spmd(nc, _in_maps(inputs), core_ids=list(range(8)))
    r = res.results
    ys = np.stack([r[b]["ys"] for b in range(8)])
    yp = np.concatenate([r[b]["yp"].reshape(2, 256, D) for b in range(8)])
    nck = np.concatenate([r[b]["nck"].reshape(2, DEPTH, 256, 8, 2, 64) for b in range(8)])
    ncv = np.concatenate([r[b]["ncv"].reshape(2, DEPTH, 256, 8, 128) for b in range(8)])
    nsg = np.concatenate([r[b]["nsg"] for b in range(8)])
    return (yp, ys, nck, ncv, nsg)
```

```python
import math
from contextlib import ExitStack
import numpy as np
import ml_dtypes
import concourse.bass as bass
import concourse.mybir as mybir
from concourse.bass_utils import run_bass_kernel_spmd

F32 = mybir.dt.float32
BF16 = mybir.dt.bfloat16
AF = mybir.ActivationFunctionType
ALU = mybir.AluOpType

D = 2048
FF = 5504
DEPTH = 2
NCOL = 15392
TS = 4096
TP = 512
T = TS + TP
ALPHA = (2 * DEPTH) ** 0.25
MAGIC = 12582912.0


class Buf:
    __slots__ = ("w", "r", "name")

    def __init__(self, name=""):
        self.w = {}
        self.r = {}
        self.name = name


class TT:
    __slots__ = ("t", "b")

    def __init__(self, t, b):
        self.t = t
        self.b = b

    def __getitem__(self, k):
        return self.t[k]


class KB:
    NDSEM = 44
    QSEM = {"sp": (0, 24), "pool": (24, 36), "cv": (36, 44)}

    def __init__(self, nc):
        self.nc = nc
        self.es = ExitStack()
        self.eng = {"pe": nc.tensor, "act": nc.scalar, "dve": nc.vector, "pool": nc.gpsimd, "sp": nc.sync}
        self.tick = {e: 0 for e in ("pe", "act", "dve", "pool")}
        self.sem = {e: self.es.enter_context(nc.semaphore("tick_" + e)) for e in self.tick}
        self.dsem = [self.es.enter_context(nc.semaphore("dma%d" % i)) for i in range(self.NDSEM)]
        self.dcnt = [0] * self.NDSEM
        self.qnext = {}
        self.waited = {}
        self.ninst = 0
        self.uid = 0
        self.wsi = 0
        self.wc = {}
        self.store_q = None

    def sb(self, stack, name, shape, dt):
        self.uid += 1
        t = stack.enter_context(self.nc.sbuf_tensor("%s_%d" % (name, self.uid), list(shape), dt))
        return TT(t, Buf(name))

    def ps(self, stack, name, shape, dt=F32):
        self.uid += 1
        t = stack.enter_context(self.nc.psum_tensor("%s_%d" % (name, self.uid), list(shape), dt))
        return TT(t, Buf(name))

    def _semh(self, key):
        return self.sem[key] if isinstance(key, str) else self.dsem[key[1]]

    def _wait(self, engine, deps):
        e = self.eng[engine]
        for key, val in deps.items():
            wk = (engine, key)
            if self.waited.get(wk, 0) >= val:
                continue
            self.waited[wk] = val
            e.wait_ge(self._semh(key), val)
            self.ninst += 1

    @staticmethod
    def _merge(d, s):
        for k, v in s.items():
            if d.get(k, 0) < v:
                d[k] = v

    def _deps(self, reads, writes):
        deps = {}
        for b in reads:
            self._merge(deps, b.w)
        for b in writes:
            self._merge(deps, b.w)
            self._merge(deps, b.r)
        return deps

    @staticmethod
    def _bufs(xs):
        return [getattr(x, "b", x) for x in xs if x is not None]

    def op(self, engine, fn, reads=(), writes=(), sig=True):
        reads = self._bufs(reads)
        writes = self._bufs(writes)
        deps = self._deps(reads, writes)
        if engine == "pe":
            deps.pop("pe", None)
        self._wait(engine, deps)
        ins = fn(self.eng[engine])
        self.ninst += 1
        if sig:
            self.tick[engine] += 1
            ins.then_inc(self.sem[engine], 1)
            tok = {engine: self.tick[engine]}
        else:
            tok = {engine: self.tick[engine] + 1}
        for b in reads:
            self._merge(b.r, tok)
        for b in writes:
            self._merge(b.w, tok)
            b.r = {}
        return ins

    def dma(self, queue, out, in_, reads=(), writes=()):
        reads = self._bufs(reads)
        writes = self._bufs(writes)
        if queue == "sp" and reads and not writes and self.store_q:
            queue = self.store_q
        deps = self._deps(reads, writes)
        lo, hi = self.QSEM[queue]
        s = self.qnext.get(queue, lo)
        self.qnext[queue] = lo + (s + 1 - lo) % (hi - lo)
        if self.dcnt[s]:
            self._merge(deps, {("d", s): 16 * self.dcnt[s]})
        engine = "pool" if queue == "cv" else queue
        self._wait(engine, deps)
        ins = self.eng[engine].dma_start(out=out, in_=in_)
        self.ninst += 1
        self.dcnt[s] += 1
        ins.then_inc(self.dsem[s], 16)
        tok = {("d", s): 16 * self.dcnt[s]}
        for b in reads:
            self._merge(b.r, tok)
        for b in writes:
            self._merge(b.w, tok)
            b.r = {}
        return tok

    def barrier(self, full=False):
        deps = {e: v for e, v in self.tick.items() if v}
        cvlo, cvhi = self.QSEM["cv"]
        for i, c in enumerate(self.dcnt):
            if c and (full or not (cvlo <= i < cvhi)):
                deps[("d", i)] = 16 * c
        for e in ("sp", "pe", "act", "dve", "pool"):
            self._wait(e, dict(deps))


_CONST_CACHE = {}


def _bf(a):
    return np.ascontiguousarray(a.astype(ml_dtypes.bfloat16))


def _dft_consts(L):
    N = 2 * L
    t = np.arange(L, dtype=np.int64)
    f = np.arange(L, dtype=np.int64)
    ang = (np.outer(t, f) % N).astype(np.float64) * (2.0 * math.pi / N)
    C = np.cos(ang)
    S = -np.sin(ang)
    sgn = np.where(t % 2 == 0, 1.0, -1.0)
    Ffwd = np.concatenate([C, S], axis=1)
    Ffwd[:, L] = sgn
    Gre = (2.0 / N) * C.T
    Gre[0, :] = 1.0 / N
    Gim = (2.0 / N) * S.T
    Gim[0, :] = sgn / N
    Ginv = np.concatenate([Gre, Gim], axis=0)
    nt = L // 128
    Ft = Ffwd.reshape(nt, 128, 2 * nt, 128).transpose(2, 1, 0, 3)
    Gt = Ginv.reshape(2 * nt, 128, nt, 128).transpose(2, 1, 0, 3)
    nyq = sgn.reshape(nt, 128).T
    return _bf(Ft), _bf(Gt), _bf(nyq)


def _feats(L):
    t = np.linspace(0.0, 1.0, L, dtype=np.float32)
    w = (np.float32(2.0 * math.pi / L) * np.arange(L, dtype=np.float32))
    f = np.linspace(1e-4, 7, 8, dtype=np.float32)
    feats = np.concatenate([t[:, None], np.cos(w[:, None] * f), -np.sin(w[:, None] * f)], -1).astype(np.float32)
    featsT = np.ascontiguousarray(feats.T)
    ntl = np.ascontiguousarray((-t).reshape(L // 128, 128).T)
    return featsT, ntl.astype(np.float32)


def _rope_tables():
    L = TS
    rows = L // 64
    r = np.repeat(np.arange(rows, dtype=np.float32), 64)
    col = np.tile(np.arange(64, dtype=np.float32), rows)
    nf = 16
    inv = (10000.0 ** (-np.arange(nf, dtype=np.float32) / nf)).astype(np.float32)
    ang = np.stack([r[:, None] * inv, col[:, None] * inv], axis=1)
    cos = np.cos(ang).astype(np.float32)
    sin = np.sin(ang).astype(np.float32)
    c64 = np.stack([cos, cos], axis=2).reshape(L, 64)
    s64 = np.stack([-sin, sin], axis=2).reshape(L, 64)
    return np.ascontiguousarray(np.tile(c64, (1, 16))), np.ascontiguousarray(np.tile(s64, (1, 16)))


def _gla_mats():
    j = np.arange(128)[:, None]
    i = np.arange(128)[None, :]
    s = -1.0 / 16.0
    m = np.zeros((6, 128, 128), np.float32)
    m[0] = s * ((j <= i).astype(np.float32) - (j <= 63).astype(np.float32))
    m[1] = s * (j <= i)
    m[2] = s * (j > i)
    m[3] = s * ((j >= i).astype(np.float32) - (j >= 64).astype(np.float32))
    m[4] = s * (j >= i)
    m[5] = s * (j < i)
    masks = np.zeros((2, 128, 128), np.float32)
    masks[0] = (j <= i)
    masks[1] = (j >= i)
    return m, masks


def _consts():
    if _CONST_CACHE:
        return _CONST_CACHE
    c = _CONST_CACHE
    c["c_f4096"], c["c_g4096"], c["c_nyq4096"] = _dft_consts(4096)
    c["c_f256"], c["c_g256"], c["c_nyq256"] = _dft_consts(256)
    c["c_featsT4096"], c["c_ntl4096"] = _feats(4096)
    c["c_featsT256"], c["c_ntl256"] = _feats(256)
    c["c_cosf"], c["c_sinf"] = _rope_tables()
    c["c_glam"], c["c_gmask"] = _gla_mats()
    c["c_ident"] = _bf(np.eye(128, dtype=np.float32))
    return c


WNAMES = ["w_mod", "b_mod", "ln_g", "ln_b", "ffn_w1", "ffn_w3", "ffn_w2", "w_in", "hy_conv_w", "hy_conv_b",
          "hy_w1", "hy_b1", "hy_freq", "hy_w2", "hy_b2", "hy_w3", "hy_decay", "hy_bias", "gla_wa", "gla_ba",
          "gla_norm_g", "diff_lam", "diff_norm_g", "w_branch_a", "w_branch_b", "w_branch_c", "w_out"]


class MK(KB):
    def __init__(self, nc, shapes, cfg):
        super().__init__(nc)
        self.cfg = cfg
        self.I = {}
        for name, (shape, dt) in shapes.items():
            self.I[name] = nc.dram_tensor(name, list(shape), dt, kind="ExternalInput").ap()
        self.O = {}
        self.dbg = cfg.get("dbg", ())

    def out(self, name, shape, dt=F32):
        self.O[name] = self.nc.dram_tensor(name, list(shape), dt, kind="ExternalOutput").ap()
        return self.O[name]

    def scratch(self, name, shape, dt=F32):
        if name in self.dbg:
            return self.out(name, shape, dt)
        return self.nc.dram_tensor(name, list(shape), dt).ap()

    def mm(self, psum_ap, pairs, reads, pw):
        n = len(pairs)
        for i, (l, r) in enumerate(pairs):
            self.op("pe", lambda e, l=l, r=r, i=i: e.matmul(psum_ap, lhsT=l, rhs=r, start=(i == 0), stop=(i == n - 1)),
                    reads=reads, writes=[pw], sig=(i == n - 1))

    def wstage_alloc(self, st, n=3):
        pass

    def wload_cast(self, dst, dst_ap_fn, src2d, c0, ncols, nk, rows0=0):
        for k0 in range(0, nk, 8):
            k1 = min(nk, k0 + 8)
            src = src2d[rows0 + k0 * 128: rows0 + k1 * 128, c0:c0 + ncols].rearrange("(kc p) n -> p kc n", p=128)
            self.dma("pool", dst_ap_fn(k0, k1), src, writes=[dst])

    def wconv(self, key, src2d, c0, ncols, nk, rows0=0):
        self.uid += 1
        t = self.nc.dram_tensor("wc_%d" % self.uid, [128, nk, ncols], BF16).ap()
        b = Buf("wc")
        for k0 in range(0, nk, 8):
            k1 = min(nk, k0 + 8)
            src = src2d[rows0 + k0 * 128: rows0 + k1 * 128, c0:c0 + ncols].rearrange("(kc p) n -> p kc n", p=128)
            self.dma("cv", t[:, k0:k1, :], src, writes=[b])
        self.wc[key] = (t, b)

    def wload(self, dst, dst_ap_fn, src2d, c0, ncols, nk, rows0=0, key=None):
        t, b = self.wc[key]
        self.dma("sp", dst_ap_fn(0, nk), t[:, :, :], reads=[b], writes=[dst])

    def groups(self):
        return self.cfg.get("groups", list(range(9)))

    def gci(self, g):
        return 0 if g < 8 else 1

    def phase_init(self, st):
        nc = self.nc
        self.ident = self.sb(st, "ident", [128, 128], BF16)
        self.dma("sp", self.ident[:], self.I["c_ident"][:, :], writes=[self.ident])
        self.ones = self.sb(st, "ones", [128, 128], F32)
        self.op("dve", lambda e: e.memset(self.ones[:], 1.0), writes=[self.ones])
        self.eps5 = self.sb(st, "eps5", [128, 1], F32)
        self.op("dve", lambda e: e.memset(self.eps5[:], 1e-5), writes=[self.eps5])
        self.eps6 = self.sb(st, "eps6", [128, 1], F32)
        self.op("dve", lambda e: e.memset(self.eps6[:], 1e-6), writes=[self.eps6])
        self.X = self.scratch("X", [T, D])
        self.MODBS = [self.scratch("MODB" if i == 0 else "MODB%d" % i, [2, 128, 9 * D]) for i in range(DEPTH)]
        self.MODB = self.MODBS[0]
        for i in range(0, TS, 1024):
            self.dma("sp", self.X[i:i + 1024, :], self.I["xs"][i:i + 1024, :])
        self.dma("sp", self.X[TS:T, :], self.I["xp"][:, :])
        self.barrier()


    def phase_convert(self, l):
        I = self.I
        fblocks = [(i * 256, 256) for i in range(21)] + [(21 * 256, 128)]

        def ffn(fi):
            for (c0, nc_) in fblocks:
                self.wconv((l, 'w1', fi, c0), I["ffn_w1"][l, fi], c0, nc_, 16)
                self.wconv((l, 'w3', fi, c0), I["ffn_w3"][l, fi], c0, nc_, 16)
            for db in range(4):
                for ch in range(11):
                    f0 = ch * 4
                    nf = min(4, 43 - f0)
                    self.wconv((l, 'w2', fi, db, f0), I["ffn_w2"][l, fi], db * 512, 512, nf, rows0=f0 * 128)
        ffn(0)
        W = I["w_in"][l]
        for c0 in list(range(0, 4096, 512)):
            self.wconv((l, 'win', c0), W, c0, 512, 16)
        self.wconv((l, 'win', 6144), W, 6144, 32, 16)
        for c0 in [4096 + i * 512 for i in range(4)] + [6176 + i * 512 for i in range(6)] + [9248 + i * 512 for i in range(12)]:
            self.wconv((l, 'win', c0), W, c0, 512, 16)
        for br, n in enumerate(("w_branch_a", "w_branch_b", "w_branch_c")):
            for db in range(4):
                self.wconv((l, 'wbr', br, db), I[n][l], db * 512, 512, 8)
        for db in range(4):
            self.wconv((l, 'wo', db), I["w_out"][l], db * 512, 512, 16)
        ffn(1)

    def phase_mod(self, l):
        I = self.I
        self.MODB = self.MODBS[l]
        with ExitStack() as st:
            cv = self.sb(st, "cv", [128, 2, 16], F32)
            sc = self.sb(st, "sc", [128, 2, 16], F32)
            lhs = self.sb(st, "modlhs", [128, 2, 16, 128], BF16)
            self.dma("sp", cv[:], I["cvec"].rearrange("c (kc p) -> p c kc", p=128), writes=[cv])
            self.op("act", lambda e: e.activation(out=sc[:], in_=cv[:], func=AF.Silu), reads=[cv], writes=[sc])
            for ci in range(2):
                for kc in range(16):
                    self.op("dve", lambda e, ci=ci, kc=kc: e.tensor_scalar(
                        out=lhs[:, ci, kc, :], in0=self.ones[:, :], scalar1=sc[:, ci, kc:kc + 1], scalar2=None, op0=ALU.mult),
                        reads=[self.ones, sc], writes=[lhs])
            self.wstage_alloc(st)
            wb = [self.sb(st, "modw%d" % i, [128, 16, 512], BF16) for i in range(3)]
            bb = [self.sb(st, "modb%d" % i, [128, 512], F32) for i in range(2)]
            res = [self.sb(st, "modr%d" % i, [128, 512], F32) for i in range(4)]
            banks = [self.ps(st, "modp%d" % i, [128, 512]) for i in range(4)]
            wsrc = I["w_mod"][l]
            for nb in range(36):
                w = wb[nb % 3]
                b = bb[nb % 2]
                self.wload_cast(w, lambda k0, k1, w=w: w[:, k0:k1, :], wsrc, nb * 512, 512, 16)
                self.dma("sp", b[:], I["b_mod"][l, nb * 512:(nb + 1) * 512].partition_broadcast(128), writes=[b])
                for ci in range(2):
                    bk = banks[(nb * 2 + ci) % 4]
                    r = res[(nb * 2 + ci) % 4]
                    self.mm(bk[:, :], [(lhs[:, ci, kc, :], w[:, kc, :]) for kc in range(16)], [lhs, w], bk)
                    self.op("dve", lambda e, bk=bk, r=r, b=b: e.tensor_tensor(out=r[:], in0=bk[:], in1=b[:], op=ALU.add),
                            reads=[bk, b], writes=[r])
                    self.dma("sp", self.MODB[ci, :, nb * 512:(nb + 1) * 512], r[:], reads=[r])
        self.barrier()

    def load_vec(self, dst, ci, idx):
        self.dma("sp", dst[:], self.MODB[ci, :, idx * D:(idx + 1) * D], writes=[dst])

    def prologue(self, g, xt, hb, hT, V0, V1, banks_bf):
        for tt in range(4):
            tok = g * 512 + tt * 128
            x = xt[tt % 2]
            self.dma("sp", x[:], self.X[tok:tok + 128, :], writes=[x])
            self.op("dve", lambda e, x=x: e.tensor_tensor(out=x[:], in0=x[:], in1=V0[:], op=ALU.mult), reads=[x, V0], writes=[x])
            self.op("dve", lambda e, x=x: e.tensor_tensor(out=hb[:], in0=x[:], in1=V1[:], op=ALU.add), reads=[x, V1], writes=[hb])
            for half in range(2):
                pb = banks_bf[half]
                for i in range(8):
                    kc = half * 8 + i
                    self.op("pe", lambda e, pb=pb, i=i, kc=kc: e.transpose(
                        out=pb[:, i * 128:(i + 1) * 128], in_=hb[:, kc * 128:(kc + 1) * 128], identity=self.ident[:]),
                        reads=[hb, self.ident], writes=[pb])
                self.op("act", lambda e, pb=pb, half=half, tt=tt: e.activation(
                    out=hT[:, half * 8:half * 8 + 8, tt * 128:(tt + 1) * 128],
                    in_=pb[:, :].rearrange("p (k t) -> p k t", k=8), func=AF.Copy),
                    reads=[pb], writes=[hT])

    def epilogue(self, l, i, tok, osrc, osrc_bufs, xt, tbuf, V0, V1, V2, half_gate, small):
        stats, mv, rstd = small
        self.dma("sp", xt[:], self.X[tok:tok + 128, :], writes=[xt])
        for db in range(4):
            sl = slice(db * 512, (db + 1) * 512)
            self.op("dve", lambda e, db=db, sl=sl: e.scalar_tensor_tensor(
                out=tbuf[:, sl], in0=osrc(db), scalar=(0.5 if half_gate else 1.0), in1=V0[:, sl], op0=ALU.mult, op1=ALU.mult),
                reads=[osrc_bufs[db], V0], writes=[tbuf])
        self.op("dve", lambda e: e.scalar_tensor_tensor(out=tbuf[:], in0=xt[:], scalar=ALPHA, in1=tbuf[:], op0=ALU.mult, op1=ALU.add),
                reads=[xt, tbuf], writes=[tbuf])
        for c in range(4):
            self.op("dve", lambda e, c=c: e.bn_stats(out=stats[:, c, :], in_=tbuf[:, c * 512:(c + 1) * 512]), reads=[tbuf], writes=[stats])
        self.op("dve", lambda e: e.bn_aggr(out=mv[:], in_=stats[:]), reads=[stats], writes=[mv])
        self.op("act", lambda e: e.activation(out=rstd[:], in_=mv[:, 1:2], func=AF.Sqrt, bias=self.eps5[:, 0:1], scale=1.0),
                reads=[mv, self.eps5], writes=[rstd])
        self.op("dve", lambda e: e.reciprocal(out=rstd[:], in_=rstd[:]), reads=[rstd], writes=[rstd])
        self.op("dve", lambda e: e.tensor_scalar(out=tbuf[:], in0=tbuf[:], scalar1=mv[:, 0:1], scalar2=rstd[:, 0:1],
                                                 op0=ALU.subtract, op1=ALU.mult), reads=[tbuf, mv, rstd], writes=[tbuf])
        self.op("dve", lambda e: e.tensor_tensor(out=tbuf[:], in0=tbuf[:], in1=V1[:], op=ALU.mult), reads=[tbuf, V1], writes=[tbuf])
        self.op("dve", lambda e: e.tensor_tensor(out=xt[:], in0=tbuf[:], in1=V2[:], op=ALU.add), reads=[tbuf, V2], writes=[xt])
        self.dma("sp", self.X[tok:tok + 128, :], xt[:], reads=[xt])
        if l == DEPTH - 1 and i == 2:
            if tok < TS:
                self.dma("sp", self.O["ys"][tok:tok + 128, :], xt[:], reads=[xt])
            else:
                self.dma("sp", self.O["yp"][tok - TS:tok - TS + 128, :], xt[:], reads=[xt])

    def load_ln(self, l, i, V1, V2):
        self.dma("sp", V1[:], self.I["ln_g"][l, i, :].partition_broadcast(128), writes=[V1])
        self.dma("sp", V2[:], self.I["ln_b"][l, i, :].partition_broadcast(128), writes=[V2])

    def phase_ffn(self, l, fi):
        I = self.I
        sub = 0 if fi == 0 else 2
        w1 = I["ffn_w1"][l, fi]
        w3 = I["ffn_w3"][l, fi]
        w2 = I["ffn_w2"][l, fi]
        fblocks = [(i * 256, 256) for i in range(21)] + [(21 * 256, 128)]
        with ExitStack() as st:
            hT = self.sb(st, "hT", [128, 16, 512], BF16)
            gT = self.sb(st, "gT", [128, 43, 512], BF16)
            wA = [self.sb(st, "w1b%d" % i, [128, 16, 256], BF16) for i in range(3)]
            wB = [self.sb(st, "w3b%d" % i, [128, 16, 256], BF16) for i in range(3)]
            wC = [self.sb(st, "w2b%d" % i, [128, 4, 512], BF16) for i in range(4)]
            self.wstage_alloc(st)
            V = [self.sb(st, "V%d" % i, [128, D], F32) for i in range(3)]
            xt = [self.sb(st, "xt%d" % i, [128, D], F32) for i in range(2)]
            hb = self.sb(st, "hb", [128, D], BF16)
            tmp = [self.sb(st, "tmp%d" % i, [128, 512], F32) for i in range(2)]
            tb = self.sb(st, "tb", [128, 4, D], F32)
            small = (self.sb(st, "stats", [128, 4, 6], F32), self.sb(st, "mv", [128, 2], F32), self.sb(st, "rstd", [128, 1], F32))
            banks = [self.ps(st, "bk%d" % i, [128, 512]) for i in range(6)]
            bbf = [self.ps(st, "bbf%d" % i, [128, 1024], BF16) for i in range(2)]
            for g in self.groups():
                ci = self.gci(g)
                self.load_vec(V[0], ci, 3 * sub + 1)
                self.load_vec(V[1], ci, 3 * sub + 0)
                self.op("dve", lambda e: e.tensor_scalar(out=V[0][:], in0=V[0][:], scalar1=1.0, scalar2=None, op0=ALU.add),
                        reads=[V[0]], writes=[V[0]])
                self.prologue(g, xt, hb, hT, V[0], V[1], bbf)
                fidx = 0
                for bi, (c0, nc_) in enumerate(fblocks):
                    a = wA[bi % 3]
                    b = wB[bi % 3]
                    self.wload(a, lambda k0, k1, a=a, nc_=nc_: a[:, k0:k1, 0:nc_], w1, c0, nc_, 16, key=(l, 'w1', fi, c0))
                    self.wload(b, lambda k0, k1, b=b, nc_=nc_: b[:, k0:k1, 0:nc_], w3, c0, nc_, 16, key=(l, 'w3', fi, c0))
                    for ft in range(nc_ // 128):
                        pa = banks[(fidx % 2) * 2]
                        pb = banks[(fidx % 2) * 2 + 1]
                        tm = tmp[fidx % 2]
                        self.mm(pa[:, :], [(a[:, kc, ft * 128:(ft + 1) * 128], hT[:, kc, :]) for kc in range(16)], [a, hT], pa)
                        self.mm(pb[:, :], [(b[:, kc, ft * 128:(ft + 1) * 128], hT[:, kc, :]) for kc in range(16)], [b, hT], pb)
                        self.op("act", lambda e, pa=pa, tm=tm: e.activation(out=tm[:], in_=pa[:], func=AF.Silu), reads=[pa], writes=[tm])
                        self.op("dve", lambda e, pb=pb, tm=tm, fidx=fidx: e.tensor_tensor(out=gT[:, fidx, :], in0=tm[:], in1=pb[:], op=ALU.mult),
                                reads=[tm, pb], writes=[gT])
                        fidx += 1
                self.load_vec(V[0], ci, 3 * sub + 2)
                self.load_ln(l, sub, V[1], V[2])
                nch = 11
                wi = 0
                for db in range(4):
                    for ch in range(nch):
                        f0 = ch * 4
                        nf = min(4, 43 - f0)
                        w = wC[wi % 4]
                        wi += 1
                        self.wload(w, lambda k0, k1, w=w: w[:, k0:k1, :], w2, db * 512, 512, nf, rows0=f0 * 128, key=(l, 'w2', fi, db, f0))
                        for tt in range(4):
                            pk = banks[2 + tt]
                            for fi_ in range(nf):
                                fc = f0 + fi_
                                self.op("pe", lambda e, pk=pk, w=w, fi_=fi_, fc=fc, tt=tt: e.matmul(
                                    pk[:, :], lhsT=gT[:, fc, tt * 128:(tt + 1) * 128], rhs=w[:, fi_, :], start=(fc == 0), stop=(fc == 42)),
                                    reads=[gT, w], writes=[pk], sig=(fi_ == nf - 1))
                    for tt in range(4):
                        pk = banks[2 + tt]
                        self.op("act", lambda e, pk=pk, tt=tt, db=db: e.activation(out=tb[:, tt, db * 512:(db + 1) * 512], in_=pk[:], func=AF.Copy),
                                reads=[pk], writes=[tb])
                for tt in range(4):
                    tok = g * 512 + tt * 128
                    x = xt[tt % 2]
                    self.epilogue(l, sub, tok, lambda db, tt=tt: tb[:, tt, db * 512:(db + 1) * 512], [tb] * 4, x, _V(tb, tt), V[0], V[1], V[2], True, small)
        self.barrier()


class _V:
    def __init__(self, t, tt):
        self.t = t
        self.tt = tt
        self.b = t.b

    def __getitem__(self, k):
        if isinstance(k, tuple):
            return self.t.t[(k[0], self.tt) + tuple(k[1:])]
        return self.t.t[k, self.tt]


SEQS = [(0, TS, 0, -1), (TS, 256, 1, 0), (TS + 256, 256, 1, 1)]


def _seqs(self):
    return [s for s in SEQS if (s[0] // 512) in self.groups()]


def mixer_alloc(self):
    if hasattr(self, "ZHT"):
        return
    S = self.scratch
    self.ZHT = S("ZHT", [3072, T]); self.GQT = S("GQT", [512, T]); self.GKT = S("GKT", [512, T])
    self.GK = S("GK", [T, 512]); self.GV = S("GV", [T, 1024], BF16); self.GR = S("GR", [T, 1024])
    self.GLRT = S("GLRT", [2, 16, T]); self.DQ = S("DQ", [T, 1024]); self.DK = S("DK", [T, 1024]); self.DV = S("DV", [T, 1024])
    self.GATES = S("GATES", [T, 6144])
    self.HTOK = [S("HTOK%d" % i, [T, 1024], BF16) for i in range(3)]
    self.KF = {4096: S("KF4096", [2, 8192, 1024]), 256: S("KF256", [2, 512, 1024])}
    self.YT = [S("Y%sT" % n, [1024, T], BF16) for n in "ABC"]
    self.OF = S("OF", [T, 1024])
    self.DQT = S("DQT", [8, 128, T], BF16); self.DKT = S("DKT", [8, 128, 256 + T], BF16); self.VA = S("VA", [256 + T, 8, 129], BF16)


def mixer_in(self, l):
    I = self.I
    W = I["w_in"][l]
    feat = [(c0, 512, self.ZHT, c0) for c0 in range(0, 3072, 512)] + [(3072, 512, self.GQT, 0), (3584, 512, self.GKT, 0)]
    tokb = [(3584, self.GK, 0, AF.Copy, F32, None)]
    tokb += [(4096 + i * 512, self.GV, i * 512, AF.Copy, BF16, None) for i in range(2)]
    tokb += [(5120 + i * 512, self.GR, i * 512, AF.Copy, F32, None) for i in range(2)]
    tokb += [(6176 + i * 512, self.DQ, i * 512, AF.Copy, F32, None) for i in range(2)]
    tokb += [(7200 + i * 512, self.DK, i * 512, AF.Copy, F32, "nck") for i in range(2)]
    tokb += [(8224 + i * 512, self.DV, i * 512, AF.Copy, F32, "ncv") for i in range(2)]
    tokb += [(9248 + i * 512, self.GATES, i * 512, AF.Sigmoid, F32, None) for i in range(12)]
    with ExitStack() as st:
        hT = self.sb(st, "hT", [128, 16, 512], BF16)
        wb = [self.sb(st, "wi%d" % i, [128, 16, 512], BF16) for i in range(3)]
        self.wstage_alloc(st)
        V = [self.sb(st, "V%d" % i, [128, D], F32) for i in range(2)]
        xt = [self.sb(st, "xt%d" % i, [128, D], F32) for i in range(2)]
        hb = self.sb(st, "hb", [128, D], BF16)
        stf = [self.sb(st, "stf%d" % i, [128, 512], F32) for i in range(4)]
        stb = [self.sb(st, "stb%d" % i, [128, 512], BF16) for i in range(2)]
        banks = [self.ps(st, "bk%d" % i, [128, 512]) for i in range(4)]
        bbf = [self.ps(st, "bbf%d" % i, [128, 1024], BF16) for i in range(2)]
        cnt = 0
        for g in self.groups():
            ci = self.gci(g)
            tok0 = g * 512
            self.load_vec(V[0], ci, 4)
            self.load_vec(V[1], ci, 3)
            self.op("dve", lambda e: e.tensor_scalar(out=V[0][:], in0=V[0][:], scalar1=1.0, scalar2=None, op0=ALU.add), reads=[V[0]], writes=[V[0]])
            self.prologue(g, xt, hb, hT, V[0], V[1], bbf)
            wi = 0
            for (c0, ncl, dst, r0) in feat:
                w = wb[wi % 3]; wi += 1
                self.wload(w, lambda k0, k1, w=w: w[:, k0:k1, :], W, c0, 512, 16, key=(l, 'win', c0))
                for ct in range(4):
                    bk = banks[cnt % 4]; sg = stf[cnt % 4]; cnt += 1
                    self.mm(bk[:, :], [(w[:, kc, ct * 128:(ct + 1) * 128], hT[:, kc, :]) for kc in range(16)], [w, hT], bk)
                    eng = "act" if cnt % 2 else "dve"
                    if eng == "act":
                        self.op("act", lambda e, bk=bk, sg=sg: e.activation(out=sg[:], in_=bk[:], func=AF.Copy), reads=[bk], writes=[sg])
                    else:
                        self.op("dve", lambda e, bk=bk, sg=sg: e.tensor_copy(out=sg[:], in_=bk[:]), reads=[bk], writes=[sg])
                    self.dma("sp", dst[r0 + ct * 128:r0 + (ct + 1) * 128, tok0:tok0 + 512], sg[:], reads=[sg])
            w = wb[wi % 3]; wi += 1
            self.wload(w, lambda k0, k1, w=w: w[:, k0:k1, 0:32], W, 6144, 32, 16, key=(l, 'win', 6144))
            for dd in range(2):
                bk = banks[cnt % 4]; sg = stf[cnt % 4]; cnt += 1
                self.mm(bk[0:16, :], [(w[:, kc, dd * 16:(dd + 1) * 16], hT[:, kc, :]) for kc in range(16)], [w, hT], bk)
                self.op("act", lambda e, bk=bk, sg=sg: e.activation(out=sg[0:16, :], in_=bk[0:16, :], func=AF.Copy), reads=[bk], writes=[sg])
                self.dma("sp", self.GLRT[dd, :, tok0:tok0 + 512], sg[0:16, :], reads=[sg])
            for (c0, dst, dc0, fn, dt, oname) in tokb:
                w = wb[wi % 3]; wi += 1
                self.wload(w, lambda k0, k1, w=w: w[:, k0:k1, :], W, c0, 512, 16, key=(l, 'win', c0))
                for tt in range(4):
                    bk = banks[cnt % 4]
                    sg = stf[cnt % 4] if dt == F32 else stb[cnt % 2]
                    cnt += 1
                    self.mm(bk[:, :], [(hT[:, kc, tt * 128:(tt + 1) * 128], w[:, kc, :]) for kc in range(16)], [w, hT], bk)
                    if fn == AF.Copy and cnt % 2 == 0:
                        self.op("dve", lambda e, bk=bk, sg=sg: e.tensor_copy(out=sg[:], in_=bk[:]), reads=[bk], writes=[sg])
                    else:
                        self.op("act", lambda e, bk=bk, sg=sg, fn=fn: e.activation(out=sg[:], in_=bk[:], func=fn), reads=[bk], writes=[sg])
                    tok = tok0 + tt * 128
                    self.dma("sp", dst[tok:tok + 128, dc0:dc0 + 512], sg[:], reads=[sg])
                    if oname is not None and tok >= TS:
                        pi = (tok - TS) // 256
                        t0 = (tok - TS) % 256
                        self.dma("sp", self.O[oname][pi, l, t0:t0 + 128, dc0:dc0 + 512], sg[:], reads=[sg])
    self.barrier()


def range_reduce(self, a, tmp, reads):
    self.op("dve", lambda e: e.tensor_scalar(out=tmp, in0=a, scalar1=1.0 / (2 * math.pi), scalar2=MAGIC, op0=ALU.mult, op1=ALU.add), reads=reads, writes=reads)
    self.op("dve", lambda e: e.tensor_scalar(out=tmp, in0=tmp, scalar1=-MAGIC, scalar2=-2 * math.pi, op0=ALU.add, op1=ALU.mult), reads=reads, writes=reads)
    self.op("dve", lambda e: e.tensor_tensor(out=a, in0=a, in1=tmp, op=ALU.add), reads=reads, writes=reads)


def hy_filters(self, l, L):
    I = self.I
    nt = L // 128
    KF = self.KF[L]
    Fc = I["c_f%d" % L]; NY = I["c_nyq%d" % L]
    with ExitStack() as st:
        fT = self.sb(st, "fT", [17, L], F32)
        w1 = self.sb(st, "hw1", [17, 64], F32); w2 = self.sb(st, "hw2", [64, 64], F32); w3 = self.sb(st, "hw3", [64, 4096], F32)
        pc = self.sb(st, "hpc", [64, 4], F32)
        ntl = self.sb(st, "ntl", [128, nt], F32)
        nyq = self.sb(st, "nyq", [128, nt], BF16)
        hd1 = self.sb(st, "hd1", [64, L], F32); hd2 = self.sb(st, "hd2", [64, L], F32)
        a_ = self.sb(st, "ha", [64, 512], F32); tm_ = self.sb(st, "htm", [64, 512], F32)
        banks = [self.ps(st, "bk%d" % i, [128, 512]) for i in range(5)]
        self.dma("sp", fT[:], I["c_featsT%d" % L][:, :], writes=[fT])
        self.dma("sp", w1[:], I["hy_w1"][l], writes=[w1]); self.dma("sp", w2[:], I["hy_w2"][l], writes=[w2]); self.dma("sp", w3[:], I["hy_w3"][l], writes=[w3])
        self.dma("sp", pc[:, 0:1], I["hy_b1"][l].rearrange("(m o) -> m o", o=1), writes=[pc])
        self.dma("sp", pc[:, 1:2], I["hy_b2"][l].rearrange("(m o) -> m o", o=1), writes=[pc])
        self.dma("sp", pc[:, 2:4], I["hy_freq"][l].rearrange("a m -> m a"), writes=[pc])
        self.dma("sp", ntl[:], I["c_ntl%d" % L][:, :], writes=[ntl])
        self.dma("sp", nyq[:], NY[:, :], writes=[nyq])
        n = min(512, L)
        for (wsrc, src, dst, bcol, fcol, kk) in ((w1, fT, hd1, 0, 2, 17), (w2, hd1, hd2, 1, 3, 64)):
            for tb in range(L // n):
                bk = banks[tb % 2]
                self.mm(bk[0:64, 0:n], [(wsrc[0:kk, :], src[0:kk, tb * n:(tb + 1) * n])], [wsrc, src], bk)
                self.op("dve", lambda e, bk=bk, bcol=bcol, fcol=fcol: e.tensor_scalar(
                    out=a_[:, 0:n], in0=bk[0:64, 0:n], scalar1=pc[:, bcol:bcol + 1], scalar2=pc[:, fcol:fcol + 1], op0=ALU.add, op1=ALU.mult),
                    reads=[bk, pc], writes=[a_])
                self.op("dve", lambda e: e.tensor_scalar(out=tm_[:, 0:n], in0=a_[:, 0:n], scalar1=1.0 / (2 * math.pi), scalar2=MAGIC, op0=ALU.mult, op1=ALU.add), reads=[a_], writes=[tm_])
                self.op("dve", lambda e: e.tensor_scalar(out=tm_[:, 0:n], in0=tm_[:, 0:n], scalar1=-MAGIC, scalar2=-2 * math.pi, op0=ALU.add, op1=ALU.mult), reads=[tm_], writes=[tm_])
                self.op("dve", lambda e: e.tensor_tensor(out=a_[:, 0:n], in0=a_[:, 0:n], in1=tm_[:, 0:n], op=ALU.add), reads=[a_, tm_], writes=[a_])
                self.op("act", lambda e, dst=dst, tb=tb: e.activation(out=dst[:, tb * n:(tb + 1) * n], in_=a_[:, 0:n], func=AF.Sin), reads=[a_], writes=[dst])
        spl = self.sb(st, "spl", [128, nt, 512], BF16); smi = self.sb(st, "smi", [128, nt, 512], BF16)
        dec = [self.sb(st, "dec%d" % i, [128, 512], F32) for i in range(2)]
        ex = [self.sb(st, "ex%d" % i, [128, 512], F32) for i in range(2)]
        hh = [self.sb(st, "hh%d" % i, [128, 512], F32) for i in range(2)]
        ft = [self.sb(st, "ft%d" % i, [128, nt, 128], BF16) for i in range(2)]
        stg = [self.sb(st, "kst%d" % i, [128, 512], F32) for i in range(2)]
        fcnt = 0
        for o in range(2):
            for cb in range(2):
                for dr in range(2):
                    c0 = o * 2048 + dr * 1024 + cb * 512
                    self.dma("sp", dec[dr][:], I["hy_decay"][l, c0:c0 + 512].partition_broadcast(128), writes=[dec[dr]])
                    self.op("act", lambda e, dr=dr: e.activation(out=dec[dr][:], in_=dec[dr][:], func=AF.Abs), reads=[dec[dr]], writes=[dec[dr]])
                for tt in range(nt):
                    for dr in range(2):
                        c0 = o * 2048 + dr * 1024 + cb * 512
                        bk = banks[dr]
                        self.mm(bk[:, :], [(hd2[:, tt * 128:(tt + 1) * 128], w3[:, c0:c0 + 512])], [hd2, w3], bk)
                        self.op("act", lambda e, dr=dr, tt=tt: e.activation(out=ex[dr][:], in_=dec[dr][:], func=AF.Exp, scale=ntl[:, tt:tt + 1]), reads=[dec[dr], ntl], writes=[ex[dr]])
                        self.op("dve", lambda e, dr=dr, bk=bk: e.tensor_tensor(out=hh[dr][:], in0=ex[dr][:], in1=bk[:], op=ALU.mult), reads=[ex[dr], bk], writes=[hh[dr]])
                    if tt == 0:
                        self.op("dve", lambda e: e.memset(hh[1][0:1, :], 0.0), writes=[hh[1]])
                    self.op("dve", lambda e, tt=tt: e.tensor_tensor(out=spl[:, tt, :], in0=hh[0][:], in1=hh[1][:], op=ALU.add), reads=hh, writes=[spl])
                    self.op("dve", lambda e, tt=tt: e.tensor_tensor(out=smi[:, tt, :], in0=hh[0][:], in1=hh[1][:], op=ALU.subtract), reads=hh, writes=[smi])
                for fti in range(2 * nt):
                    src = spl if fti < nt else smi
                    f = ft[fcnt % 2]; sg = stg[fcnt % 2]; bk = banks[2 + fcnt % 2]; fcnt += 1
                    self.dma("sp", f[:], Fc[fti], writes=[f])
                    self.mm(bk[:, :], [(f[:, kc, :], src[:, kc, :]) for kc in range(nt)], [f, src], bk)
                    self.op("act", lambda e, bk=bk, sg=sg: e.activation(out=sg[:], in_=bk[:], func=AF.Copy), reads=[bk], writes=[sg])
                    if fti == nt:
                        b4 = banks[4]
                        self.mm(b4[0:1, :], [(nyq[:, kc:kc + 1], spl[:, kc, :]) for kc in range(nt)], [nyq, spl], b4)
                        self.op("act", lambda e, sg=sg, b4=b4: e.activation(out=sg[0:1, :], in_=b4[0:1, :], func=AF.Copy), reads=[b4], writes=[sg])
                    self.dma("sp", KF[o, fti * 128:(fti + 1) * 128, cb * 512:(cb + 1) * 512], sg[:], reads=[sg])
    self.barrier()


def hy_prep(self, l):
    I = self.I
    for (s0, L, ci, pi) in self.seqs():
        nt = L // 128
        with ExitStack() as st:
            cT = self.sb(st, "cT", [128, 4, L], BF16)
            u = [self.sb(st, "u%d" % i, [128, L + 2], F32) for i in range(2)]
            acc = self.sb(st, "acc", [128, L], F32)
            wc = [self.sb(st, "wc%d" % i, [128, 4], F32) for i in range(2)]
            stg = [self.sb(st, "pst%d" % i, [128, 512], BF16) for i in range(2)]
            bbf = [self.ps(st, "bbf%d" % i, [128, 1024], BF16) for i in range(2)]
            for i in range(2):
                self.op("dve", lambda e, i=i: e.memset(u[i][:, 0:1], 0.0), writes=[u[i]])
                self.op("dve", lambda e, i=i: e.memset(u[i][:, L + 1:L + 2], 0.0), writes=[u[i]])
            cnt = 0
            for r in range(3):
                for cb in range(2):
                    for ct in range(4):
                        ch0 = r * 1024 + cb * 512 + ct * 128
                        uu = u[cnt % 2]; w = wc[cnt % 2]; cnt += 1
                        self.dma("sp", uu[:, 1:L + 1], self.ZHT[ch0:ch0 + 128, s0:s0 + L], writes=[uu])
                        self.dma("sp", w[:, 0:3], I["hy_conv_w"][l, :, ch0:ch0 + 128].rearrange("k c -> c k"), writes=[w])
                        self.dma("sp", w[:, 3:4], I["hy_conv_b"][l, ch0:ch0 + 128].rearrange("(c o) -> c o", o=1), writes=[w])
                        self.op("dve", lambda e, uu=uu, w=w: e.tensor_scalar(out=acc[:], in0=uu[:, 1:L + 1], scalar1=w[:, 1:2], scalar2=w[:, 3:4], op0=ALU.mult, op1=ALU.add), reads=[uu, w], writes=[acc])
                        self.op("dve", lambda e, uu=uu, w=w: e.scalar_tensor_tensor(out=acc[:], in0=uu[:, 0:L], scalar=w[:, 0:1], in1=acc[:], op0=ALU.mult, op1=ALU.add), reads=[uu, w, acc], writes=[acc])
                        self.op("dve", lambda e, uu=uu, w=w, ct=ct: e.scalar_tensor_tensor(out=cT[:, ct, :], in0=uu[:, 2:L + 2], scalar=w[:, 2:3], in1=acc[:], op0=ALU.mult, op1=ALU.add), reads=[uu, w, acc], writes=[cT])
                    for tt in range(nt):
                        pb = bbf[tt % 2]; sg = stg[tt % 2]
                        for ct in range(4):
                            self.op("pe", lambda e, pb=pb, ct=ct, tt=tt: e.transpose(out=pb[:, ct * 128:(ct + 1) * 128], in_=cT[:, ct, tt * 128:(tt + 1) * 128], identity=self.ident[:]), reads=[cT, self.ident], writes=[pb])
                        self.op("act", lambda e, pb=pb, sg=sg: e.activation(out=sg[:], in_=pb[:, 0:512], func=AF.Copy), reads=[pb], writes=[sg])
                        self.dma("sp", self.HTOK[r][s0 + tt * 128:s0 + (tt + 1) * 128, cb * 512:(cb + 1) * 512], sg[:], reads=[sg])
        self.barrier()


def hy_conv(self, l):
    I = self.I
    for (s0, L, ci, pi) in self.seqs():
        nt = L // 128
        KF = self.KF[L]; Fc = I["c_f%d" % L]; Gc = I["c_g%d" % L]
        with ExitStack() as st:
            vt = self.sb(st, "vt", [128, nt, 512], BF16)
            Y = self.sb(st, "Y", [128, 2 * nt, 512], BF16)
            ft = [self.sb(st, "ft%d" % i, [128, nt, 128], BF16) for i in range(4)]
            gt = [self.sb(st, "gt%d" % i, [128, 2 * nt, 128], BF16) for i in range(2)]
            kf = [self.sb(st, "kf%d" % i, [128, 512], F32) for i in range(4)]
            tq = [self.sb(st, "tq%d" % i, [128, 512], F32) for i in range(4)]
            bia = [self.sb(st, "bia%d" % i, [128, 512], F32) for i in range(2)]
            xm = [self.sb(st, "xm%d" % i, [128, 512], BF16) for i in range(2)]
            ya = [self.sb(st, "ya%d" % i, [128, 512], BF16) for i in range(2)]
            ystg = [self.sb(st, "ystg%d" % i, [128, 4, 128], BF16) for i in range(2)]
            banks = [self.ps(st, "bk%d" % i, [128, 512]) for i in range(6)]
            bbf = [self.ps(st, "bbf%d" % i, [128, 1024], BF16) for i in range(2)]
            for cb in range(2):
                for k0 in range(0, nt, 8):
                    k1 = min(nt, k0 + 8)
                    self.dma("sp", vt[:, k0:k1, :], self.HTOK[0][s0 + k0 * 128:s0 + k1 * 128, cb * 512:(cb + 1) * 512].rearrange("(kc p) c -> p kc c", p=128), writes=[vt])
                for o in range(2):
                    self.dma("sp", bia[o][:], I["hy_bias"][l, o, cb * 512:(cb + 1) * 512].partition_broadcast(128), writes=[bia[o]])
                for o in range(2):
                    for j in range(nt):
                        fr = ft[(j % 2) * 2]; fi = ft[(j % 2) * 2 + 1]
                        kr = kf[(j % 2) * 2]; ki = kf[(j % 2) * 2 + 1]
                        br = banks[(j % 2) * 2]; bi = banks[(j % 2) * 2 + 1]
                        self.dma("sp", fr[:], Fc[j], writes=[fr]); self.dma("sp", fi[:], Fc[nt + j], writes=[fi])
                        self.dma("sp", kr[:], KF[o, j * 128:(j + 1) * 128, cb * 512:(cb + 1) * 512], writes=[kr])
                        self.dma("sp", ki[:], KF[o, (nt + j) * 128:(nt + j + 1) * 128, cb * 512:(cb + 1) * 512], writes=[ki])
                        self.mm(br[:, :], [(fr[:, kc, :], vt[:, kc, :]) for kc in range(nt)], [fr, vt], br)
                        self.mm(bi[:, :], [(fi[:, kc, :], vt[:, kc, :]) for kc in range(nt)], [fi, vt], bi)
                        self.op("dve", lambda e, br=br, kr=kr: e.tensor_tensor(out=tq[0][:], in0=kr[:], in1=br[:], op=ALU.mult), reads=[kr, br], writes=[tq[0]])
                        self.op("dve", lambda e, bi=bi, ki=ki: e.tensor_tensor(out=tq[1][:], in0=ki[:], in1=bi[:], op=ALU.mult), reads=[ki, bi], writes=[tq[1]])
                        self.op("dve", lambda e, br=br, ki=ki: e.tensor_tensor(out=tq[2][:], in0=ki[:], in1=br[:], op=ALU.mult), reads=[ki, br], writes=[tq[2]])
                        self.op("dve", lambda e, bi=bi, kr=kr: e.tensor_tensor(out=tq[3][:], in0=kr[:], in1=bi[:], op=ALU.mult), reads=[kr, bi], writes=[tq[3]])
                        self.op("dve", lambda e, j=j: e.tensor_tensor(out=Y[:, j, :], in0=tq[0][:], in1=tq[1][:], op=ALU.subtract), reads=[tq[0], tq[1]], writes=[Y])
                        self.op("dve", lambda e, j=j: e.tensor_tensor(out=Y[:, nt + j, :], in0=tq[2][:], in1=tq[3][:], op=ALU.add), reads=[tq[2], tq[3]], writes=[Y])
                        if j == 0:
                            self.op("dve", lambda e: e.tensor_copy(out=Y[0:1, 0, :], in_=tq[0][0:1, :]), reads=[tq[0]], writes=[Y])
                            self.op("dve", lambda e: e.tensor_copy(out=Y[0:1, nt, :], in_=tq[1][0:1, :]), reads=[tq[1]], writes=[Y])
                    for tt in range(nt):
                        g_ = gt[tt % 2]; bk = banks[4 + tt % 2]; x_ = xm[tt % 2]
                        self.dma("sp", g_[:], Gc[tt], writes=[g_])
                        self.dma("sp", x_[:], self.HTOK[1 + o][s0 + tt * 128:s0 + (tt + 1) * 128, cb * 512:(cb + 1) * 512], writes=[x_])
                        self.mm(bk[:, :], [(g_[:, kc, :], Y[:, kc, :]) for kc in range(2 * nt)], [g_, Y], bk)
                        t_ = tq[tt % 2]
                        self.op("dve", lambda e, tt=tt, t_=t_, o=o: e.tensor_tensor(out=t_[:], in0=vt[:, tt, :], in1=bia[o][:], op=ALU.mult), reads=[vt, bia[o]], writes=[t_])
                        self.op("dve", lambda e, t_=t_, bk=bk: e.tensor_tensor(out=t_[:], in0=t_[:], in1=bk[:], op=ALU.add), reads=[t_, bk], writes=[t_])
                        if o == 0:
                            self.op("dve", lambda e, tt=tt, t_=t_, x_=x_: e.tensor_tensor(out=vt[:, tt, :], in0=t_[:], in1=x_[:], op=ALU.mult), reads=[t_, x_], writes=[vt])
                        else:
                            y_ = ya[tt % 2]; pb = bbf[tt % 2]; sg = ystg[tt % 2]
                            self.op("dve", lambda e, t_=t_, x_=x_, y_=y_: e.tensor_tensor(out=y_[:], in0=t_[:], in1=x_[:], op=ALU.mult), reads=[t_, x_], writes=[y_])
                            for ct in range(4):
                                self.op("pe", lambda e, pb=pb, ct=ct, y_=y_: e.transpose(out=pb[:, ct * 128:(ct + 1) * 128], in_=y_[:, ct * 128:(ct + 1) * 128], identity=self.ident[:]), reads=[y_, self.ident], writes=[pb])
                            self.op("act", lambda e, pb=pb, sg=sg: e.activation(out=sg[:], in_=pb[:, 0:512].rearrange("p (c t) -> p c t", c=4), func=AF.Copy), reads=[pb], writes=[sg])
                            self.dma("sp", self.YT[0][cb * 512:(cb + 1) * 512, s0 + tt * 128:s0 + (tt + 1) * 128].rearrange("(c p) t -> p c t", p=128), sg[:], reads=[sg])
        self.barrier()


def gla(self, l):
    I = self.I
    with ExitStack() as st:
        mats = self.sb(st, "mats", [128, 6, 128], F32); masks = self.sb(st, "masks", [128, 2, 128], F32)
        wa = self.sb(st, "wa", [16, 2, 512], F32); babc = self.sb(st, "babc", [128, 2, 512], F32); gn = self.sb(st, "gn", [128, 256], F32)
        S = self.sb(st, "S", [128, 4, 256], F32); Sb = self.sb(st, "Sb", [128, 4, 256], BF16)
        self.dma("sp", mats[:], I["c_glam"].rearrange("m j i -> j m i"), writes=[mats])
        self.dma("sp", masks[:], I["c_gmask"].rearrange("m j i -> j m i"), writes=[masks])
        self.dma("sp", wa[:], I["gla_wa"][l].rearrange("d r n -> r d n"), writes=[wa])
        for d in range(2):
            self.dma("sp", babc[:, d, :], I["gla_ba"][l, d, :].partition_broadcast(128), writes=[babc])
        self.dma("sp", gn[:], I["gla_norm_g"][l, :].partition_broadcast(128), writes=[gn])
        NB = 2
        lrT = [self.sb(st, "lrT%d" % i, [16, 128], F32) for i in range(NB)]
        qT = [self.sb(st, "qT%d" % i, [128, 4, 128], F32) for i in range(NB)]
        kT = [self.sb(st, "kT%d" % i, [128, 4, 128], F32) for i in range(NB)]
        kt = [self.sb(st, "kt%d" % i, [128, 512], F32) for i in range(NB)]
        vt = [self.sb(st, "vt%d" % i, [128, 1024], BF16) for i in range(NB)]
        of = [self.sb(st, "of%d" % i, [128, 1024], F32) for i in range(NB)]
        gr = [self.sb(st, "gr%d" % i, [128, 1024], F32) for i in range(NB)]
        lx = self.sb(st, "lx", [128, 512], F32); ll = self.sb(st, "ll", [128, 512], F32)
        ex = [self.sb(st, "ex%d" % i, [128, 512], F32) for i in range(2)]
        Q1 = [self.sb(st, "Q1%d" % i, [128, 128], BF16) for i in range(2)]; K1 = [self.sb(st, "K1%d" % i, [128, 128], BF16) for i in range(2)]
        Q2 = [self.sb(st, "Q2%d" % i, [128, 128], BF16) for i in range(2)]; K2 = [self.sb(st, "K2%d" % i, [128, 128], BF16) for i in range(2)]
        am = [self.sb(st, "am%d" % i, [128, 128], BF16) for i in range(2)]
        ot = self.sb(st, "ot", [128, 1024], F32); sq = self.sb(st, "sq", [128, 256], F32)
        ss = self.sb(st, "ss", [128, 4], F32); rs = self.sb(st, "rs", [128, 4], F32)
        sgr = self.sb(st, "sgr", [128, 1024], F32); yb = self.sb(st, "yb", [128, 1024], BF16)
        ystg = [self.sb(st, "ystg%d" % i, [128, 8, 128], BF16) for i in range(2)]
        bL = self.ps(st, "bL", [128, 512]); bE = [self.ps(st, "bE%d" % i, [128, 512]) for i in range(2)]
        bA = self.ps(st, "bA", [128, 512]); bO = [self.ps(st, "bO%d" % i, [128, 512]) for i in range(2)]
        bS = self.ps(st, "bS", [128, 512]); bT = self.ps(st, "bT", [128, 1024], BF16)
        qscale = 128.0 ** -0.5
        it = 0
        for (s0, L, ci, pi) in self.seqs():
            nt = L // 128
            for d in range(2):
                if pi < 0:
                    for h in range(4):
                        self.dma("sp", S[:, h, :], I["sg"][l, d, h], writes=[S])
                else:
                    self.op("dve", lambda e: e.memset(S[:], 0.0), writes=[S])
                self.op("act", lambda e: e.activation(out=Sb[:], in_=S[:], func=AF.Copy), reads=[S], writes=[Sb])
                order = list(range(nt)) if d == 0 else list(range(nt - 1, -1, -1))
                for c in order:
                    tok = s0 + c * 128
                    b_ = it % NB; it += 1
                    self.dma("sp", lrT[b_][:], self.GLRT[d, :, tok:tok + 128], writes=[lrT[b_]])
                    self.dma("sp", qT[b_][:], self.GQT[:, tok:tok + 128].rearrange("(h k) t -> k h t", k=128), writes=[qT[b_]])
                    self.dma("sp", kT[b_][:], self.GKT[:, tok:tok + 128].rearrange("(h k) t -> k h t", k=128), writes=[kT[b_]])
                    self.dma("sp", kt[b_][:], self.GK[tok:tok + 128, :], writes=[kt[b_]])
                    self.dma("sp", vt[b_][:], self.GV[tok:tok + 128, :], writes=[vt[b_]])
                    if d == 1:
                        self.dma("sp", of[b_][:], self.OF[tok:tok + 128, :], writes=[of[b_]])
                        self.dma("sp", gr[b_][:], self.GR[tok:tok + 128, :], writes=[gr[b_]])
                    self.mm(bL[:, :], [(lrT[b_][:, :], wa[:, d, :])], [lrT[b_], wa], bL)
                    self.op("dve", lambda e, d=d: e.tensor_tensor(out=lx[:], in0=bL[:], in1=babc[:, d, :], op=ALU.add), reads=[bL, babc], writes=[lx])
                    self.op("dve", lambda e: e.tensor_scalar(out=lx[:], in0=lx[:], scalar1=-80.0, scalar2=None, op0=ALU.max), reads=[lx], writes=[lx])
                    self.op("act", lambda e: e.activation(out=lx[:], in_=lx[:], func=AF.Exp, scale=-1.0), reads=[lx], writes=[lx])
                    self.op("act", lambda e: e.activation(out=ll[:], in_=lx[:], func=AF.Ln, bias=self.ones[:, 0:1], scale=1.0), reads=[lx, self.ones], writes=[ll])
                    for h in range(4):
                        e_ = bE[h % 2]; x_ = ex[h % 2]; q1 = Q1[h % 2]; k1 = K1[h % 2]; q2 = Q2[h % 2]; k2 = K2[h % 2]; a_ = am[h % 2]
                        hs = slice(h * 128, (h + 1) * 128)
                        self.mm(e_[:, 0:128], [(ll[:, hs], mats[:, 3 * d + 0, :])], [ll, mats], e_)
                        self.mm(e_[:, 128:256], [(ll[:, hs], mats[:, 3 * d + 1, :])], [ll, mats], e_)
                        self.mm(e_[:, 256:384], [(mats[:, 3 * d + 2, :], ll[:, hs])], [ll, mats], e_)
                        self.op("act", lambda e, e_=e_, x_=x_: e.activation(out=x_[:, 0:384], in_=e_[:, 0:384], func=AF.Exp), reads=[e_], writes=[x_])
                        self.op("act", lambda e, e_=e_, x_=x_: e.activation(out=x_[:, 384:512], in_=e_[:, 0:128], func=AF.Exp, scale=-1.0), reads=[e_], writes=[x_])
                        self.op("dve", lambda e, x_=x_, q1=q1, h=h, b_=b_: e.scalar_tensor_tensor(out=q1[:], in0=qT[b_][:, h, :], scalar=qscale, in1=x_[:, 0:128], op0=ALU.mult, op1=ALU.mult), reads=[qT[b_], x_], writes=[q1])
                        self.op("dve", lambda e, x_=x_, k1=k1, h=h, b_=b_: e.tensor_tensor(out=k1[:], in0=kT[b_][:, h, :], in1=x_[:, 384:512], op=ALU.mult), reads=[kT[b_], x_], writes=[k1])
                        self.op("dve", lambda e, x_=x_, q2=q2, h=h, b_=b_: e.scalar_tensor_tensor(out=q2[:], in0=qT[b_][:, h, :], scalar=qscale, in1=x_[:, 128:256], op0=ALU.mult, op1=ALU.mult), reads=[qT[b_], x_], writes=[q2])
                        self.op("dve", lambda e, x_=x_, k2=k2, hs=hs, b_=b_: e.tensor_tensor(out=k2[:], in0=kt[b_][:, hs], in1=x_[:, 256:384], op=ALU.mult), reads=[kt[b_], x_], writes=[k2])
                        self.mm(bA[:, 0:128], [(k1[:, :], q1[:, :])], [k1, q1], bA)
                        self.op("dve", lambda e, a_=a_, d=d: e.tensor_tensor(out=a_[:], in0=bA[:, 0:128], in1=masks[:, d, :], op=ALU.mult), reads=[bA, masks], writes=[a_])
                        ob = bO[h // 2]
                        vs = slice(h * 256, (h + 1) * 256)
                        self.mm(ob[:, (h % 2) * 256:(h % 2) * 256 + 256], [(a_[:, :], vt[b_][:, vs]), (q2[:, :], Sb[:, h, :])], [a_, vt[b_], q2, Sb], ob)
                        self.mm(bS[:, 0:256], [(k2[:, :], vt[b_][:, vs])], [k2, vt[b_]], bS)
                        dc = x_[:, 255:256] if d == 0 else x_[:, 128:129]
                        self.op("dve", lambda e, h=h, dc=dc: e.scalar_tensor_tensor(out=S[:, h, :], in0=S[:, h, :], scalar=dc, in1=bS[:, 0:256], op0=ALU.mult, op1=ALU.add), reads=[S, x_, bS], writes=[S])
                        self.op("act", lambda e, h=h: e.activation(out=Sb[:, h, :], in_=S[:, h, :], func=AF.Copy), reads=[S], writes=[Sb])
                    if d == 0:
                        for hh in range(2):
                            self.op("act", lambda e, hh=hh: e.activation(out=ot[:, hh * 512:(hh + 1) * 512], in_=bO[hh][:], func=AF.Copy), reads=[bO[hh]], writes=[ot])
                        self.dma("sp", self.OF[tok:tok + 128, :], ot[:], reads=[ot])
                    else:
                        for hh in range(2):
                            self.op("dve", lambda e, hh=hh, b_=b_: e.tensor_tensor(out=ot[:, hh * 512:(hh + 1) * 512], in0=of[b_][:, hh * 512:(hh + 1) * 512], in1=bO[hh][:], op=ALU.add), reads=[of[b_], bO[hh]], writes=[ot])
                        for h in range(4):
                            self.op("act", lambda e, h=h: e.activation(out=sq[:], in_=ot[:, h * 256:(h + 1) * 256], func=AF.Square), reads=[ot], writes=[sq])
                            self.op("dve", lambda e, h=h: e.reduce_sum(out=ss[:, h:h + 1], in_=sq[:], axis=mybir.AxisListType.X), reads=[sq], writes=[ss])
                        self.op("act", lambda e: e.activation(out=rs[:], in_=ss[:], func=AF.Sqrt, bias=self.eps6[:, 0:1], scale=1.0 / 256.0), reads=[ss, self.eps6], writes=[rs])
                        self.op("dve", lambda e: e.reciprocal(out=rs[:], in_=rs[:]), reads=[rs], writes=[rs])
                        self.op("act", lambda e, b_=b_: e.activation(out=sgr[:], in_=gr[b_][:], func=AF.Silu), reads=[gr[b_]], writes=[sgr])
                        for h in range(4):
                            self.op("dve", lambda e, h=h: e.scalar_tensor_tensor(out=ot[:, h * 256:(h + 1) * 256], in0=ot[:, h * 256:(h + 1) * 256], scalar=rs[:, h:h + 1], in1=gn[:], op0=ALU.mult, op1=ALU.mult), reads=[ot, rs, gn], writes=[ot])
                        self.op("dve", lambda e: e.tensor_tensor(out=yb[:], in0=ot[:], in1=sgr[:], op=ALU.mult), reads=[ot, sgr], writes=[yb])
                        sg_ = ystg[c % 2]
                        for cc in range(8):
                            self.op("pe", lambda e, cc=cc: e.transpose(out=bT[:, cc * 128:(cc + 1) * 128], in_=yb[:, cc * 128:(cc + 1) * 128], identity=self.ident[:]), reads=[yb, self.ident], writes=[bT])
                        self.op("act", lambda e, sg_=sg_: e.activation(out=sg_[:], in_=bT[:, :].rearrange("p (c t) -> p c t", c=8), func=AF.Copy), reads=[bT], writes=[sg_])
                        self.dma("sp", self.YT[1][:, tok:tok + 128].rearrange("(c p) t -> p c t", p=128), sg_[:], reads=[sg_])
                if pi >= 0:
                    for h in range(4):
                        self.dma("sp", self.O["nsg"][pi, l, d, h], S[:, h, :], reads=[S])
                self.barrier()
    self.barrier()


def diff(self, l):
    I = self.I
    lam_init = 0.8 - 0.6 * math.exp(-0.3 * l)
    with ExitStack() as st:
        dl = self.sb(st, "dl", [128, 256], F32); pr = self.sb(st, "pr", [128, 128], F32); sm = self.sb(st, "sm", [128, 4], F32)
        lam = self.sb(st, "lam", [128, 2], F32); gnc = self.sb(st, "gnc", [128, 128], F32)
        self.dma("sp", dl[:], I["diff_lam"][l].rearrange("a b -> (a b)").partition_broadcast(128), writes=[dl])
        self.op("dve", lambda e: e.tensor_tensor(out=pr[:, 0:64], in0=dl[:, 0:64], in1=dl[:, 64:128], op=ALU.mult), reads=[dl], writes=[pr])
        self.op("dve", lambda e: e.tensor_tensor(out=pr[:, 64:128], in0=dl[:, 128:192], in1=dl[:, 192:256], op=ALU.mult), reads=[dl], writes=[pr])
        self.op("dve", lambda e: e.reduce_sum(out=sm[:, 0:1], in_=pr[:, 0:64], axis=mybir.AxisListType.X), reads=[pr], writes=[sm])
        self.op("dve", lambda e: e.reduce_sum(out=sm[:, 1:2], in_=pr[:, 64:128], axis=mybir.AxisListType.X), reads=[pr], writes=[sm])
        self.op("act", lambda e: e.activation(out=sm[:, 2:4], in_=sm[:, 0:2], func=AF.Exp), reads=[sm], writes=[sm])
        self.op("dve", lambda e: e.tensor_tensor(out=lam[:, 0:1], in0=sm[:, 2:3], in1=sm[:, 3:4], op=ALU.subtract), reads=[sm], writes=[lam])
        self.op("dve", lambda e: e.tensor_scalar(out=lam[:, 1:2], in0=lam[:, 0:1], scalar1=lam_init, scalar2=-1.0, op0=ALU.add, op1=ALU.mult), reads=[lam], writes=[lam])
        self.dma("sp", gnc[:], I["diff_norm_g"][l, :].partition_broadcast(128), writes=[gnc])
        self.op("dve", lambda e: e.tensor_scalar(out=gnc[:], in0=gnc[:], scalar1=(1.0 - lam_init), scalar2=None, op0=ALU.mult), reads=[gnc], writes=[gnc])
        with ExitStack() as s2:
            xin = [self.sb(s2, "xin%d" % i, [128, 1024], F32) for i in range(2)]
            xsw = self.sb(s2, "xsw", [128, 1024], F32); t1 = self.sb(s2, "t1", [128, 1024], F32)
            cs = [self.sb(s2, "cs%d" % i, [128, 1024], F32) for i in range(2)]
            xb = [self.sb(s2, "xb%d" % i, [128, 1024], BF16) for i in range(2)]
            va = [self.sb(s2, "va%d" % i, [128, 8, 129], BF16) for i in range(2)]
            stg = [self.sb(s2, "dstg%d" % i, [128, 8, 128], BF16) for i in range(2)]
            bT = [self.ps(s2, "bT%d" % i, [128, 1024], BF16) for i in range(2)]
            for i in range(2):
                self.op("dve", lambda e, i=i: e.memset(va[i][:, :, 128:129], 1.0), writes=[va[i]])
            cnt = 0
            work = []
            for (s0, L, ci, pi) in self.seqs():
                if pi < 0:
                    for c in range(2):
                        work.append(("k", I["ck"][l, c * 128:(c + 1) * 128, :], None, c * 128))
                        work.append(("v", I["cv"][l, c * 128:(c + 1) * 128, :], None, c * 128))
                for c in range(L // 128):
                    tok = s0 + c * 128
                    pos = tok if pi < 0 else None
                    work.append(("q", self.DQ[tok:tok + 128, :], pos, tok))
                    work.append(("k", self.DK[tok:tok + 128, :], pos, 256 + tok))
                    work.append(("v", self.DV[tok:tok + 128, :], None, 256 + tok))
            lastpos = None
            for (kind, src, pos, dst) in work:
                x = xin[cnt % 2]; b = xb[cnt % 2]; v_ = va[cnt % 2]; sg = stg[cnt % 2]; pb = bT[cnt % 2]; cnt += 1
                self.dma("sp", x[:], src, writes=[x])
                if kind == "v":
                    self.op("act", lambda e, x=x, v_=v_: e.activation(out=v_[:, :, 0:128], in_=x[:, :].rearrange("p (h e) -> p h e", h=8), func=AF.Copy), reads=[x], writes=[v_])
                    self.dma("sp", self.VA[dst:dst + 128, :, :], v_[:], reads=[v_])
                    continue
                if pos is not None:
                    if pos != lastpos:
                        self.dma("sp", cs[0][:], I["c_cosf"][pos:pos + 128, :], writes=[cs[0]])
                        self.dma("sp", cs[1][:], I["c_sinf"][pos:pos + 128, :], writes=[cs[1]])
                        lastpos = pos
                    x4 = x[:, :].rearrange("p (g h n) -> p g h n", h=2, n=16)
                    s4 = xsw[:, :].rearrange("p (g h n) -> p g h n", h=2, n=16)
                    self.op("act", lambda e, x4=x4, s4=s4: e.activation(out=s4[:, :, 0, :], in_=x4[:, :, 1, :], func=AF.Copy), reads=[x], writes=[xsw])
                    self.op("act", lambda e, x4=x4, s4=s4: e.activation(out=s4[:, :, 1, :], in_=x4[:, :, 0, :], func=AF.Copy), reads=[x], writes=[xsw])
                    self.op("dve", lambda e, x=x: e.tensor_tensor(out=t1[:], in0=x[:], in1=cs[0][:], op=ALU.mult), reads=[x, cs[0]], writes=[t1])
                    self.op("dve", lambda e: e.tensor_tensor(out=xsw[:], in0=xsw[:], in1=cs[1][:], op=ALU.mult), reads=[xsw, cs[1]], writes=[xsw])
                    self.op("dve", lambda e, b=b: e.tensor_tensor(out=b[:], in0=t1[:], in1=xsw[:], op=ALU.add), reads=[t1, xsw], writes=[b])
                else:
                    self.op("act", lambda e, x=x, b=b: e.activation(out=b[:], in_=x[:], func=AF.Copy), reads=[x], writes=[b])
                for h in range(8):
                    self.op("pe", lambda e, h=h, b=b, pb=pb: e.transpose(out=pb[:, h * 128:(h + 1) * 128], in_=b[:, h * 128:(h + 1) * 128], identity=self.ident[:]), reads=[b, self.ident], writes=[pb])
                self.op("act", lambda e, pb=pb, sg=sg: e.activation(out=sg[:], in_=pb[:, :].rearrange("p (h t) -> p h t", h=8), func=AF.Copy), reads=[pb], writes=[sg])
                dstT = self.DQT if kind == "q" else self.DKT
                self.dma("sp", dstT[:, :, dst:dst + 128].rearrange("h p t -> p h t"), sg[:], reads=[sg])
        self.barrier()
        with ExitStack() as s2:
            KT = [self.sb(s2, "KT%d" % i, [128, 4352], BF16) for i in range(2)]
            VAh = [self.sb(s2, "VAh%d" % i, [128, 34, 129], BF16) for i in range(2)]
            QT = [[self.sb(s2, "QT%d_%d" % (i, j), [128, 4096], BF16) for j in range(2)] for i in range(2)]
            for i in range(2):
                for j in range(2):
                    self.op("dve", lambda e, i=i, j=j: e.memset(QT[i][j][:], 0.0), writes=[QT[i][j]])
            PT = [self.sb(s2, "PT%d" % i, [128, 512], BF16) for i in range(3)]
            rr = self.sb(s2, "rr", [128, 4], F32); o32 = self.sb(s2, "o32", [128, 128], F32); sq = self.sb(s2, "sq", [128, 128], F32)
            ycb = [self.sb(s2, "ycb%d" % i, [128, 128], BF16) for i in range(2)]
            ystg = [self.sb(s2, "ycs%d" % i, [128, 512], BF16) for i in range(2)]
            accb = [self.ps(s2, "acc%d" % i, [128, 512]) for i in range(4)]
            sbk = [self.ps(s2, "sbk%d" % i, [128, 512]) for i in range(3)]
            bT = self.ps(s2, "bT", [128, 1024], BF16)
            hi = 0; sc = 0
            for (s0, L, ci, pi) in self.seqs():
                nk = L + (256 if pi < 0 else 0)
                kcol0 = 0 if pi < 0 else 256 + s0
                nkc = nk // 128
                nq = min(512, L); nqs = nq // 128
                for h in range(8):
                    kt_ = KT[hi % 2]; va_ = VAh[hi % 2]; qt_ = QT[hi % 2]; hi += 1
                    self.dma("sp", kt_[:, 0:nk], self.DKT[h, :, kcol0:kcol0 + nk], writes=[kt_])
                    for j in range(2):
                        self.dma("sp", qt_[j][64 * j:64 * j + 64, 0:L], self.DQT[h, 64 * j:64 * j + 64, s0:s0 + L], writes=[qt_[j]])
                    for k0 in range(0, nkc, 8):
                        k1 = min(nkc, k0 + 8)
                        self.dma("sp", va_[:, k0:k1, :], self.VA[kcol0 + k0 * 128:kcol0 + k1 * 128, h, :].rearrange("(kc p) e -> p kc e", p=128), writes=[va_])
                    for qb in range(L // nq):
                        units = [(kc, j) for kc in range(nkc) for j in range(2)]
                        slots = {}

                        def emit_qk(ui, units=units, slots=slots, kt_=kt_, qt_=qt_, qb=qb, nq=nq):
                            nonlocal sc
                            kc, j = units[ui]
                            sb_ = sbk[sc % 3]; pt = PT[sc % 3]; sc += 1
                            slots[ui] = pt
                            self.mm(sb_[:, 0:nq], [(kt_[:, kc * 128:(kc + 1) * 128], qt_[j][:, qb * nq:(qb + 1) * nq])], [kt_, qt_[j]], sb_)
                            self.op("act", lambda e, sb_=sb_, pt=pt: e.activation(out=pt[:, 0:nq], in_=sb_[:, 0:nq], func=AF.Exp, scale=0.125), reads=[sb_], writes=[pt])

                        def emit_av(ui, units=units, slots=slots, va_=va_, nqs=nqs, nkc=nkc):
                            kc, j = units[ui]
                            pt = slots.pop(ui)
                            for qs in range(nqs):
                                a = j * nqs + qs
                                ab = accb[a // 2]; co = (a % 2) * 256
                                first = (kc == 0 and a % 2 == 0)
                                self.op("pe", lambda e, ab=ab, co=co, pt=pt, qs=qs, kc=kc, first=first: e.matmul(
                                    ab[:, co:co + 129], lhsT=pt[:, qs * 128:(qs + 1) * 128], rhs=va_[:, kc, :], start=first, stop=(kc == nkc - 1), skip_group_check=True),
                                    reads=[pt, va_], writes=[ab], sig=(qs == nqs - 1))

                        for ui in range(0, len(units), 2):
                            emit_qk(ui)
                            emit_qk(ui + 1)
                            emit_av(ui)
                            emit_av(ui + 1)
                        sg = ystg[qb % 2]
                        for qs in range(nqs):
                            a0 = qs; a1 = nqs + qs
                            A0 = accb[a0 // 2]; c0_ = (a0 % 2) * 256; A1 = accb[a1 // 2]; c1_ = (a1 % 2) * 256
                            yc_ = ycb[qs % 2]
                            self.op("dve", lambda e, A0=A0, c0_=c0_: e.reciprocal(out=rr[:, 0:1], in_=A0[:, c0_ + 128:c0_ + 129]), reads=[A0], writes=[rr])
                            self.op("dve", lambda e, A1=A1, c1_=c1_: e.reciprocal(out=rr[:, 1:2], in_=A1[:, c1_ + 128:c1_ + 129]), reads=[A1], writes=[rr])
                            self.op("dve", lambda e: e.tensor_tensor(out=rr[:, 1:2], in0=rr[:, 1:2], in1=lam[:, 1:2], op=ALU.mult), reads=[rr, lam], writes=[rr])
                            self.op("dve", lambda e, A0=A0, c0_=c0_: e.tensor_scalar(out=o32[:], in0=A0[:, c0_:c0_ + 128], scalar1=rr[:, 0:1], scalar2=None, op0=ALU.mult), reads=[A0, rr], writes=[o32])
                            self.op("dve", lambda e, A1=A1, c1_=c1_: e.scalar_tensor_tensor(out=o32[:], in0=A1[:, c1_:c1_ + 128], scalar=rr[:, 1:2], in1=o32[:], op0=ALU.mult, op1=ALU.add), reads=[A1, rr, o32], writes=[o32])
                            self.op("act", lambda e: e.activation(out=sq[:], in_=o32[:], func=AF.Square), reads=[o32], writes=[sq])
                            self.op("dve", lambda e: e.reduce_sum(out=rr[:, 2:3], in_=sq[:], axis=mybir.AxisListType.X), reads=[sq], writes=[rr])
                            self.op("act", lambda e: e.activation(out=rr[:, 3:4], in_=rr[:, 2:3], func=AF.Sqrt, bias=self.eps6[:, 0:1], scale=1.0 / 128.0), reads=[rr, self.eps6], writes=[rr])
                            self.op("dve", lambda e: e.reciprocal(out=rr[:, 3:4], in_=rr[:, 3:4]), reads=[rr], writes=[rr])
                            self.op("dve", lambda e, yc_=yc_: e.scalar_tensor_tensor(out=yc_[:], in0=o32[:], scalar=rr[:, 3:4], in1=gnc[:], op0=ALU.mult, op1=ALU.mult), reads=[o32, rr, gnc], writes=[yc_])
                            self.op("pe", lambda e, yc_=yc_, qs=qs: e.transpose(out=bT[:, qs * 128:(qs + 1) * 128], in_=yc_[:], identity=self.ident[:]), reads=[yc_, self.ident], writes=[bT])
                        self.op("act", lambda e, sg=sg: e.activation(out=sg[:, 0:nq], in_=bT[:, 0:nq], func=AF.Copy), reads=[bT], writes=[sg])
                        q0 = s0 + qb * nq
                        self.dma("sp", self.YT[2][h * 128:(h + 1) * 128, q0:q0 + nq], sg[:, 0:nq], reads=[sg])
    self.barrier()


def mixer_out(self, l):
    I = self.I
    WB = [I["w_branch_a"][l], I["w_branch_b"][l], I["w_branch_c"][l]]
    WO = I["w_out"][l]
    with ExitStack() as st:
        yT = [self.sb(st, "yT%d" % i, [128, 8, 512], BF16) for i in range(3)]
        yacc = self.sb(st, "yacc", [128, 4, D], F32)
        wbr = [self.sb(st, "wbr%d" % i, [128, 8, 512], BF16) for i in range(3)]
        self.wstage_alloc(st)
        wo = [self.sb(st, "wo%d" % i, [128, 16, 512], BF16) for i in range(2)]
        gt = [self.sb(st, "gt%d" % i, [128, 512], F32) for i in range(3)]
        tmp = [self.sb(st, "tmp%d" % i, [128, 512], F32) for i in range(2)]
        yb = self.sb(st, "yb", [128, D], BF16)
        yT2 = self.sb(st, "yT2", [128, 16, 512], BF16)
        V = [self.sb(st, "V%d" % i, [128, D], F32) for i in range(3)]
        xt = [self.sb(st, "xt%d" % i, [128, D], F32) for i in range(2)]
        small = (self.sb(st, "stats", [128, 4, 6], F32), self.sb(st, "mv", [128, 2], F32), self.sb(st, "rstd", [128, 1], F32))
        banks = [self.ps(st, "bk%d" % i, [128, 512]) for i in range(6)]
        bbf = [self.ps(st, "bbf%d" % i, [128, 1024], BF16) for i in range(2)]
        cnt = 0
        for g in self.groups():
            ci = self.gci(g)
            tok0 = g * 512
            for br in range(3):
                self.dma("sp", yT[br][:], self.YT[br][:, tok0:tok0 + 512].rearrange("(kc p) t -> p kc t", p=128), writes=[yT[br]])
            wi = 0
            for br in range(3):
                for db in range(4):
                    w = wbr[wi % 3]; wi += 1
                    self.wload(w, lambda k0, k1, w=w: w[:, k0:k1, :], WB[br], db * 512, 512, 8, key=(l, 'wbr', br, db))
                    for tt in range(4):
                        bk = banks[cnt % 4]; g_ = gt[cnt % 3]; tm = tmp[cnt % 2]; cnt += 1
                        tok = tok0 + tt * 128
                        self.dma("sp", g_[:], self.GATES[tok:tok + 128, br * D + db * 512: br * D + (db + 1) * 512], writes=[g_])
                        self.mm(bk[:, :], [(yT[br][:, kc, tt * 128:(tt + 1) * 128], w[:, kc, :]) for kc in range(8)], [yT[br], w], bk)
                        if br == 0:
                            self.op("dve", lambda e, bk=bk, g_=g_, tt=tt, db=db: e.tensor_tensor(out=yacc[:, tt, db * 512:(db + 1) * 512], in0=g_[:], in1=bk[:], op=ALU.mult), reads=[g_, bk], writes=[yacc])
                        else:
                            self.op("dve", lambda e, bk=bk, g_=g_, tm=tm: e.tensor_tensor(out=tm[:], in0=g_[:], in1=bk[:], op=ALU.mult), reads=[g_, bk], writes=[tm])
                            self.op("dve", lambda e, tm=tm, tt=tt, db=db: e.tensor_tensor(out=yacc[:, tt, db * 512:(db + 1) * 512], in0=yacc[:, tt, db * 512:(db + 1) * 512], in1=tm[:], op=ALU.add), reads=[tm, yacc], writes=[yacc])
            for tt in range(4):
                self.op("act", lambda e, tt=tt: e.activation(out=yb[:], in_=yacc[:, tt, :], func=AF.Copy), reads=[yacc], writes=[yb])
                for half in range(2):
                    pb = bbf[half]
                    for i in range(8):
                        kc = half * 8 + i
                        self.op("pe", lambda e, pb=pb, i=i, kc=kc: e.transpose(out=pb[:, i * 128:(i + 1) * 128], in_=yb[:, kc * 128:(kc + 1) * 128], identity=self.ident[:]), reads=[yb, self.ident], writes=[pb])
                    self.op("act", lambda e, pb=pb, half=half, tt=tt: e.activation(out=yT2[:, half * 8:half * 8 + 8, tt * 128:(tt + 1) * 128], in_=pb[:, :].rearrange("p (k t) -> p k t", k=8), func=AF.Copy), reads=[pb], writes=[yT2])
            self.load_vec(V[0], ci, 5)
            self.load_ln(l, 1, V[1], V[2])
            for db in range(4):
                w = wo[db % 2]
                self.wload(w, lambda k0, k1, w=w: w[:, k0:k1, :], WO, db * 512, 512, 16, key=(l, 'wo', db))
                for tt in range(4):
                    bk = banks[4 + tt % 2]
                    self.mm(bk[:, :], [(yT2[:, kc, tt * 128:(tt + 1) * 128], w[:, kc, :]) for kc in range(16)], [yT2, w], bk)
                    self.op("act", lambda e, bk=bk, tt=tt, db=db: e.activation(out=yacc[:, tt, db * 512:(db + 1) * 512], in_=bk[:], func=AF.Copy), reads=[bk], writes=[yacc])
            for tt in range(4):
                tok = tok0 + tt * 128
                self.epilogue(l, 1, tok, lambda db, tt=tt: yacc[:, tt, db * 512:(db + 1) * 512], [yacc] * 4, xt[tt % 2], _V(yacc, tt), V[0], V[1], V[2], False, small)
    self.barrier()


def phase_mixer(self, l):
    self.mixer_alloc()
    self.store_q = "pool"
    sub = self.cfg.get("mix")
    def on(p):
        return sub is None or p in sub
    if on("in"):
        self.mixer_in(l)
    if on("hyf"):
        for L in sorted({s[1] for s in self.seqs()}):
            self.hy_filters(l, L)
    if on("hyp"):
        self.hy_prep(l)
    if on("hyc"):
        self.hy_conv(l)
    if on("gla"):
        self.gla(l)
    if on("diff"):
        self.diff(l)
    if on("out"):
        self.mixer_out(l)
    self.store_q = None


for _f in (mixer_alloc, mixer_in, range_reduce, hy_filters, hy_prep, hy_conv, gla, diff, mixer_out, phase_mixer):
    setattr(MK, _f.__name__, _f)
MK.seqs = _seqs


def _shapes():
    sh = {
        "xs": ((TS, D), F32), "xp": ((TP, D), F32), "cvec": ((2, D), F32),
        "ck": ((DEPTH, 256, 1024), F32), "cv": ((DEPTH, 256, 1024), F32), "sg": ((DEPTH, 2, 4, 128, 256), F32),
        "w_mod": ((DEPTH, D, 9 * D), F32), "b_mod": ((DEPTH, 9 * D), F32), "ln_g": ((DEPTH, 3, D), F32), "ln_b": ((DEPTH, 3, D), F32),
        "ffn_w1": ((DEPTH, 2, D, FF), F32), "ffn_w3": ((DEPTH, 2, D, FF), F32), "ffn_w2": ((DEPTH, 2, FF, D), F32),
        "w_in": ((DEPTH, D, NCOL), F32), "hy_conv_w": ((DEPTH, 3, 3072), F32), "hy_conv_b": ((DEPTH, 3072), F32),
        "hy_w1": ((DEPTH, 17, 64), F32), "hy_b1": ((DEPTH, 64), F32), "hy_freq": ((DEPTH, 2, 64), F32),
        "hy_w2": ((DEPTH, 64, 64), F32), "hy_b2": ((DEPTH, 64), F32), "hy_w3": ((DEPTH, 64, 4096), F32),
        "hy_decay": ((DEPTH, 4096), F32), "hy_bias": ((DEPTH, 2, 1024), F32), "gla_wa": ((DEPTH, 2, 16, 512), F32),
        "gla_ba": ((DEPTH, 2, 512), F32), "gla_norm_g": ((DEPTH, 256), F32), "diff_lam": ((DEPTH, 4, 64), F32),
        "diff_norm_g": ((DEPTH, 128), F32), "w_branch_a": ((DEPTH, 1024, D), F32), "w_branch_b": ((DEPTH, 1024, D), F32),
        "w_branch_c": ((DEPTH, 1024, D), F32), "w_out": ((DEPTH, D, D), F32),
    }
    for k, v in _consts().items():
        sh[k] = (v.shape, BF16 if v.dtype == ml_dtypes.bfloat16 else F32)
    return sh


def build(cfg=None):
    cfg = cfg or {}
    nc = bass.Bass("TRN2", target_bir_lowering=False)
    k = MK(nc, _shapes(), cfg)
    k.out("ys", [TS, D])
    k.out("yp", [TP, D])
    k.out("nck", [2, DEPTH, 256, 1024])
    k.out("ncv", [2, DEPTH, 256, 1024])
    k.out("nsg", [2, DEPTH, 2, 4, 128, 256])
    phases = cfg.get("phases")
    with nc.allow_non_contiguous_dma(reason="small strided parameter loads"):
        with ExitStack() as st:
            k.phase_init(st)
            nl = cfg.get("depth", DEPTH)
            for l in range(nl):
                k.phase_mod(l)
            for l in range(nl):
                k.phase_convert(l)
            for l in range(nl):
                phases = cfg.get("phases%d" % l, cfg.get("phases"))
                k.MODB = k.MODBS[l]

                def on(p, phases=phases):
                    return phases is None or p in phases
                if on("ffn1"):
                    k.phase_ffn(l, 0)
                if on("mixer"):
                    k.phase_mixer(l)
                if on("ffn2"):
                    k.phase_ffn(l, 1)
            k.barrier(full=True)
    k.es.close()
    return nc, k


def _in_maps(inputs):
    c = _consts()
    maps = []
    f = lambda a: np.ascontiguousarray(np.asarray(a, dtype=np.float32))
    shared = {n: f(inputs[n]) for n in WNAMES}
    for b in range(8):
        m = dict(shared)
        m.update(c)
        m["xs"] = f(inputs["x_sample"][b])
        m["xp"] = f(inputs["x_prompt"][2 * b:2 * b + 2]).reshape(TP, D)
        m["cvec"] = f(np.stack([np.asarray(inputs["c"][b]), np.asarray(inputs["c_ctx"])]))
        m["ck"] = f(inputs["cache_k"][b]).reshape(DEPTH, 256, 1024)
        m["cv"] = f(inputs["cache_v"][b]).reshape(DEPTH, 256, 1024)
        m["sg"] = f(inputs["state_gla"][b])
        maps.append(m)
    return maps


def kernel(**inputs):
    nc, k = build()
    res = run_bass_kernel_spmd(nc, _in_maps(inputs), core_ids=list(range(8)))
    r = res.results
    ys = np.stack([r[b]["ys"] for b in range(8)])
    yp = np.concatenate([r[b]["yp"].reshape(2, 256, D) for b in range(8)])
    nck = np.concatenate([r[b]["nck"].reshape(2, DEPTH, 256, 8, 2, 64) for b in range(8)])
    ncv = np.concatenate([r[b]["ncv"].reshape(2, DEPTH, 256, 8, 128) for b in range(8)])
    nsg = np.concatenate([r[b]["nsg"] for b in range(8)])
    return (yp, ys, nck, ncv, nsg)
```
